# Optimizing a Trainium2 kernel written in Bass

```python
import jax, jax.numpy as jnp
from jax import lax
import numpy as np

D_MODEL = 1024
BATCH = 8
SEQ = 8192
DEPTH = 4
DEC_BATCH = 16
DEC_SEQ = 64
PAST_LEN = 1024

CHUNK = 64
N_EVEN = (DEPTH + 1) // 2
N_ODD = DEPTH // 2
A_WIDTH = D_MODEL // 2
A_HEAD_DIM = 64
A_HEADS = A_WIDTH // A_HEAD_DIM
A_DECAY_LORA = 64
A_ICL_LORA = 64
A_GATE_LORA = 128
A_PROJ = 3 * A_WIDTH + A_DECAY_LORA + A_ICL_LORA + A_GATE_LORA
A_NORM_EPS = 64e-5
B_VWIDTH = D_MODEL // 2
B_KWIDTH = B_VWIDTH // 2
B_HEADS = 4
B_KEY_DIM = B_KWIDTH // B_HEADS
B_VAL_DIM = B_VWIDTH // B_HEADS
B_GATE_LORA = 16
B_GATE_NORM = 16.0
B_PROJ = 2 * B_KWIDTH + 2 * B_VWIDTH + B_GATE_LORA
MIX_PROJ = A_PROJ + B_PROJ
C_HEAD_DIM = 64
C_HEADS = D_MODEL // C_HEAD_DIM
C_PAST_CHUNKS = 8
C_PAST_ROWS = C_PAST_CHUNKS * CHUNK
C_BAND = C_PAST_ROWS + CHUNK
C_REL_CLIP = 128
FFN_HIDDEN = ((8 * D_MODEL + 3 * 256 - 1) // (3 * 256)) * 256
LN_EPS = 1e-5
DEEPNORM_ALPHA = (2.0 * DEPTH) ** 0.25
DEEPNORM_BETA = (8.0 * DEPTH) ** -0.25
NEG_INF = -1e30

kernel_name = "hybrid_rwkv7_gla_chunkband_streaming_step"

F32 = jnp.float32


def layer_norm(x, w, b):
    xf = x.astype(F32)
    mu = jnp.mean(xf, -1, keepdims=True)
    var = jnp.mean(jnp.square(xf - mu), -1, keepdims=True)
    return ((xf - mu) * lax.rsqrt(var + LN_EPS) * w.astype(F32) + b.astype(F32)).astype(x.dtype)


def swiglu(x, w_in, w_out):
    gate, up = jnp.split(x @ w_in, 2, axis=-1)
    return (jax.nn.silu(gate) * up) @ w_out


def wkv7_scan(r, decay, k, v, kk, a, S0):
    def step(S, inp):
        r_t, w_t, k_t, v_t, kk_t, a_t = inp
        sa = jnp.einsum('bhvk,bhk->bhv', S, kk_t)
        S = (S * w_t[:, :, None, :] - sa[..., None] * (kk_t * a_t)[:, :, None, :]
             + v_t[..., None] * k_t[:, :, None, :])
        return S, jnp.einsum('bhvk,bhk->bhv', S, r_t)
    xs = tuple(jnp.swapaxes(t, 0, 1) for t in (r, decay, k, v, kk, a))
    S, o = lax.scan(step, S0.astype(F32), xs)
    return S, jnp.swapaxes(o, 0, 1)


def rwkv7_time_mix(pa, shift0, wkv0, mu, w0, w2, a0, a2, g2, k_k, k_a, r_k, ln_w, ln_b):
    Bsz, T, _ = pa.shape
    prev = jnp.concatenate([shift0[:, None, :].astype(pa.dtype), pa[:, :-1]], axis=1)
    xs = pa + (prev - pa) * mu
    i3 = 3 * A_WIDTH
    r, k, v, xw, xa, xg = jnp.split(
        xs, [A_WIDTH, 2 * A_WIDTH, i3, i3 + A_DECAY_LORA, i3 + A_DECAY_LORA + A_ICL_LORA], axis=-1)
    w = -jax.nn.softplus(-(w0 + jnp.tanh(xw) @ w2).astype(F32)) - 0.5
    decay = jnp.exp(-jnp.exp(w))
    a = jax.nn.sigmoid((a0 + xa @ a2).astype(F32))
    g = jax.nn.sigmoid(xg) @ g2
    hs = (Bsz, T, A_HEADS, A_HEAD_DIM)
    hd = (A_HEADS, A_HEAD_DIM)
    r, k, v, decay, a = (t.astype(F32).reshape(hs) for t in (r, k, v, decay, a))
    kk = k * k_k.astype(F32).reshape(hd)
    kk = kk * lax.rsqrt(jnp.sum(kk * kk, -1, keepdims=True) + 1e-12)
    k = k * (1.0 + (a - 1.0) * k_a.astype(F32).reshape(hd))
    wkv, o = wkv7_scan(r, decay, k, v, kk, a, wkv0)
    mean = jnp.mean(o, -1, keepdims=True)
    var = jnp.mean(jnp.square(o - mean), -1, keepdims=True)
    o = (o - mean) * lax.rsqrt(var + A_NORM_EPS) * ln_w.astype(F32).reshape(hd) + ln_b.astype(F32).reshape(hd)
    o = o + jnp.sum(r * k * r_k.astype(F32), -1, keepdims=True) * v
    y = o.reshape(Bsz, T, A_WIDTH).astype(pa.dtype) * g
    return y, wkv.astype(wkv0.dtype), pa[:, -1]


def _to_chunks(t, L):
    Bsz, T, H, d = t.shape
    return t.reshape(Bsz, T // L, L, H, d).transpose(1, 0, 3, 2, 4)


def gla_chunked(q, k, v, log_a, S0):
    Bsz, T, H, _ = q.shape
    L = min(CHUNK, T)
    qc, kc, vc, lc = (_to_chunks(t, L) for t in (q, k, v, log_a))
    bc = jnp.cumsum(lc, axis=3)
    mask = jnp.tril(jnp.ones((L, L), bool))

    def step(S, inp):
        q_i, k_i, v_i, b_i = inp
        b_last = b_i[:, :, -1, :]
        q_dec = q_i * jnp.exp(b_i)
        scores = jnp.einsum('bhtd,bhsd->bhts', q_dec, k_i * jnp.exp(-b_i))
        scores = jnp.where(mask, scores, 0.0)
        o = jnp.einsum('bhtd,bhdv->bhtv', q_dec, S) + jnp.einsum('bhts,bhsv->bhtv', scores, v_i)
        k_state = k_i * jnp.exp(b_last[:, :, None, :] - b_i)
        S = S * jnp.exp(b_last)[..., None] + jnp.einsum('bhsd,bhsv->bhdv', k_state, v_i)
        return S, o

    S, o = lax.scan(step, S0.astype(F32), (qc, kc, vc, bc))
    o = o.transpose(1, 0, 3, 2, 4).reshape(Bsz, T, H, v.shape[-1])
    return o, S


def gla_mix(pb, S0, alpha_up, alpha_bias, norm_w):
    Bsz, T, _ = pb.shape
    kw, vw = B_KWIDTH, B_VWIDTH
    q, k, v, xa, rg = jnp.split(pb, [kw, 2 * kw, 2 * kw + vw, 2 * kw + vw + B_GATE_LORA], axis=-1)
    log_a = jax.nn.log_sigmoid((xa @ alpha_up + alpha_bias).astype(F32)) / B_GATE_NORM
    ks = (Bsz, T, B_HEADS, B_KEY_DIM)
    q = q.astype(F32).reshape(ks) * (B_KEY_DIM ** -0.5)
    k = k.astype(F32).reshape(ks)
    log_a = log_a.reshape(ks)
    v = v.astype(F32).reshape(Bsz, T, B_HEADS, B_VAL_DIM)
    o, S = gla_chunked(q, k, v, log_a, S0)
    o = o * lax.rsqrt(jnp.mean(o * o, -1, keepdims=True) + LN_EPS) * norm_w.astype(F32)
    y = o.reshape(Bsz, T, B_VWIDTH).astype(pb.dtype) * jax.nn.silu(rg)
    return y, S.astype(S0.dtype)


def rel_bias_lookup(table, rel):
    idx = jnp.clip(rel, -C_REL_CLIP, C_REL_CLIP) + C_REL_CLIP
    return table.astype(F32)[:, idx]


def band_attention_prompt(q, k, v, rel_table):
    Bsz, H, S, hd = q.shape
    kp = jnp.pad(k, ((0, 0), (0, 0), (C_PAST_ROWS, 0), (0, 0)))
    vp = jnp.pad(v, ((0, 0), (0, 0), (C_PAST_ROWS, 0), (0, 0)))
    i = jnp.arange(CHUNK)
    j = jnp.arange(C_BAND)
    bias = rel_bias_lookup(rel_table, C_PAST_ROWS + i[:, None] - j[None, :])
    scale = hd ** -0.5

    def one_chunk(c):
        qc = lax.dynamic_slice_in_dim(q, c * CHUNK, CHUNK, axis=2)
        kc = lax.dynamic_slice_in_dim(kp, c * CHUNK, C_BAND, axis=2)
        vc = lax.dynamic_slice_in_dim(vp, c * CHUNK, C_BAND, axis=2)
        valid = j >= C_PAST_ROWS - c * CHUNK
        s = jnp.einsum('bhqd,bhkd->bhqk', qc, kc).astype(F32) * scale + bias
        s = jnp.where(valid[None, None, None, :], s, NEG_INF)
        p = jax.nn.softmax(s, axis=-1)
        return jnp.einsum('bhqk,bhkd->bhqd', p.astype(vc.dtype), vc)

    out = lax.map(one_chunk, jnp.arange(S // CHUNK))
    return out.transpose(1, 2, 0, 3, 4).reshape(Bsz, H, S, hd)


def band_attention_sample(q, k_new, v_new, k_cache, v_cache, rel_table):
    R = k_cache.shape[2]
    T = q.shape[2]
    k = jnp.concatenate([k_cache.astype(k_new.dtype), k_new], axis=2)
    v = jnp.concatenate([v_cache.astype(v_new.dtype), v_new], axis=2)
    rel = R + jnp.arange(T)[:, None] - jnp.arange(R + T)[None, :]
    s = jnp.einsum('bhqd,bhkd->bhqk', q, k).astype(F32) * (q.shape[-1] ** -0.5) + rel_bias_lookup(rel_table, rel)
    p = jax.nn.softmax(s, axis=-1)
    return jnp.einsum('bhqk,bhkd->bhqd', p.astype(v.dtype), v)


def chunk_band_attention(x, w_qkv, rel_table, w_o, cache_k, cache_v):
    Bsz, T, _ = x.shape
    qkv = (x @ w_qkv).reshape(Bsz, T, 3, C_HEADS, C_HEAD_DIM)
    q, k, v = (jnp.swapaxes(qkv[:, :, i], 1, 2) for i in range(3))
    if cache_k is None:
        o = band_attention_prompt(q, k, v, rel_table)
        keep = min(C_PAST_ROWS, T)
        k_rows, v_rows = k[:, :, T - keep:], v[:, :, T - keep:]
    else:
        o = band_attention_sample(q, k, v, cache_k, cache_v, rel_table)
        k_rows, v_rows = k, v
    y = jnp.swapaxes(o, 1, 2).reshape(Bsz, T, D_MODEL) @ w_o
    return y, k_rows, v_rows


def run_trunk(x, wkv0, shift0, gla0, cache_k, cache_v, P):
    wkv_o, shift_o, gla_o, k_o, v_o = [], [], [], [], []
    for layer in range(DEPTH):
        if layer % 2 == 0:
            e = layer // 2
            p = x @ P['w_in_mix'][e]
            ya, wkv, sh = rwkv7_time_mix(
                p[..., :A_PROJ], shift0[e], wkv0[e], P['a_mu'][e], P['a_w0'][e], P['a_w2'][e],
                P['a_a0'][e], P['a_a2'][e], P['a_g2'][e], P['a_k_k'][e], P['a_k_a'][e],
                P['a_r_k'][e], P['a_ln_w'][e], P['a_ln_b'][e])
            yb, gs = gla_mix(p[..., A_PROJ:], gla0[e], P['b_alpha_up'][e], P['b_alpha_bias'][e], P['b_norm_w'][e])
            mix = jnp.concatenate([ya, yb], axis=-1) @ P['w_out_mix'][e]
            wkv_o.append(wkv)
            shift_o.append(sh)
            gla_o.append(gs)
        else:
            o = layer // 2
            ck = None if cache_k is None else cache_k[o]
            cv = None if cache_v is None else cache_v[o]
            mix, kr, vr = chunk_band_attention(x, P['c_w_qkv'][o], P['c_rel_bias'][o], P['c_w_o'][o], ck, cv)
            k_o.append(kr)
            v_o.append(vr)
        x = layer_norm(DEEPNORM_ALPHA * x + mix, P['ln1_w'][layer], P['ln1_b'][layer])
        x = layer_norm(DEEPNORM_ALPHA * x + swiglu(x, P['ffn_w_in'][layer], P['ffn_w_out'][layer]),
                       P['ln2_w'][layer], P['ln2_b'][layer])
    return x, jnp.stack(wkv_o), jnp.stack(shift_o), jnp.stack(gla_o), jnp.stack(k_o), jnp.stack(v_o)


def setup_inputs(seed: int = 0) -> dict:
    key = jax.random.key(seed)
    keys = iter(jax.random.split(key, 40))

    def nrm(shape, scale):
        return jax.random.normal(next(keys), shape, F32) * scale

    def uni(shape, lo, hi):
        return jax.random.uniform(next(keys), shape, F32, minval=lo, maxval=hi)

    kv_rows = min(C_PAST_ROWS, PAST_LEN)
    return {
        'x_prompt': nrm((BATCH, SEQ, D_MODEL), 1.0),
        'x_sample': nrm((DEC_BATCH, DEC_SEQ, D_MODEL), 1.0),
        'state_a_wkv': nrm((N_EVEN, DEC_BATCH, A_HEADS, A_HEAD_DIM, A_HEAD_DIM), 0.3),
        'state_a_shift': nrm((N_EVEN, DEC_BATCH, A_PROJ), 1.0),
        'state_b_gla': nrm((N_EVEN, DEC_BATCH, B_HEADS, B_KEY_DIM, B_VAL_DIM), 0.3),
        'cache_c_k': nrm((N_ODD, DEC_BATCH, C_HEADS, kv_rows, C_HEAD_DIM), 1.0),
        'cache_c_v': nrm((N_ODD, DEC_BATCH, C_HEADS, kv_rows, C_HEAD_DIM), 1.0),
        'w_in_mix': nrm((N_EVEN, D_MODEL, MIX_PROJ), D_MODEL ** -0.5),
        'a_mu': uni((N_EVEN, A_PROJ), 0.0, 1.0),
        'a_w0': uni((N_EVEN, A_WIDTH), -6.0, 1.0),
        'a_w2': nrm((N_EVEN, A_DECAY_LORA, A_WIDTH), 0.1 * A_DECAY_LORA ** -0.5),
        'a_a0': nrm((N_EVEN, A_WIDTH), 0.1),
        'a_a2': nrm((N_EVEN, A_ICL_LORA, A_WIDTH), A_ICL_LORA ** -0.5),
        'a_g2': nrm((N_EVEN, A_GATE_LORA, A_WIDTH), A_GATE_LORA ** -0.5),
        'a_k_k': 0.85 + nrm((N_EVEN, A_WIDTH), 0.05),
        'a_k_a': 1.0 + nrm((N_EVEN, A_WIDTH), 0.05),
        'a_r_k': nrm((N_EVEN, A_HEADS, A_HEAD_DIM), 0.1),
        'a_ln_w': 1.0 + nrm((N_EVEN, A_WIDTH), 0.05),
        'a_ln_b': nrm((N_EVEN, A_WIDTH), 0.02),
        'b_alpha_up': nrm((N_EVEN, B_GATE_LORA, B_KWIDTH), B_GATE_LORA ** -0.5),
        'b_alpha_bias': nrm((N_EVEN, B_KWIDTH), 0.1) + 1.0,
        'b_norm_w': 1.0 + nrm((N_EVEN, B_VAL_DIM), 0.05),
        'w_out_mix': nrm((N_EVEN, A_WIDTH + B_VWIDTH, D_MODEL), DEEPNORM_BETA * (A_WIDTH + B_VWIDTH) ** -0.5),
        'c_w_qkv': nrm((N_ODD, D_MODEL, 3 * D_MODEL), D_MODEL ** -0.5),
        'c_rel_bias': nrm((N_ODD, C_HEADS, 2 * C_REL_CLIP + 1), 0.2),
        'c_w_o': nrm((N_ODD, D_MODEL, D_MODEL), DEEPNORM_BETA * D_MODEL ** -0.5),
        'ln1_w': 1.0 + nrm((DEPTH, D_MODEL), 0.05),
        'ln1_b': nrm((DEPTH, D_MODEL), 0.02),
        'ln2_w': 1.0 + nrm((DEPTH, D_MODEL), 0.05),
        'ln2_b': nrm((DEPTH, D_MODEL), 0.02),
        'ffn_w_in': nrm((DEPTH, D_MODEL, 2 * FFN_HIDDEN), D_MODEL ** -0.5),
        'ffn_w_out': nrm((DEPTH, FFN_HIDDEN, D_MODEL), DEEPNORM_BETA * FFN_HIDDEN ** -0.5),
    }


def reference(x_prompt, x_sample, state_a_wkv, state_a_shift, state_b_gla, cache_c_k, cache_c_v,
              w_in_mix, a_mu, a_w0, a_w2, a_a0, a_a2, a_g2, a_k_k, a_k_a, a_r_k, a_ln_w, a_ln_b,
              b_alpha_up, b_alpha_bias, b_norm_w, w_out_mix, c_w_qkv, c_rel_bias, c_w_o,
              ln1_w, ln1_b, ln2_w, ln2_b, ffn_w_in, ffn_w_out):
    P = dict(w_in_mix=w_in_mix, a_mu=a_mu, a_w0=a_w0, a_w2=a_w2, a_a0=a_a0, a_a2=a_a2, a_g2=a_g2,
             a_k_k=a_k_k, a_k_a=a_k_a, a_r_k=a_r_k, a_ln_w=a_ln_w, a_ln_b=a_ln_b,
             b_alpha_up=b_alpha_up, b_alpha_bias=b_alpha_bias, b_norm_w=b_norm_w, w_out_mix=w_out_mix,
             c_w_qkv=c_w_qkv, c_rel_bias=c_rel_bias, c_w_o=c_w_o,
             ln1_w=ln1_w, ln1_b=ln1_b, ln2_w=ln2_w, ln2_b=ln2_b, ffn_w_in=ffn_w_in, ffn_w_out=ffn_w_out)
    bp = x_prompt.shape[0]
    dt = x_prompt.dtype
    wkv_zero = jnp.zeros((N_EVEN, bp, A_HEADS, A_HEAD_DIM, A_HEAD_DIM), dt)
    shift_zero = jnp.zeros((N_EVEN, bp, A_PROJ), dt)
    gla_zero = jnp.zeros((N_EVEN, bp, B_HEADS, B_KEY_DIM, B_VAL_DIM), dt)
    y_prompt, p_wkv, p_shift, p_gla, p_k, p_v = run_trunk(
        x_prompt, wkv_zero, shift_zero, gla_zero, None, None, P)
    y_sample, s_wkv, s_shift, s_gla, s_k, s_v = run_trunk(
        x_sample, state_a_wkv, state_a_shift, state_b_gla, cache_c_k, cache_c_v, P)
    return (y_prompt, y_sample, p_wkv, p_shift, p_gla, p_k, p_v, s_wkv, s_shift, s_gla, s_k, s_v)
```

```python
import os
import numpy as np
import concourse.bass as bass
import concourse.mybir as mybir
from concourse.bass_utils import run_bass_kernel_spmd

F32 = mybir.dt.float32
BF16 = mybir.dt.bfloat16
AF = mybir.ActivationFunctionType
ALU = mybir.AluOpType
AX = mybir.AxisListType


class Op:
    __slots__ = ("eng", "fn", "deps", "signal", "dkey", "ndma", "sem", "semname", "val", "idx")


class Prog:
    ENGS = ("pe", "act", "dve", "pool", "sp")
    EPOCH = int(os.environ.get('EPOCH', '8000'))
    DEPOCH = int(os.environ.get('DEPOCH', '500'))

    def __init__(self, nc):
        self.nc = nc
        self.ops = {e: [] for e in self.ENGS}
        self.lastw = {}
        self.readers = {}
        self.dma_last = {}

    @staticmethod
    def _cls(op):
        return op.dkey if op.dkey is not None else op.eng

    def add(self, eng, fn, r=(), w=(), dkey=None, ndma=1):
        op = Op()
        op.eng = eng
        op.fn = fn
        op.signal = dkey is not None
        op.dkey = dkey
        op.ndma = ndma
        op.sem = None
        op.val = 0
        op.idx = len(self.ops[eng])
        deps = {}

        def push(d):
            if d is None or d is op:
                return
            if d.dkey is None and dkey is None and d.eng == "pe" and eng == "pe":
                return
            c = self._cls(d)
            o = deps.get(c)
            if o is None or o.idx < d.idx:
                deps[c] = d

        for k in r:
            for d in self.lastw.get(k, {}).values():
                push(d)
            if isinstance(k, str) and k.startswith("pb"):
                for d in self.readers.get(k, {}).values():
                    if d.eng != eng:
                        push(d)
        for k in w:
            for d in self.lastw.get(k, {}).values():
                push(d)
            for d in self.readers.get(k, {}).values():
                push(d)
        op.deps = list(deps.values())
        for d in op.deps:
            d.signal = True
        me = self._cls(op)
        for k in r:
            self.readers.setdefault(k, {})[me] = op
        for k in w:
            self.lastw.setdefault(k, {})[me] = op
            self.readers[k] = {}
        self.ops[eng].append(op)
        if dkey is not None:
            self.dma_last[dkey] = op
        return op

    def emit(self, block_cm):
        nc = self.nc
        sems = {}
        stack = []

        def getsem(name):
            s = sems.get(name)
            if s is None:
                cm = nc.semaphore(name)
                s = cm.__enter__()
                stack.append(cm)
                sems[name] = s
            return s

        ecount = {e: 0 for e in self.ENGS}
        dcount = {}
        for e in self.ENGS:
            for op in self.ops[e]:
                if op.dkey is not None:
                    n = dcount.get(op.dkey, 0)
                    ep = n // self.DEPOCH
                    base = ep * self.DEPOCH
                    if (n - base) + op.ndma > self.DEPOCH:
                        n = base + self.DEPOCH
                        ep += 1
                        base = ep * self.DEPOCH
                    n += op.ndma
                    dcount[op.dkey] = n
                    op.semname = "d_%s_%d" % (op.dkey, ep)
                    op.sem = getsem(op.semname)
                    op.val = 16 * (n - base)
                elif op.signal:
                    n = ecount[e]
                    ep = n // self.EPOCH
                    ecount[e] = n + 1
                    op.semname = "e_%s_%d" % (e, ep)
                    op.sem = getsem(op.semname)
                    op.val = n - ep * self.EPOCH + 1
        finals = list(self.dma_last.values())

        with block_cm as block:
            def run(engname, eng):
                seen = {}

                def waits(deps):
                    for d in deps:
                        key = d.semname
                        if seen.get(key, 0) < d.val:
                            eng.wait_ge(d.sem, d.val)
                            seen[key] = d.val

                for op in self.ops[engname]:
                    waits(op.deps)
                    res = op.fn(eng)
                    if op.dkey is not None:
                        if not isinstance(res, (list, tuple)):
                            res = [res]
                        assert len(res) == op.ndma, (len(res), op.ndma)
                        for ins in res:
                            ins.then_inc(op.sem, 16)
                    elif op.signal:
                        res.then_inc(op.sem, 1)
                if engname == "sp":
                    waits(finals)

            @block.tensor
            def _(eng):
                run("pe", eng)

            @block.scalar
            def _(eng):
                run("act", eng)

            @block.vector
            def _(eng):
                run("dve", eng)

            @block.gpsimd
            def _(eng):
                run("pool", eng)

            @block.sync
            def _(eng):
                run("sp", eng)
        for cm in reversed(stack):
            cm.__exit__(None, None, None)


D = 1024
TT = 128
FFH = 2816
ALPHA = (2.0 * 4) ** 0.25
LN_EPS = 1e-5
A_EPS = 64e-5
NEG = -30000.0
ED = float(np.exp(-0.5))


class KB:
    def __init__(self, SEQ, DEPTH):
        self.SEQ = SEQ
        self.DEPTH = DEPTH
        self.NE = (DEPTH + 1) // 2
        self.NO = DEPTH // 2
        self.NG = SEQ // TT
        self.KEEP = min(512, SEQ)
        self.nc = bass.Bass("TRN2", target_bir_lowering=False)
        self.P = Prog(self.nc)
        self.stack = []
        self.tiles = {}
        self.uid = 0

    def sb(self, name, shape, dt=F32):
        cm = self.nc.sbuf_tensor(name, list(shape), dt)
        t = cm.__enter__()
        self.stack.append(cm)
        return t

    def din(self, name, shape):
        return self.nc.dram_tensor(name, list(shape), F32, kind="ExternalInput").ap()

    def dout(self, name, shape):
        return self.nc.dram_tensor(name, list(shape), F32, kind="ExternalOutput").ap()

    def bank(self):
        b = self.rot[self.rot_i % len(self.rot)]
        self.rot_i += 1
        return b

    def mm(self, out, lhsT, rhs, r, w, start=True, stop=True):
        self.P.add("pe", lambda e: e.matmul(out, lhsT, rhs, start=start, stop=stop), r=r, w=w)

    def tr(self, out, in_, idn, r, w):
        self.P.add("pe", lambda e: e.transpose(out, in_, idn), r=list(r) + ["const"], w=w)

    def act(self, out, in_, func, r, w, bias=None, scale=None):
        kw = {}
        if bias is not None:
            kw["bias"] = bias
        if scale is not None:
            kw["scale"] = scale
        self.P.add("act", lambda e: e.activation(out=out, in_=in_, func=func, **kw), r=r, w=w)

    def tt(self, eng, out, in0, in1, op, r, w):
        self.P.add(eng, lambda e: e.tensor_tensor(out=out, in0=in0, in1=in1, op=op), r=r, w=w)

    def stt(self, eng, out, in0, scalar, in1, op0, op1, r, w):
        self.P.add(eng, lambda e: e.scalar_tensor_tensor(out=out, in0=in0, scalar=scalar, in1=in1, op0=op0, op1=op1), r=r, w=w)

    def ts(self, eng, out, in0, s1, s2, op0, op1, r, w):
        self.P.add(eng, lambda e: e.tensor_scalar(out=out, in0=in0, scalar1=s1, scalar2=s2, op0=op0, op1=op1), r=r, w=w)

    def cp(self, eng, out, in_, r, w):
        self.P.add(eng, lambda e: e.tensor_copy(out=out, in_=in_), r=r, w=w)

    def ms(self, eng, ap, val, w):
        self.P.add(eng, lambda e: e.memset(ap, val), w=w)

    def rcp(self, out, in_, r, w):
        self.P.add("dve", lambda e: e.reciprocal(out=out, in_=in_), r=r, w=w)

    def dma(self, q, out, in_, r, w, dkey, slow=False):
        if slow:
            self.P.add(q, lambda e: e.dma_start(out=out, in_=in_, allow_slow_non_contiguous=True), r=r, w=w, dkey=dkey)
        else:
            self.P.add(q, lambda e: e.dma_start(out=out, in_=in_), r=r, w=w, dkey=dkey)

    def dense(self, wname, Wb, passes, rhs_fn, NT):
        for ps in passes:
            segs = ps["segs"]
            tot = sum(n for _, n in segs)
            groups = ps["groups"]
            banks = [self.bank() for _ in groups]
            nks = len(ps["kslabs"])
            for si, (row0, p, nk) in enumerate(ps["kslabs"]):
                slot = self.slab_i % len(self.slabs)
                self.slab_i += 1
                st = self.slabs[slot]
                skey = "slab%d" % slot
                view = st[0:p, 0:nk * tot].rearrange("p (k n) -> p k n", k=nk)
                off = 0
                outs = []
                for (c0, n) in segs:
                    outs.append((view[:, :, off:off + n], Wb[row0:row0 + nk * p, c0:c0 + n].rearrange("(k p) n -> p k n", p=p)))
                    off += n
                self.P.add("sp", lambda e, outs=outs: [e.dma_start(out=o, in_=i) for o, i in outs],
                           r=[("wb", wname)], w=[skey], dkey=skey, ndma=len(outs))
                for gi, (col, M, n) in enumerate(groups):
                    bt, bk = banks[gi]
                    for j in range(n):
                        for kc in range(nk):
                            rap, rkey = rhs_fn(si, kc)
                            self.mm(bt[0:M, j * NT:(j + 1) * NT], view[:, kc, col + j * M: col + (j + 1) * M], rap,
                                    r=[skey, rkey], w=[bk], start=(si == 0 and kc == 0), stop=(si == nks - 1 and kc == nk - 1))
            ps["evac"](banks)


def build(SEQ=8192, DEPTH=4):
    kb = KB(SEQ, DEPTH)
    nc, P = kb.nc, kb.P
    NE, NO, NG, KEEP = kb.NE, kb.NO, kb.NG, kb.KEEP
    NT = TT
    I = {}
    I["x_prompt"] = kb.din("x_prompt", [SEQ, D])
    I["x_sample"] = kb.din("x_sample", [128, D])
    I["state_a_wkv"] = kb.din("state_a_wkv", [NE, 2, 8, 64, 64])
    I["state_a_shift"] = kb.din("state_a_shift", [NE, 2, 1792])
    I["state_b_gla"] = kb.din("state_b_gla", [NE, 2, 4, 64, 128])
    I["cache_c_k"] = kb.din("cache_c_k", [max(NO, 1), 2, 16, 512, 64])
    I["cache_c_v"] = kb.din("cache_c_v", [max(NO, 1), 2, 16, 512, 64])
    wshapes = dict(w_in_mix=[NE, D, 3344], w_out_mix=[NE, D, D], c_w_qkv=[max(NO, 1), D, 3 * D], c_w_o=[max(NO, 1), D, D],
                   ffn_w_in=[DEPTH, D, 2 * FFH], ffn_w_out=[DEPTH, FFH, D])
    for k_, s_ in wshapes.items():
        I[k_] = kb.din(k_, s_)
    small = dict(a_mu=[NE, 1792], a_w0=[NE, 512], a_w2=[NE, 64, 512], a_a0=[NE, 512], a_a2=[NE, 64, 512], a_g2=[NE, 128, 512],
                 a_k_k=[NE, 512], a_k_a=[NE, 512], a_r_k=[NE, 8, 64], a_ln_w=[NE, 512], a_ln_b=[NE, 512],
                 b_alpha_up=[NE, 16, 256], b_alpha_bias=[NE, 256], b_norm_w=[NE, 128], c_rel_bias=[max(NO, 1), 16, 257],
                 ln1_w=[DEPTH, D], ln1_b=[DEPTH, D], ln2_w=[DEPTH, D], ln2_b=[DEPTH, D])
    for k_, s_ in small.items():
        I[k_] = kb.din(k_, s_)
    O = {}
    O["y_p"] = kb.dout("y_p", [SEQ, D])
    O["y_s"] = kb.dout("y_s", [128, D])
    O["p_wkv"] = kb.dout("p_wkv", [NE, 8, 64, 64])
    O["p_shift"] = kb.dout("p_shift", [NE, 1792])
    O["p_gla"] = kb.dout("p_gla", [NE, 4, 64, 128])
    O["p_k"] = kb.dout("p_k", [max(NO, 1), 16, KEEP, 64])
    O["p_v"] = kb.dout("p_v", [max(NO, 1), 16, KEEP, 64])
    O["s_wkv"] = kb.dout("s_wkv", [NE, 2, 8, 64, 64])
    O["s_shift"] = kb.dout("s_shift", [NE, 2, 1792])
    O["s_gla"] = kb.dout("s_gla", [NE, 2, 4, 64, 128])
    O["s_k"] = kb.dout("s_k", [max(NO, 1), 2, 16, 64, 64])
    O["s_v"] = kb.dout("s_v", [max(NO, 1), 2, 16, 64, 64])
    WB = {k_: nc.dram_tensor(k_ + "_bf", list(s_), BF16).ap() for k_, s_ in wshapes.items()}
    EXT = nc.dram_tensor("relext", [16, 512], F32).ap()
    REP_T = nc.dram_tensor("relrep", [16, 128, 512], F32)
    REP = REP_T.ap()

    pbanks = []
    for i in range(8):
        cm = nc.psum_tensor("pb%d" % i, [128, 512], F32)
        t = cm.__enter__()
        kb.stack.append(cm)
        pbanks.append((t, "pb%d" % i))
    fixed = pbanks[0:3]
    kb.rot = pbanks[3:8]
    kb.rot_i = 0
    sb = kb.sb
    kb.slabs = [sb("slab%d" % i, [128, 4096], BF16) for i in range(2)]
    kb.slab_i = 0
    idn = sb("idn", [128, 128]); idnb = sb("idnb", [128, 128], BF16)
    ones = sb("ones", [128, 128]); onesb = sb("onesb", [128, 128], BF16)
    SELb = sb("SELb", [64, 64, 64], BF16)
    X = sb("X", [128, 8, NT]); Xb = sb("Xb", [128, 8, NT], BF16); Z = sb("Z", [128, 8, NT])
    XIO = sb("XIO", [128, D])
    MEAN = sb("MEAN", [128, NT]); MSQ = sb("MSQ", [128, NT]); VAR = sb("VAR", [128, NT]); RSTD = sb("RSTD", [128, NT])
    Hh = sb("Hh", [128, 22, NT], BF16); SGT = sb("SGT", [128, 2, NT])
    ZB = Hh[:, 0:8, :]; ZQ = Hh[:, 8:16, :]
    LNW = sb("LNW", [128, DEPTH, 4, 8])
    ARN = sb("ARN", [128, 6144])
    if NE:
        RKV = ARN[0:64, 0:3072].rearrange("p (c n) -> p c n", c=24); DRKV = ARN[0:64, 3072:6144].rearrange("p (c n) -> p c n", c=24)
        XWA = sb("XWA", [64, 2, NT]); DXWA = sb("DXWA", [64, 2, NT])
        XG = sb("XG", [128, 1, NT]); DXG = sb("DXG", [128, 1, NT])
        BQK = sb("BQK", [64, 8, NT]); BV = sb("BV", [128, 4, NT]); BXA = sb("BXA", [16, 1, NT]); BRG = sb("BRG", [128, 4, NT])
        TH = sb("TH", [64, NT]); SGX = sb("SGX", [128, NT])
        WD = sb("WD", [64, 8, NT]); AA = sb("AA", [64, 8, NT]); GG = sb("GG", [64, 8, NT]); KK = sb("KK", [64, 8, NT])
        NB = sb("NB", [64, 8, NT])
        T1 = DRKV[:, 0:8, :]; T2 = DRKV[:, 8:16, :]; BON = DRKV[:, 16:24, :]
        OT = AA
        AL = sb("AL", [64, 4, NT]); OG = sb("OG", [128, 4, NT]); T3 = BV
        YAb = sb("YAb", [64, 8, NT], BF16); YBb = sb("YBb", [128, 4, NT], BF16)
        VTOKb = sb("VTOKb", [64, 512], BF16); GVTOKb = sb("GVTOKb", [64, 512], BF16)
        RT = [ARN[0:64, 3072 + i * 512:3072 + (i + 1) * 512] for i in range(4)]
        GT = [sb("GT%d" % i, [64, 512]) for i in range(2)]
        STG = XIO[0:64, 0:512]
        HW = [[sb("HW%d_%d" % (e, s), [64, 512]) for s in range(2)] for e in range(NE)]
        SG = [[sb("SG%d_%d" % (e, s), [64, 512]) for s in range(2)] for e in range(NE)]
        SH1 = [[sb("SH1_%d_%d" % (e, s), [64, 24, 1]) for s in range(2)] for e in range(NE)]
        SH2 = [[sb("SH2_%d_%d" % (e, s), [64, 2, 1]) for s in range(2)] for e in range(NE)]
        SH3 = [[sb("SH3_%d_%d" % (e, s), [128, 1, 1]) for s in range(2)] for e in range(NE)]
        EPW = dict(w2=sb("w2s", [64, 512]), a2=sb("a2s", [64, 512]), g2=sb("g2s", [128, 512]), au=sb("aus", [16, 256]))
        EP = []
        for e in range(NE):
            EP.append(dict(mu1=sb("mu1_%d" % e, [64, 24, 1]), mu2=sb("mu2_%d" % e, [64, 2, 1]), mu3=sb("mu3_%d" % e, [128, 1, 1]),
                           w0=sb("w0_%d" % e, [64, 8]), a0=sb("a0_%d" % e, [64, 8]), kk=sb("kk_%d" % e, [64, 8, 1]), ka=sb("ka_%d" % e, [64, 8, 1]),
                           rk=sb("rk_%d" % e, [64, 8, 1]), lw=sb("lw_%d" % e, [64, 8, 1]), lb=sb("lb_%d" % e, [64, 8, 1]),
                           ab=sb("ab_%d" % e, [64, 4]), nw=sb("nw_%d" % e, [128, 1])))
    if NO:
        QTe = sb("QTe", [128, 8, NT], BF16); QTo = sb("QTo", [128, 8, NT], BF16)
        KTf = ARN[:, 0:1024].rearrange("p (c n) -> p c n", c=8); VT = ARN[:, 1024:2048].rearrange("p (c n) -> p c n", c=8)
        KR = [sb("KR%d" % o, [128, 8, 5 * 128], BF16) for o in range(NO)]
        VR = [sb("VR%d" % o, [128, 5, 16, 65], BF16) for o in range(NO)]
        M3s = sb("M3s", [128, 16, 128], BF16); M4s = sb("M4s", [128, 16, 128], BF16)
        M3 = [M3s for o in range(NO)]
        M4 = [M4s for o in range(NO)]
        MSD = nc.dram_tensor("msd", [NO, 2, 128, 16, 128], BF16).ap()
        CH = [sb("CH_%d" % o, [128, 16]) for o in range(NO)]
        M0 = sb("M0", [128, 128], BF16); MLOW = sb("MLOW", [128, 128])
        PT = [sb("PT%d" % i, [128, 5, 128], BF16) for i in range(2)]
        OTOK = ARN[:, 2048:3072].rearrange("p (h d) -> p h d", h=16); RDEN = sb("RDEN", [128, 16, 1]); OTb = sb("OTb", [128, 8, NT], BF16)
        XIO3 = XIO[:, :].rearrange("p (h d) -> p h d", h=16)

    kb.ms("pool", idn[:], 0.0, ["const"])
    P.add("pool", lambda e: e.affine_select(out=idn[:], in_=idn[:], pattern=[[-1, 128]], compare_op=ALU.not_equal, fill=1.0, base=0, channel_multiplier=1), r=["const"], w=["const"])
    kb.ms("pool", ones[:], 1.0, ["const"])
    kb.ms("pool", onesb[:], 1.0, ["const"])
    kb.cp("dve", idnb[:], idn[:], ["const"], ["const"])
    kb.cp("dve", SELb[:], idn[0:64, 0:64].unsqueeze(2).to_broadcast([64, 64, 64]), ["const"], ["const"])
    for l in range(DEPTH):
        for j, nm in enumerate(["ln1_w", "ln1_b", "ln2_w", "ln2_b"]):
            kb.dma("pool", LNW[:, l, j, :], I[nm][l].rearrange("(c p) -> p c", p=128), [], ["const"], "setup", slow=True)
    for e in range(NE):
        ep = EP[e]
        sl = lambda ap, p: ap.rearrange("(c p) -> p c", p=p)
        kb.dma("pool", ep["mu1"][:, :, 0], sl(I["a_mu"][e, 0:1536], 64), [], ["const"], "setup", slow=True)
        kb.dma("pool", ep["mu2"][:, :, 0], sl(I["a_mu"][e, 1536:1664], 64), [], ["const"], "setup", slow=True)
        kb.dma("pool", ep["mu3"][:, :, 0], sl(I["a_mu"][e, 1664:1792], 128), [], ["const"], "setup", slow=True)
        kb.dma("pool", ep["w0"][:, :], sl(I["a_w0"][e], 64), [], ["const"], "setup", slow=True)
        kb.dma("pool", ep["a0"][:, :], sl(I["a_a0"][e], 64), [], ["const"], "setup", slow=True)
        kb.dma("pool", ep["kk"][:, :, 0], sl(I["a_k_k"][e], 64), [], ["const"], "setup", slow=True)
        kb.dma("pool", ep["ka"][:, :, 0], sl(I["a_k_a"][e], 64), [], ["const"], "setup", slow=True)
        kb.dma("pool", ep["rk"][:, :, 0], I["a_r_k"][e].rearrange("h p -> p h"), [], ["const"], "setup", slow=True)
        kb.dma("pool", ep["lw"][:, :, 0], sl(I["a_ln_w"][e], 64), [], ["const"], "setup", slow=True)
        kb.dma("pool", ep["lb"][:, :, 0], sl(I["a_ln_b"][e], 64), [], ["const"], "setup", slow=True)
        kb.dma("pool", ep["ab"][:, :], sl(I["b_alpha_bias"][e], 64), [], ["const"], "setup", slow=True)
        kb.dma("pool", ep["nw"][:, :], sl(I["b_norm_w"][e], 128), [], ["const"], "setup", slow=True)
        kb.act(ep["ab"][:, :], ep["ab"][:, :], AF.Copy, ["const"], ["const"], scale=-1.0)
        for s in range(2):
            for t_ in (HW[e][s], SG[e][s]):
                kb.ms("pool", t_[:], 0.0, [t_.name])
            for t_ in (SH1[e][s], SH2[e][s], SH3[e][s]):
                kb.ms("pool", t_[:], 0.0, [t_.name])
    def cast_w(name, l, rows):
        for r0 in range(0, rows, 128):
            r1 = min(rows, r0 + 128)
            kb.dma("pool", WB[name][l, r0:r1, :], I[name][l, r0:r1, :], [], [("wb", name + str(l))], "cast_" + name + str(l))
    for l in range(DEPTH if not os.environ.get('NOCAST') else 0):
        if l % 2 == 0:
            cast_w("w_in_mix", l // 2, D); cast_w("w_out_mix", l // 2, D)
        else:
            cast_w("c_w_qkv", l // 2, D); cast_w("c_w_o", l // 2, D)
        cast_w("ffn_w_in", l, D); cast_w("ffn_w_out", l, FFH)
    if NO:
        kb.ms("pool", MLOW[:], 1.0, ["const"])
        P.add("pool", lambda e: e.affine_select(out=MLOW[:], in_=MLOW[:], pattern=[[-1, 128]], compare_op=ALU.is_gt, fill=0.0, base=0, channel_multiplier=1), r=["const"], w=["const"])
        kb.ms("pool", M0[:], 0.0, ["const"])
        kb.ms("pool", M0[0:64, 64:128], NEG, ["const"])
        TS8 = XIO[:, :].rearrange("p (h i) -> p h i", h=8)
        for o in range(NO):
            kb.ms("dve", XIO[0:16, 0:512], 0.0, ["XIO"])
            kb.dma("pool", EXT[:, :], XIO[0:16, 0:512], ["XIO"], ["ext"], "setup")
            kb.dma("pool", EXT[:, 0:257], I["c_rel_bias"][o], ["ext"], ["ext"], "setup")
            kb.dma("pool", REP, EXT.unsqueeze(1).to_broadcast([16, 128, 512]), ["ext"], ["rep"], "setup")
            kb.dma("pool", CH[o][:, :], I["c_rel_bias"][o][:, 256].partition_broadcast(128), [], ["const"], "setup", slow=True)
            for (MM, off) in ((M3[o], 256), (M4[o], 128)):
                for hh in range(2):
                    src = bass.AP(REP_T, off + hh * 8 * 65536, [[511, 128], [65536, 8], [1, 128]])
                    kb.dma("pool", TS8, src, ["rep"], ["XIO"], "setup")
                    kb.tt("dve", TS8, TS8, CH[o][:, hh * 8:hh * 8 + 8].unsqueeze(2).to_broadcast([128, 8, 128]), ALU.subtract, ["XIO", "const"], ["XIO"])
                    if off == 256:
                        kb.tt("dve", MM[:, hh * 8:hh * 8 + 8, :], TS8, MLOW[:].unsqueeze(1).to_broadcast([128, 8, 128]), ALU.mult, ["XIO", "const"], ["MB"])
                    else:
                        kb.cp("dve", MM[:, hh * 8:hh * 8 + 8, :], TS8, ["XIO"], ["MB"])
                if off == 128:
                    kb.ms("dve", MM[64:128, :, 0:64], NEG, ["MB"])
                kb.dma("pool", MSD[o, 0 if off == 256 else 1], MM[:], ["MB"], [("msd", o)], "stm")
            kb.ms("pool", VR[o][:, :, :, 64:65], 1.0, ["VR%d" % o])
            kb.ms("pool", QTe[:], 0.0, ["QTe"])
            kb.ms("pool", QTo[:], 0.0, ["QTo"])

    DBG = os.environ.get('DBG')
    dbg_on = [False]

    def dbg(name, ap, key, shape, dt=F32):
        if not (DBG and dbg_on[0]):
            return
        d = nc.dram_tensor("dbg_" + name, list(shape), dt, kind="ExternalOutput").ap()
        kb.dma("pool", d, ap, [key], [], "dbg_" + name)
    ev_flip = [0]

    def evac_copy(out, in_, r, w):
        ev_flip[0] ^= 1
        if ev_flip[0]:
            kb.act(out, in_, AF.Copy, r, w)
        else:
            kb.cp("dve", out, in_, r, w)

    def layer_norm(l, which):
        dbg("z%d_%d" % (l, which), Z[:], "Z", [128, 8, NT])
        kb.act(ZB, Z[:], AF.Copy, ["Z"], ["Hh"])
        kb.tt("dve", ZQ, Z[:], Z[:], ALU.mult, ["Z"], ["Hh"])
        bt, bk = kb.bank()
        for kc in range(8):
            kb.mm(bt[:, 0:NT], onesb[:], ZB[:, kc, :], ["const", "Hh"], [bk], start=(kc == 0), stop=(kc == 7))
        for kc in range(8):
            kb.mm(bt[:, NT:2 * NT], onesb[:], ZQ[:, kc, :], ["const", "Hh"], [bk], start=(kc == 0), stop=(kc == 7))
        kb.act(MEAN[:], bt[:, 0:NT], AF.Copy, [bk], ["MEAN"], scale=1.0 / D)
        kb.tt("dve", MSQ[:], MEAN[:], MEAN[:], ALU.mult, ["MEAN"], ["MSQ"])
        kb.stt("dve", VAR[:], bt[:, NT:2 * NT], 1.0 / D, MSQ[:], ALU.mult, ALU.subtract, [bk, "MSQ"], ["VAR"])
        kb.act(VAR[:], VAR[:], AF.Sqrt, ["VAR"], ["VAR"], bias=LN_EPS)
        kb.rcp(RSTD[:], VAR[:], ["VAR"], ["RSTD"])
        kb.tt("dve", Z[:], Z[:], MEAN[:].unsqueeze(1).to_broadcast([128, 8, NT]), ALU.subtract, ["Z", "MEAN"], ["Z"])
        kb.tt("dve", Z[:], Z[:], RSTD[:].unsqueeze(1).to_broadcast([128, 8, NT]), ALU.mult, ["Z", "RSTD"], ["Z"])
        kb.tt("dve", Z[:], Z[:], LNW[:, l, 2 * which, :].unsqueeze(2).to_broadcast([128, 8, NT]), ALU.mult, ["Z", "const"], ["Z"])
        kb.tt("dve", X[:], Z[:], LNW[:, l, 2 * which + 1, :].unsqueeze(2).to_broadcast([128, 8, NT]), ALU.add, ["Z", "const"], ["X"])
        kb.act(Xb[:], X[:], AF.Copy, ["X"], ["Xb"])
        dbg("x%d_%d" % (l, which), X[:], "X", [128, 8, NT])

    def resid_evac(pi):
        def f(banks):
            per = 4 // len(banks)
            for bi, (bt, bk) in enumerate(banks):
                c0 = 4 * pi + bi * per
                kb.stt("dve", Z[:, c0:c0 + per, :], X[:, c0:c0 + per, :], ALPHA,
                       bt[:, 0:per * NT].rearrange("p (c n) -> p c n", c=per), ALU.mult, ALU.add, ["X", bk], ["Z"])
        return f

    G4 = [(0, 128, 1), (128, 128, 1), (256, 128, 1), (384, 128, 1)]

    def ffn(l):
        def ev(j, n):
            def f(banks):
                (g, gk), (u, uk) = banks
                kb.act(SGT[:, 0:n, :], g[:, 0:n * NT].rearrange("p (c n) -> p c n", c=n), AF.Silu, [gk], ["SGT"])
                kb.tt("dve", Hh[:, 2 * j:2 * j + n, :], SGT[:, 0:n, :], u[:, 0:n * NT].rearrange("p (c n) -> p c n", c=n), ALU.mult, ["SGT", uk], ["Hh"])
            return f
        passes = []
        for j in range(11):
            passes.append(dict(segs=[(j * 256, 256), (FFH + j * 256, 256)], kslabs=[(0, 128, 8)], groups=[(0, 128, 2), (256, 128, 2)], evac=ev(j, 2)))
        kb.dense("ffn_w_in%d" % l, WB["ffn_w_in"][l], passes, lambda si, kc: (Xb[:, kc, :], "Xb"), NT)
        passes = [dict(segs=[(p * 512, 512)], kslabs=[(0, 128, 8), (1024, 128, 8), (2048, 128, 6)], groups=G4, evac=resid_evac(p)) for p in range(2)]
        kb.dense("ffn_w_out%d" % l, WB["ffn_w_out"][l], passes, lambda si, kc: (Hh[:, si * 8 + kc, :], "Hh"), NT)
        layer_norm(l, 1)

    def even_layer(l, segs):
        e = l // 2
        ep = dict(EP[e])
        ep.update(EPW)
        P.add("sp", lambda en: [en.dma_start(out=EPW["w2"][:, :], in_=I["a_w2"][e]), en.dma_start(out=EPW["a2"][:, :], in_=I["a_a2"][e]),
                                en.dma_start(out=EPW["g2"][:, :], in_=I["a_g2"][e]), en.dma_start(out=EPW["au"][:, :], in_=I["b_alpha_up"][e])],
              r=[], w=["EPW"], dkey="ldep", ndma=4)
        hb = lambda ap: ap.to_broadcast([64, 8, NT])

        def gcopy(dst, key, i0, n, M):
            def f(banks, gi):
                bt, bk = banks[gi]
                evac_copy(dst[0:M, i0:i0 + n, :], bt[0:M, 0:n * NT].rearrange("p (c n) -> p c n", c=n), [bk], [key])
            return f

        def mk(segs_, groups, fs):
            def ev(banks):
                for gi, f in enumerate(fs):
                    f(banks, gi)
            return dict(segs=segs_, kslabs=[(0, 128, 8)], groups=groups, evac=ev)
        passes = []
        for s_ in range(3):
            passes.append(mk([(s_ * 512, 512)], [(0, 64, 4), (256, 64, 4)], [gcopy(RKV, "RKV", s_ * 8, 4, 64), gcopy(RKV, "RKV", s_ * 8 + 4, 4, 64)]))
        passes.append(mk([(1536, 512)], [(0, 64, 2), (128, 128, 1), (256, 64, 4)], [gcopy(XWA, "XWA", 0, 2, 64), gcopy(XG, "XG", 0, 1, 128), gcopy(BQK, "BQK", 0, 4, 64)]))
        passes.append(mk([(2048, 512)], [(0, 64, 4), (256, 128, 2)], [gcopy(BQK, "BQK", 4, 4, 64), gcopy(BV, "BV", 0, 2, 128)]))
        passes.append(mk([(2560, 272)], [(0, 128, 2), (256, 16, 1)], [gcopy(BV, "BV", 2, 2, 128), gcopy(BXA, "BXA", 0, 1, 16)]))
        passes.append(mk([(2832, 512)], [(0, 128, 4)], [gcopy(BRG, "BRG", 0, 4, 128)]))
        kb.dense("w_in_mix%d" % e, WB["w_in_mix"][e], passes, lambda si, kc: (Xb[:, kc, :], "Xb"), NT)
        for (T_, Dt, MU, SHL, Pn, C, tk, dk) in ((RKV, DRKV, ep["mu1"], SH1[e], 64, 24, "RKV", "DRKV"), (XWA, DXWA, ep["mu2"], SH2[e], 64, 2, "XWA", "DXWA"),
                                                (XG, DXG, ep["mu3"], SH3[e], 128, 1, "XG", "DXG")):
            for (slot, t0, t1) in segs:
                SH = SHL[slot]
                kb.tt("dve", Dt[:, :, t0 + 1:t1], T_[:, :, t0:t1 - 1], T_[:, :, t0 + 1:t1], ALU.subtract, [tk], [dk])
                kb.tt("dve", Dt[:, :, t0:t0 + 1], SH[:], T_[:, :, t0:t0 + 1], ALU.subtract, [tk, SH.name], [dk])
                kb.cp("dve", SH[:], T_[:, :, t1 - 1:t1], [tk], [SH.name])
            kb.tt("dve", Dt[:], Dt[:], MU[:].to_broadcast([Pn, C, NT]), ALU.mult, [dk, "const"], [dk])
            kb.tt("dve", T_[:], T_[:], Dt[:], ALU.add, [tk, dk], [tk])
        R_ = RKV[:, 0:8, :]; K_ = RKV[:, 8:16, :]; V_ = RKV[:, 16:24, :]
        kb.act(TH[:], XWA[:, 0, :], AF.Tanh, ["XWA"], ["TH"])
        kb.act(SGX[:], XG[:, 0, :], AF.Sigmoid, ["XG"], ["SGX"])
        for (wt, rhs, rk_, dst, dk_, bias, fn) in ((ep["w2"], TH[:], "TH", WD, "WD", ep["w0"], AF.Sigmoid), (ep["a2"], XWA[:, 1, :], "XWA", AA, "AA", ep["a0"], AF.Sigmoid),
                                                  (ep["g2"], SGX[:], "SGX", GG, "GG", None, AF.Copy)):
            for half in range(2):
                bt, bk = kb.bank()
                for j in range(4):
                    h = half * 4 + j
                    kb.mm(bt[0:64, j * NT:(j + 1) * NT], wt[:, h * 64:(h + 1) * 64], rhs, ["EPW", rk_], [bk])
                for j in range(4):
                    h = half * 4 + j
                    if bias is not None:
                        kb.act(dst[:, h, :], bt[0:64, j * NT:(j + 1) * NT], fn, [bk, "const"], [dk_], bias=bias[:, h:h + 1])
                    else:
                        kb.act(dst[:, h, :], bt[0:64, j * NT:(j + 1) * NT], fn, [bk], [dk_])
        kb.act(WD[:], WD[:], AF.Exp, ["WD"], ["WD"], scale=-ED)
        kb.tt("dve", KK[:], K_, hb(ep["kk"][:]), ALU.mult, ["RKV", "const"], ["KK"])
        kb.tt("dve", T1, KK[:], KK[:], ALU.mult, ["KK"], ["DRKV"])
        for half in range(2):
            bt, bk = kb.bank()
            kb.mm(bt[0:64, :], ones[0:64, 0:64], T1[:, half * 4:half * 4 + 4, :], ["const", "DRKV"], [bk])
            kb.act(T2[:, half * 4:half * 4 + 4, :], bt[0:64, :].rearrange("p (c n) -> p c n", c=4), AF.Sqrt, [bk], ["DRKV"], bias=1e-12)
        kb.rcp(T2, T2, ["DRKV"], ["DRKV"])
        kb.tt("dve", KK[:], KK[:], T2, ALU.mult, ["KK", "DRKV"], ["KK"])
        kb.stt("dve", NB[:], KK[:], -1.0, AA[:], ALU.mult, ALU.mult, ["KK", "AA"], ["NB"])
        kb.stt("dve", T1, AA[:], -1.0, hb(ep["ka"][:]), ALU.add, ALU.mult, ["AA", "const"], ["DRKV"])
        kb.stt("dve", K_, T1, 1.0, K_, ALU.add, ALU.mult, ["DRKV", "RKV"], ["RKV"])
        kb.tt("dve", T1, R_, K_, ALU.mult, ["RKV"], ["DRKV"])
        kb.tt("dve", T1, T1, hb(ep["rk"][:]), ALU.mult, ["DRKV", "const"], ["DRKV"])
        for half in range(2):
            bt, bk = kb.bank()
            kb.mm(bt[0:64, :], ones[0:64, 0:64], T1[:, half * 4:half * 4 + 4, :], ["const", "DRKV"], [bk])
            kb.tt("dve", BON[:, half * 4:half * 4 + 4, :], bt[0:64, :].rearrange("p (c n) -> p c n", c=4), V_[:, half * 4:half * 4 + 4, :], ALU.mult, [bk, "RKV"], ["DRKV"])
        bt, bk = kb.bank()
        for h in range(4):
            kb.mm(bt[0:64, h * NT:(h + 1) * NT], ep["au"][:, h * 64:(h + 1) * 64], BXA[:, 0, :], ["EPW", "BXA"], [bk])
        for h in range(4):
            kb.act(AL[:, h, :], bt[0:64, h * NT:(h + 1) * NT], AF.Exp, [bk, "const"], ["AL"], bias=ep["ab"][:, h:h + 1], scale=-1.0)
        kb.act(AL[:], AL[:], AF.Ln, ["AL"], ["AL"], bias=1.0)
        kb.act(AL[:], AL[:], AF.Exp, ["AL"], ["AL"], scale=-1.0 / 16.0)
        kb.act(BQK[:, 0:4, :], BQK[:, 0:4, :], AF.Copy, ["BQK"], ["BQK"], scale=0.125)
        P.add("dve", lambda en: en.memset(RT[0][:, 0:1], 0.0), r=[], w=["DRKV", "RT0", "RT1", "RT2", "RT3"])
        obr, obrk = fixed[0]
        obg, obgk = fixed[1]
        for (slot, t0, t1) in segs:
            Hs = HW[e][slot]; Ss = SG[e][slot]
            hk = Hs.name; sk = Ss.name
            H3 = Hs[:].rearrange("p (h v) -> p h v", h=8)
            S3 = Ss[:].rearrange("p (h v) -> p h v", h=4)
            for c0 in range(t0, t1, 64):
                bt, bk = kb.bank()
                for h in range(8):
                    kb.tr(bt[0:64, h * 64:(h + 1) * 64], V_[:, h, c0:c0 + 64], idn[0:64, 0:64], ["RKV"], [bk])
                kb.act(VTOKb[:], bt[0:64, :], AF.Copy, [bk], ["VTOKb"])
                bt, bk = kb.bank()
                for h in range(4):
                    kb.tr(bt[0:64, h * 128:(h + 1) * 128], BV[:, h, c0:c0 + 64], idn[:, :], ["BV"], [bk])
                kb.cp("dve", GVTOKb[:], bt[0:64, :], [bk], ["GVTOKb"])
                for tl in range(0 if not os.environ.get('SKIP_REC') else 64, 64):
                    t = c0 + tl
                    col = lambda T_: T_[:, :, t:t + 1]
                    kb.tt("pool", RT[0][:].rearrange("p (h v) -> p h v", h=8), H3, col(KK).to_broadcast([64, 8, 64]), ALU.mult, [hk, "KK"], ["RT0"])
                    sa, sak = kb.bank()
                    kb.mm(sa[0:64, :], ones[0:64, 0:64], RT[0][:], ["const", "RT0"], [sak])
                    vb, vbk = kb.bank()
                    kb.mm(vb[0:64, :], SELb[:, tl, :], VTOKb[:], ["const", "VTOKb"], [vbk])
                    kb.tt("pool", RT[1][:].rearrange("p (h v) -> p h v", h=8), H3, col(WD).to_broadcast([64, 8, 64]), ALU.mult, [hk, "WD"], ["RT1"])
                    kb.tt("dve", RT[3][:].rearrange("p (h v) -> p h v", h=8), vb[0:64, :].rearrange("p (h v) -> p h v", h=8), col(K_).to_broadcast([64, 8, 64]), ALU.mult, [vbk, "RKV"], ["RT3"])
                    kb.tt("dve", RT[2][:].rearrange("p (h v) -> p h v", h=8), sa[0:64, :].rearrange("p (h v) -> p h v", h=8), col(NB).to_broadcast([64, 8, 64]), ALU.mult, [sak, "NB"], ["RT2"])
                    kb.tt("dve", RT[2][:], RT[2][:], RT[3][:], ALU.add, ["RT2", "RT3"], ["RT2"])
                    kb.tt("dve", Hs[:], RT[1][:], RT[2][:], ALU.add, ["RT1", "RT2"], [hk])
                    for h in range(8):
                        kb.mm(obr[0:64, h * 64 + tl:h * 64 + tl + 1], Hs[:, h * 64:(h + 1) * 64], R_[:, h, t:t + 1], [hk, "RKV"], [obrk])
                    vg, vgk = kb.bank()
                    kb.mm(vg[0:64, :], SELb[:, tl, :], GVTOKb[:], ["const", "GVTOKb"], [vgk])
                    kb.tt("pool", GT[0][:].rearrange("p (h v) -> p h v", h=4), S3, col(AL).to_broadcast([64, 4, 128]), ALU.mult, [sk, "AL"], ["GT0"])
                    kb.tt("dve", GT[1][:].rearrange("p (h v) -> p h v", h=4), vg[0:64, :].rearrange("p (h v) -> p h v", h=4), BQK[:, 4:8, t:t + 1].to_broadcast([64, 4, 128]), ALU.mult, [vgk, "BQK"], ["GT1"])
                    kb.tt("dve", Ss[:], GT[0][:], GT[1][:], ALU.add, ["GT0", "GT1"], [sk])
                    for h in range(4):
                        kb.mm(obg[:, h * 64 + tl:h * 64 + tl + 1], Ss[:, h * 128:(h + 1) * 128], BQK[:, h, t:t + 1], [sk, "BQK"], [obgk])
                kb.act(OT[:, :, c0:c0 + 64], obr[0:64, :].rearrange("p (h t) -> p h t", h=8), AF.Copy, [obrk], ["AA"])
                kb.cp("dve", OG[:, :, c0:c0 + 64], obg[:, 0:256].rearrange("p (h t) -> p h t", h=4), [obgk], ["OG"])
        P.add("dve", lambda en: en.tensor_tensor(out=T1, in0=OT[:], in1=OT[:], op=ALU.mult), r=["AA"], w=["DRKV", "RT0", "RT1", "RT2", "RT3"])
        for half in range(2):
            hs = slice(half * 4, half * 4 + 4)
            b1, b1k = kb.bank()
            kb.mm(b1[0:64, :], ones[0:64, 0:64], OT[:, hs, :], ["const", "AA"], [b1k])
            b2, b2k = kb.bank()
            kb.mm(b2[0:64, :], ones[0:64, 0:64], T1[:, hs, :], ["const", "DRKV"], [b2k])
            v4 = lambda b: b[0:64, :].rearrange("p (c n) -> p c n", c=4)
            kb.act(T2[:, hs, :], v4(b1), AF.Copy, [b1k], ["DRKV"], scale=1.0 / 64)
            kb.stt("dve", OT[:, hs, :], v4(b1), -1.0 / 64, OT[:, hs, :], ALU.mult, ALU.add, [b1k, "AA"], ["AA"])
            kb.tt("dve", T2[:, hs, :], T2[:, hs, :], T2[:, hs, :], ALU.mult, ["DRKV"], ["DRKV"])
            kb.stt("dve", T2[:, hs, :], v4(b2), 1.0 / 64, T2[:, hs, :], ALU.mult, ALU.subtract, [b2k, "DRKV"], ["DRKV"])
        kb.act(T2, T2, AF.Sqrt, ["DRKV"], ["DRKV"], bias=A_EPS)
        kb.rcp(T2, T2, ["DRKV"], ["DRKV"])
        kb.tt("dve", OT[:], OT[:], T2, ALU.mult, ["AA", "DRKV"], ["AA"])
        kb.tt("dve", OT[:], OT[:], hb(ep["lw"][:]), ALU.mult, ["AA", "const"], ["AA"])
        kb.tt("dve", OT[:], OT[:], hb(ep["lb"][:]), ALU.add, ["AA", "const"], ["AA"])
        kb.tt("dve", OT[:], OT[:], BON, ALU.add, ["AA", "DRKV"], ["AA"])
        kb.tt("dve", YAb[:], OT[:], GG[:], ALU.mult, ["AA", "GG"], ["YAb"])
        kb.tt("dve", T3[:], OG[:], OG[:], ALU.mult, ["OG"], ["BV"])
        bt, bk = kb.bank()
        kb.mm(bt[:, :], ones[:, :], T3[:], ["const", "BV"], [bk])
        kb.act(T3[:], bt[:, :].rearrange("p (c n) -> p c n", c=4), AF.Sqrt, [bk], ["BV"], bias=LN_EPS, scale=1.0 / 128)
        kb.rcp(T3[:], T3[:], ["BV"], ["BV"])
        kb.tt("dve", OG[:], OG[:], T3[:], ALU.mult, ["OG", "BV"], ["OG"])
        kb.act(T3[:], BRG[:], AF.Silu, ["BRG"], ["BV"])
        kb.stt("dve", YBb[:], OG[:], ep["nw"][:, 0:1], T3[:], ALU.mult, ALU.mult, ["OG", "const", "BV"], ["YBb"])
        dbg("ya%d" % l, YAb[:], "YAb", [64, 8, NT], BF16)
        dbg("yb%d" % l, YBb[:], "YBb", [128, 4, NT], BF16)
        dbg("gg%d" % l, GG[:], "GG", [64, 8, NT])
        dbg("ot%d" % l, OT[:], "AA", [64, 8, NT])
        passes = [dict(segs=[(p * 512, 512)], kslabs=[(0, 64, 8), (512, 128, 4)], groups=G4, evac=resid_evac(p)) for p in range(2)]
        kb.dense("w_out_mix%d" % e, WB["w_out_mix"][e], passes, lambda si, kc: ((YAb[:, kc, :], "YAb") if si == 0 else (YBb[:, kc, :], "YBb")), NT)
        layer_norm(l, 0)

    def odd_layer(l, tile):
        o = l // 2
        segs = tile["segs"]
        krk = "KR%d" % o; vrk = "VR%d" % o
        if NO > 1:
            P.add("sp", lambda en: [en.dma_start(out=M3s[:], in_=MSD[o, 0]), en.dma_start(out=M4s[:], in_=MSD[o, 1])],
                  r=[("msd", o)], w=["MB"], dkey="ldm", ndma=2)
        is_p = tile["kind"] == "p"
        g = tile.get("g", 0)
        kslot = (g % 5) if is_p else 4

        def ev_q(ci):
            def f(banks, gi):
                bt, bk = banks[gi]
                v = bt[:, 0:4 * NT].rearrange("p (c n) -> p c n", c=4)
                kb.act(QTe[0:64, ci:ci + 4, :], v[0:64], AF.Copy, [bk], ["QTe"], scale=0.125)
                kb.act(QTo[64:128, ci:ci + 4, :], v[64:128], AF.Copy, [bk], ["QTo"], scale=0.125)
            return f

        def ev_k(ci):
            def f(banks, gi):
                bt, bk = banks[gi]
                v = bt[:, 0:4 * NT].rearrange("p (c n) -> p c n", c=4)
                kb.cp("dve", KTf[:, ci:ci + 4, :], v, [bk], ["KTf"])
            return f

        def ev_v(ci):
            def f(banks, gi):
                bt, bk = banks[gi]
                v = bt[:, 0:4 * NT].rearrange("p (c n) -> p c n", c=4)
                evac_copy(VT[:, ci:ci + 4, :], v, [bk], ["VT"])
            return f

        passes = []
        for j, evf in enumerate([ev_q(0), ev_q(4), ev_k(0), ev_k(4), ev_v(0), ev_v(4)]):
            passes.append(dict(segs=[(j * 512, 512)], kslabs=[(0, 128, 8)], groups=[(0, 128, 4)], evac=(lambda banks, evf=evf: evf(banks, 0))))
        kb.dense("c_w_qkv%d" % o, WB["c_w_qkv"][o], passes, lambda si, kc: (Xb[:, kc, :], "Xb"), NT)

        def tok_major(src, skey, t0, n, dst3, dkey_):
            for half in range(2):
                bt, bk = kb.bank()
                for j in range(4):
                    c = half * 4 + j
                    kb.tr(bt[0:n, j * 128:(j + 1) * 128], src[:, c, t0:t0 + n], idn[:, :], [skey], [bk])
                evac_copy(dst3[0:n, half * 8:half * 8 + 8, :], bt[0:n, :].rearrange("p (h d) -> p h d", h=8), [bk], [dkey_])

        for (slot, t0, t1) in segs:
            n = t1 - t0
            if is_p:
                kb.act(KR[o][:, :, kslot * 128:kslot * 128 + n], KTf[:, :, t0:t1], AF.Copy, ["KTf"], [krk])
            else:
                sq = tile["seqs"][slot]
                for kbk in range(4):
                    kb.dma("sp", XIO3, I["cache_c_k"][o, sq, :, kbk * 128:(kbk + 1) * 128, :].rearrange("h t d -> t h d"), [], ["XIO"], "ldx")
                    for half in range(2):
                        bt, bk = kb.bank()
                        for j in range(4):
                            c = half * 4 + j
                            kb.tr(bt[:, j * 128:(j + 1) * 128], XIO[:, c * 128:(c + 1) * 128], idn[:, :], ["XIO"], [bk])
                        evac_copy(KR[o][:, half * 4:half * 4 + 4, kbk * 128:(kbk + 1) * 128], bt[:, :].rearrange("p (c n) -> p c n", c=4), [bk], [krk])
                    kb.dma("pool", VR[o][:, kbk, :, 0:64], I["cache_c_v"][o, sq, :, kbk * 128:(kbk + 1) * 128, :].rearrange("h t d -> t h d"), [], [vrk], "ldv")
                kb.act(KR[o][:, :, 4 * 128:4 * 128 + n], KTf[:, :, t0:t1], AF.Copy, ["KTf"], [krk])
            tok_major(VT, "VT", t0, n, XIO3, "XIO")
            kb.cp("dve", VR[o][0:n, kslot, :, 0:64], XIO3[0:n], ["XIO"], [vrk])
            emit = (not is_p) or (g >= NG - KEEP // 128)
            if emit:
                if is_p:
                    r0 = (g - (NG - KEEP // 128)) * 128
                    dv = O["p_v"][o, :, r0:r0 + 128, :]; dk_ = O["p_k"][o, :, r0:r0 + 128, :]
                else:
                    dv = O["s_v"][o, slot, :, :, :]; dk_ = O["s_k"][o, slot, :, :, :]
                kb.dma("pool", dv.rearrange("h t d -> t h d"), XIO3[0:n], ["XIO"], [], "stx")
                tok_major(KTf, "KTf", t0, n, XIO3, "XIO")
                kb.dma("pool", dk_.rearrange("h t d -> t h d"), XIO3[0:n], ["XIO"], [], "stx")
            if is_p:
                blocks = []
                for b in range(5):
                    kbi = g - 4 + b
                    if kbi < 0:
                        continue
                    bias = {0: "M0", 3: "M3", 4: "M4"}.get(b)
                    blocks.append((kbi % 5, 128, bias))
            else:
                blocks = [(0, 128, None), (1, 128, None), (2, 128, None), (3, 128, "M3"), (4, n, "M4")]
            nb_ = len(blocks)
            obanks = fixed
            for h in range(16 if not os.environ.get('SKIP_ATT') else 0):
                c = h // 2
                QTs = QTe if h % 2 == 0 else QTo
                qk = "QTe" if h % 2 == 0 else "QTo"
                pt = PT[h % 2]; ptk = "PT%d" % (h % 2)
                b1 = kb.bank(); b2 = kb.bank()
                for bi, (ks, nk, bias) in enumerate(blocks):
                    bt, bk = (b1 if bi < 4 else b2)
                    oc = (bi % 4) * 128
                    kb.mm(bt[0:nk, oc:oc + n], KR[o][:, c, ks * 128:ks * 128 + nk], QTs[:, c, t0:t1], [krk, qk], [bk], start=True, stop=(bias is None))
                    if bias == "M0":
                        kb.mm(bt[0:nk, oc:oc + n], idnb[0:nk, 0:nk], M0[0:nk, 0:n], ["const"], [bk], start=False, stop=True)
                    elif bias == "M3":
                        kb.mm(bt[0:nk, oc:oc + n], idnb[0:nk, 0:nk], M3[o][0:nk, h, 0:n], ["const", "MB"], [bk], start=False, stop=True)
                    elif bias == "M4":
                        kb.mm(bt[0:nk, oc:oc + n], idnb[0:nk, 0:nk], M4[o][0:nk, h, 0:n], ["const", "MB"], [bk], start=False, stop=True)
                for bi, (ks, nk, bias) in enumerate(blocks):
                    bt, bk = (b1 if bi < 4 else b2)
                    oc = (bi % 4) * 128
                    kb.act(pt[0:nk, bi, 0:n], bt[0:nk, oc:oc + n], AF.Exp, [bk, "const"], [ptk], bias=CH[o][0:nk, h:h + 1])
                ob, obk = obanks[h // 7]
                hc = (h % 7) * 65
                for bi, (ks, nk, bias) in enumerate(blocks):
                    kb.mm(ob[0:n, hc:hc + 65], pt[0:nk, bi, 0:n], VR[o][0:nk, ks, h, :], [ptk, vrk], [obk], start=(bi == 0), stop=(bi == nb_ - 1))
            for bi_, (h0, nh) in enumerate(((0, 7), (7, 7), (14, 2))):
                ob, obk = obanks[bi_]
                v = ob[0:n, 0:nh * 65].rearrange("p (h d) -> p h d", h=nh)
                kb.rcp(RDEN[0:n, h0:h0 + nh, :], v[:, :, 64:65], [obk], ["RDEN"])
                kb.tt("dve", OTOK[0:n, h0:h0 + nh, :], v[:, :, 0:64], RDEN[0:n, h0:h0 + nh, :].to_broadcast([n, nh, 64]), ALU.mult, [obk, "RDEN"], ["OTOK"])
            OT2 = OTOK[:, :, :].rearrange("p h d -> p (h d)")
            for half in range(2):
                bt, bk = kb.bank()
                for j in range(4):
                    c = half * 4 + j
                    kb.tr(bt[:, j * 128:j * 128 + n], OT2[0:n, c * 128:(c + 1) * 128], idn[0:n, 0:n], ["OTOK"], [bk])
                evac_copy(OTb[:, half * 4:half * 4 + 4, t0:t1], bt[:, :].rearrange("p (c n) -> p c n", c=4)[:, :, 0:n], [bk], ["OTb"])
        passes = [dict(segs=[(p * 512, 512)], kslabs=[(0, 128, 8)], groups=[(0, 128, 4)], evac=resid_evac(p)) for p in range(2)]
        kb.dense("c_w_o%d" % o, WB["c_w_o"][o], passes, lambda si, kc: (OTb[:, kc, :], "OTb"), NT)
        layer_norm(l, 0)

    def run_tile(tile):
        xsrc = tile["x"]
        IOSTEP = int(os.environ.get('IOSTEP', '9'))
        kb.dma("sp", XIO[:, :], xsrc, [], ["XIO"], "ldx")
        for half in range(2 if IOSTEP >= 2 else 0):
            bt, bk = kb.bank()
            for j in range(4):
                c = half * 4 + j
                kb.tr(bt[:, j * 128:(j + 1) * 128], XIO[:, c * 128:(c + 1) * 128], idn[:, :], ["XIO"], [bk])
            v = bt[:, :].rearrange("p (c n) -> p c n", c=4)
            if IOSTEP >= 3:
                kb.act(X[:, half * 4:half * 4 + 4, :], v, AF.Copy, [bk], ["X"])
            if IOSTEP >= 4:
                kb.cp("dve", Xb[:, half * 4:half * 4 + 4, :], v, [bk, "X"], ["Xb"])
        MAXL = int(os.environ.get('MAXL', '99'))
        NOFFN = os.environ.get('NOFFN')
        NOMIX = os.environ.get('NOMIX')
        for l in range(min(DEPTH, MAXL)):
            if NOMIX:
                pass
            elif l % 2 == 0:
                even_layer(l, tile["segs"])
            else:
                odd_layer(l, tile)
            if not NOFFN:
                ffn(l)
        for half in range(2 if IOSTEP >= 5 else 0):
            bt, bk = kb.bank()
            for j in range(4):
                c = half * 4 + j
                kb.tr(bt[:, j * 128:(j + 1) * 128], X[:, c, :], idn[:, :], ["X"], [bk])
            if IOSTEP >= 6:
                evac_copy(XIO[:, half * 512:(half + 1) * 512], bt[:, :], [bk], ["XIO"])
        kb.dma("pool", tile["y"], XIO[:, :], ["XIO"], [], "stx")

    def state_out(e, slot, dw, dsh, dg):
        bt, bk = kb.bank()
        for h in range(8):
            kb.tr(bt[0:64, h * 64:(h + 1) * 64], HW[e][slot][:, h * 64:(h + 1) * 64], idn[0:64, 0:64], [HW[e][slot].name], [bk])
        kb.act(STG, bt[0:64, :], AF.Copy, [bk], ["XIO"])
        kb.dma("pool", dw.rearrange("h v k -> v h k"), STG.rearrange("p (h k) -> p h k", h=8), ["XIO"], [], "stx")
        kb.dma("pool", dg.rearrange("h d v -> d h v"), SG[e][slot][:].rearrange("p (h v) -> p h v", h=4), [SG[e][slot].name], [], "sts_g%d" % e)
        kb.dma("pool", dsh[0:1536].rearrange("(c p) -> p c", p=64), SH1[e][slot][:, :, 0], [SH1[e][slot].name], [], "sts_1%d" % e, slow=True)
        kb.dma("pool", dsh[1536:1664].rearrange("(c p) -> p c", p=64), SH2[e][slot][:, :, 0], [SH2[e][slot].name], [], "sts_2%d" % e, slow=True)
        kb.dma("pool", dsh[1664:1792].rearrange("(c p) -> p c", p=128), SH3[e][slot][:, :, 0], [SH3[e][slot].name], [], "sts_3%d" % e, slow=True)

    STOP = os.environ.get('STOP', '')
    for g in range(NG if STOP != 'setup' else 0):
        dbg_on[0] = (g == 0)
        run_tile(dict(kind="p", g=g, segs=[(0, 0, 128)], x=I["x_prompt"][g * 128:(g + 1) * 128, :], y=O["y_p"][g * 128:(g + 1) * 128, :]))
    for e in range(NE if STOP not in ('setup', 'p2') else 0):
        state_out(e, 0, O["p_wkv"][e], O["p_shift"][e], O["p_gla"][e])
    dbg_on[0] = False
    for e in range(NE if STOP == '' else 0):
        for slot in range(2):
            kb.dma("sp", STG.rearrange("p (h k) -> p h k", h=8), I["state_a_wkv"][e, slot].rearrange("h v k -> v h k"), [], ["XIO"], "ldx")
            bt, bk = kb.bank()
            for h in range(8):
                kb.tr(bt[0:64, h * 64:(h + 1) * 64], STG[:, h * 64:(h + 1) * 64], idn[0:64, 0:64], ["XIO"], [bk])
            kb.act(HW[e][slot][:], bt[0:64, :], AF.Copy, [bk], [HW[e][slot].name])
            kb.dma("sp", SG[e][slot][:].rearrange("p (h v) -> p h v", h=4), I["state_b_gla"][e, slot].rearrange("h d v -> d h v"), [], [SG[e][slot].name], "ldsg")
            sh = I["state_a_shift"][e, slot]
            kb.dma("pool", SH1[e][slot][:, :, 0], sh[0:1536].rearrange("(c p) -> p c", p=64), [], [SH1[e][slot].name], "ldsh1", slow=True)
            kb.dma("pool", SH2[e][slot][:, :, 0], sh[1536:1664].rearrange("(c p) -> p c", p=64), [], [SH2[e][slot].name], "ldsh2", slow=True)
            kb.dma("pool", SH3[e][slot][:, :, 0], sh[1664:1792].rearrange("(c p) -> p c", p=128), [], [SH3[e][slot].name], "ldsh3", slow=True)
    if STOP == '':
      run_tile(dict(kind="s", segs=[(0, 0, 64), (1, 64, 128)], seqs=[0, 1], x=I["x_sample"][:, :], y=O["y_s"][:, :]))
    for e in range(NE if STOP == '' else 0):
        for slot in range(2):
            state_out(e, slot, O["s_wkv"][e, slot], O["s_shift"][e, slot], O["s_gla"][e, slot])
    P.emit(nc.Block())
    for cm in reversed(kb.stack):
        cm.__exit__(None, None, None)
    return nc


_OUT_ORDER = ["y_p", "y_s", "p_wkv", "p_shift", "p_gla", "p_k", "p_v", "s_wkv", "s_shift", "s_gla", "s_k", "s_v"]


def run(inputs, SEQ=8192, DEPTH=4, ncores=8):
    nc = build(SEQ, DEPTH)
    f = lambda a: np.ascontiguousarray(np.asarray(a, dtype=np.float32))
    in_maps = []
    for c in range(ncores):
        m = {"x_prompt": f(inputs["x_prompt"][c]), "x_sample": f(inputs["x_sample"][2 * c:2 * c + 2]).reshape(128, D)}
        for k_ in ("state_a_wkv", "state_a_shift", "state_b_gla", "cache_c_k", "cache_c_v"):
            m[k_] = f(np.asarray(inputs[k_])[:, 2 * c:2 * c + 2])
        for k_ in inputs:
            if k_ not in m:
                m[k_] = f(inputs[k_])
        in_maps.append(m)
    res = run_bass_kernel_spmd(nc, in_maps, core_ids=list(range(ncores)))
    R = res.results
    cat = lambda name, ax: np.stack([np.asarray(r[name]) for r in R], axis=ax)
    y_p = cat("y_p", 0)
    y_s = np.concatenate([np.asarray(r["y_s"]).reshape(2, 64, D) for r in R], axis=0)
    p_wkv = cat("p_wkv", 1); p_shift = cat("p_shift", 1); p_gla = cat("p_gla", 1)
    p_k = cat("p_k", 1); p_v = cat("p_v", 1)
    c2 = lambda name: np.concatenate([np.asarray(r[name]) for r in R], axis=1)
    return (y_p, y_s, p_wkv, p_shift, p_gla, p_k, p_v, c2("s_wkv"), c2("s_shift"), c2("s_gla"), c2("s_k"), c2("s_v"))


def kernel(**inputs):
    outs = run(inputs)
    return tuple(np.ascontiguousarray(o, dtype=np.float32) for o in outs)
```

```python
import os
import numpy as np
import concourse.bass as bass
import concourse.mybir as mybir
from concourse.bass_utils import run_bass_kernel_spmd

F32 = mybir.dt.float32
BF16 = mybir.dt.bfloat16
AF = mybir.ActivationFunctionType
ALU = mybir.AluOpType
AX = mybir.AxisListType


class Op:
    __slots__ = ("eng", "fn", "deps", "signal", "dkey", "ndma", "sem", "semname", "val", "idx")


class Prog:
    ENGS = ("pe", "act", "dve", "pool", "sp")
    EPOCH = int(os.environ.get('EPOCH', '8000'))
    DEPOCH = int(os.environ.get('DEPOCH', '500'))

    def __init__(self, nc):
        self.nc = nc
        self.ops = {e: [] for e in self.ENGS}
        self.lastw = {}
        self.readers = {}
        self.dma_last = {}

    @staticmethod
    def _cls(op):
        return op.dkey if op.dkey is not None else op.eng

    def add(self, eng, fn, r=(), w=(), dkey=None, ndma=1):
        op = Op()
        op.eng = eng
        op.fn = fn
        op.signal = dkey is not None
        op.dkey = dkey
        op.ndma = ndma
        op.sem = None
        op.val = 0
        op.idx = len(self.ops[eng])
        deps = {}

        def push(d):
            if d is None or d is op:
                return
            if d.dkey is None and dkey is None and d.eng == "pe" and eng == "pe":
                return
            c = self._cls(d)
            o = deps.get(c)
            if o is None or o.idx < d.idx:
                deps[c] = d

        for k in r:
            for d in self.lastw.get(k, {}).values():
                push(d)
            if isinstance(k, str) and k.startswith("pb"):
                for d in self.readers.get(k, {}).values():
                    if d.eng != eng:
                        push(d)
        for k in w:
            for d in self.lastw.get(k, {}).values():
                push(d)
            for d in self.readers.get(k, {}).values():
                push(d)
        op.deps = list(deps.values())
        for d in op.deps:
            d.signal = True
        me = self._cls(op)
        for k in r:
            self.readers.setdefault(k, {})[me] = op
        for k in w:
            self.lastw.setdefault(k, {})[me] = op
            self.readers[k] = {}
        self.ops[eng].append(op)
        if dkey is not None:
            self.dma_last[dkey] = op
        return op

    def emit(self, block_cm):
        nc = self.nc
        sems = {}
        stack = []

        def getsem(name):
            s = sems.get(name)
            if s is None:
                cm = nc.semaphore(name)
                s = cm.__enter__()
                stack.append(cm)
                sems[name] = s
            return s

        ecount = {e: 0 for e in self.ENGS}
        dcount = {}
        for e in self.ENGS:
            for op in self.ops[e]:
                if op.dkey is not None:
                    n = dcount.get(op.dkey, 0)
                    ep = n // self.DEPOCH
                    base = ep * self.DEPOCH
                    if (n - base) + op.ndma > self.DEPOCH:
                        n = base + self.DEPOCH
                        ep += 1
                        base = ep * self.DEPOCH
                    n += op.ndma
                    dcount[op.dkey] = n
                    op.semname = "d_%s_%d" % (op.dkey, ep)
                    op.sem = getsem(op.semname)
                    op.val = 16 * (n - base)
                elif op.signal:
                    n = ecount[e]
                    ep = n // self.EPOCH
                    ecount[e] = n + 1
                    op.semname = "e_%s_%d" % (e, ep)
                    op.sem = getsem(op.semname)
                    op.val = n - ep * self.EPOCH + 1
        finals = list(self.dma_last.values())

        with block_cm as block:
            def run(engname, eng):
                seen = {}

                def waits(deps):
                    for d in deps:
                        key = d.semname
                        if seen.get(key, 0) < d.val:
                            eng.wait_ge(d.sem, d.val)
                            seen[key] = d.val

                for op in self.ops[engname]:
                    waits(op.deps)
                    res = op.fn(eng)
                    if op.dkey is not None:
                        if not isinstance(res, (list, tuple)):
                            res = [res]
                        assert len(res) == op.ndma, (len(res), op.ndma)
                        for ins in res:
                            ins.then_inc(op.sem, 16)
                    elif op.signal:
                        res.then_inc(op.sem, 1)
                if engname == "sp":
                    waits(finals)

            @block.tensor
            def _(eng):
                run("pe", eng)

            @block.scalar
            def _(eng):
                run("act", eng)

            @block.vector
            def _(eng):
                run("dve", eng)

            @block.gpsimd
            def _(eng):
                run("pool", eng)

            @block.sync
            def _(eng):
                run("sp", eng)
        for cm in reversed(stack):
            cm.__exit__(None, None, None)


D = 1024
TT = 128
FFH = 2816
ALPHA = (2.0 * 4) ** 0.25
LN_EPS = 1e-5
A_EPS = 64e-5
NEG = -30000.0
ED = float(np.exp(-0.5))


class KB:
    def __init__(self, SEQ, DEPTH):
        self.SEQ = SEQ
        self.DEPTH = DEPTH
        self.NE = (DEPTH + 1) // 2
        self.NO = DEPTH // 2
        self.NG = SEQ // TT
        self.KEEP = min(512, SEQ)
        self.nc = bass.Bass("TRN2", target_bir_lowering=False)
        self.P = Prog(self.nc)
        self.stack = []
        self.tiles = {}
        self.uid = 0

    def sb(self, name, shape, dt=F32):
        cm = self.nc.sbuf_tensor(name, list(shape), dt)
        t = cm.__enter__()
        self.stack.append(cm)
        return t

    def din(self, name, shape):
        return self.nc.dram_tensor(name, list(shape), F32, kind="ExternalInput").ap()

    def dout(self, name, shape):
        return self.nc.dram_tensor(name, list(shape), F32, kind="ExternalOutput").ap()

    def bank(self):
        b = self.rot[self.rot_i % len(self.rot)]
        self.rot_i += 1
        return b

    def mm(self, out, lhsT, rhs, r, w, start=True, stop=True):
        self.P.add("pe", lambda e: e.matmul(out, lhsT, rhs, start=start, stop=stop), r=r, w=w)

    def tr(self, out, in_, idn, r, w):
        self.P.add("pe", lambda e: e.transpose(out, in_, idn), r=list(r) + ["const"], w=w)

    def act(self, out, in_, func, r, w, bias=None, scale=None):
        kw = {}
        if bias is not None:
            kw["bias"] = bias
        if scale is not None:
            kw["scale"] = scale
        self.P.add("act", lambda e: e.activation(out=out, in_=in_, func=func, **kw), r=r, w=w)

    def tt(self, eng, out, in0, in1, op, r, w):
        self.P.add(eng, lambda e: e.tensor_tensor(out=out, in0=in0, in1=in1, op=op), r=r, w=w)

    def stt(self, eng, out, in0, scalar, in1, op0, op1, r, w):
        self.P.add(eng, lambda e: e.scalar_tensor_tensor(out=out, in0=in0, scalar=scalar, in1=in1, op0=op0, op1=op1), r=r, w=w)

    def ts(self, eng, out, in0, s1, s2, op0, op1, r, w):
        self.P.add(eng, lambda e: e.tensor_scalar(out=out, in0=in0, scalar1=s1, scalar2=s2, op0=op0, op1=op1), r=r, w=w)

    def cp(self, eng, out, in_, r, w):
        self.P.add(eng, lambda e: e.tensor_copy(out=out, in_=in_), r=r, w=w)

    def ms(self, eng, ap, val, w):
        self.P.add(eng, lambda e: e.memset(ap, val), w=w)

    def rcp(self, out, in_, r, w):
        self.P.add("dve", lambda e: e.reciprocal(out=out, in_=in_), r=r, w=w)

    def dma(self, q, out, in_, r, w, dkey, slow=False):
        if slow:
            self.P.add(q, lambda e: e.dma_start(out=out, in_=in_, allow_slow_non_contiguous=True), r=r, w=w, dkey=dkey)
        else:
            self.P.add(q, lambda e: e.dma_start(out=out, in_=in_), r=r, w=w, dkey=dkey)

    def dense(self, wname, Wb, passes, rhs_fn, NT):
        for ps in passes:
            segs = ps["segs"]
            tot = sum(n for _, n in segs)
            groups = ps["groups"]
            banks = [self.bank() for _ in groups]
            nks = len(ps["kslabs"])
            for si, (row0, p, nk) in enumerate(ps["kslabs"]):
                slot = self.slab_i % len(self.slabs)
                self.slab_i += 1
                st = self.slabs[slot]
                skey = "slab%d" % slot
                view = st[0:p, 0:nk * tot].rearrange("p (k n) -> p k n", k=nk)
                off = 0
                outs = []
                for (c0, n) in segs:
                    outs.append((view[:, :, off:off + n], Wb[row0:row0 + nk * p, c0:c0 + n].rearrange("(k p) n -> p k n", p=p)))
                    off += n
                self.P.add("sp", lambda e, outs=outs: [e.dma_start(out=o, in_=i) for o, i in outs],
                           r=[("wb", wname)], w=[skey], dkey=skey, ndma=len(outs))
                for gi, (col, M, n) in enumerate(groups):
                    bt, bk = banks[gi]
                    for j in range(n):
                        for kc in range(nk):
                            rap, rkey = rhs_fn(si, kc)
                            self.mm(bt[0:M, j * NT:(j + 1) * NT], view[:, kc, col + j * M: col + (j + 1) * M], rap,
                                    r=[skey, rkey], w=[bk], start=(si == 0 and kc == 0), stop=(si == nks - 1 and kc == nk - 1))
            ps["evac"](banks)


def build(SEQ=8192, DEPTH=4):
    kb = KB(SEQ, DEPTH)
    nc, P = kb.nc, kb.P
    NE, NO, NG, KEEP = kb.NE, kb.NO, kb.NG, kb.KEEP
    NT = TT
    I = {}
    I["x_prompt"] = kb.din("x_prompt", [SEQ, D])
    I["x_sample"] = kb.din("x_sample", [128, D])
    I["state_a_wkv"] = kb.din("state_a_wkv", [NE, 2, 8, 64, 64])
    I["state_a_shift"] = kb.din("state_a_shift", [NE, 2, 1792])
    I["state_b_gla"] = kb.din("state_b_gla", [NE, 2, 4, 64, 128])
    I["cache_c_k"] = kb.din("cache_c_k", [max(NO, 1), 2, 16, 512, 64])
    I["cache_c_v"] = kb.din("cache_c_v", [max(NO, 1), 2, 16, 512, 64])
    wshapes = dict(w_in_mix=[NE, D, 3344], w_out_mix=[NE, D, D], c_w_qkv=[max(NO, 1), D, 3 * D], c_w_o=[max(NO, 1), D, D],
                   ffn_w_in=[DEPTH, D, 2 * FFH], ffn_w_out=[DEPTH, FFH, D])
    for k_, s_ in wshapes.items():
        I[k_] = kb.din(k_, s_)
    small = dict(a_mu=[NE, 1792], a_w0=[NE, 512], a_w2=[NE, 64, 512], a_a0=[NE, 512], a_a2=[NE, 64, 512], a_g2=[NE, 128, 512],
                 a_k_k=[NE, 512], a_k_a=[NE, 512], a_r_k=[NE, 8, 64], a_ln_w=[NE, 512], a_ln_b=[NE, 512],
                 b_alpha_up=[NE, 16, 256], b_alpha_bias=[NE, 256], b_norm_w=[NE, 128], c_rel_bias=[max(NO, 1), 16, 257],
                 ln1_w=[DEPTH, D], ln1_b=[DEPTH, D], ln2_w=[DEPTH, D], ln2_b=[DEPTH, D])
    for k_, s_ in small.items():
        I[k_] = kb.din(k_, s_)
    O = {}
    O["y_p"] = kb.dout("y_p", [SEQ, D])
    O["y_s"] = kb.dout("y_s", [128, D])
    O["p_wkv"] = kb.dout("p_wkv", [NE, 8, 64, 64])
    O["p_shift"] = kb.dout("p_shift", [NE, 1792])
    O["p_gla"] = kb.dout("p_gla", [NE, 4, 64, 128])
    O["p_k"] = kb.dout("p_k", [max(NO, 1), 16, KEEP, 64])
    O["p_v"] = kb.dout("p_v", [max(NO, 1), 16, KEEP, 64])
    O["s_wkv"] = kb.dout("s_wkv", [NE, 2, 8, 64, 64])
    O["s_shift"] = kb.dout("s_shift", [NE, 2, 1792])
    O["s_gla"] = kb.dout("s_gla", [NE, 2, 4, 64, 128])
    O["s_k"] = kb.dout("s_k", [max(NO, 1), 2, 16, 64, 64])
    O["s_v"] = kb.dout("s_v", [max(NO, 1), 2, 16, 64, 64])
    WB = {k_: nc.dram_tensor(k_ + "_bf", list(s_), BF16).ap() for k_, s_ in wshapes.items()}
    EXT = nc.dram_tensor("relext", [16, 512], F32).ap()
    REP_T = nc.dram_tensor("relrep", [16, 128, 512], F32)
    REP = REP_T.ap()

    pbanks = []
    for i in range(8):
        cm = nc.psum_tensor("pb%d" % i, [128, 512], F32)
        t = cm.__enter__()
        kb.stack.append(cm)
        pbanks.append((t, "pb%d" % i))
    fixed = pbanks[0:3]
    kb.rot = pbanks[3:8]
    kb.rot_i = 0
    sb = kb.sb
    kb.slabs = [sb("slab%d" % i, [128, 4096], BF16) for i in range(2)]
    kb.slab_i = 0
    idn = sb("idn", [128, 128]); idnb = sb("idnb", [128, 128], BF16)
    ones = sb("ones", [128, 128]); onesb = sb("onesb", [128, 128], BF16)
    SELb = sb("SELb", [64, 64, 64], BF16)
    X = sb("X", [128, 8, NT]); Xb = sb("Xb", [128, 8, NT], BF16); Z = sb("Z", [128, 8, NT])
    XIO = sb("XIO", [128, D])
    MEAN = sb("MEAN", [128, NT]); MSQ = sb("MSQ", [128, NT]); VAR = sb("VAR", [128, NT]); RSTD = sb("RSTD", [128, NT])
    Hh = sb("Hh", [128, 22, NT], BF16); SGT = sb("SGT", [128, 2, NT])
    ZB = Hh[:, 0:8, :]; ZQ = Hh[:, 8:16, :]
    LNW = sb("LNW", [128, DEPTH, 4, 8])
    ARN = sb("ARN", [128, 6144])
    if NE:
        RKV = ARN[0:64, 0:3072].rearrange("p (c n) -> p c n", c=24); DRKV = ARN[0:64, 3072:6144].rearrange("p (c n) -> p c n", c=24)
        XWA = sb("XWA", [64, 2, NT]); DXWA = sb("DXWA", [64, 2, NT])
        XG = sb("XG", [128, 1, NT]); DXG = sb("DXG", [128, 1, NT])
        BQK = sb("BQK", [64, 8, NT]); BV = sb("BV", [128, 4, NT]); BXA = sb("BXA", [16, 1, NT]); BRG = sb("BRG", [128, 4, NT])
        TH = sb("TH", [64, NT]); SGX = sb("SGX", [128, NT])
        WD = sb("WD", [64, 8, NT]); AA = sb("AA", [64, 8, NT]); GG = sb("GG", [64, 8, NT]); KK = sb("KK", [64, 8, NT])
        NB = sb("NB", [64, 8, NT])
        T1 = DRKV[:, 0:8, :]; T2 = DRKV[:, 8:16, :]; BON = DRKV[:, 16:24, :]
        OT = AA
        AL = sb("AL", [64, 4, NT]); OG = sb("OG", [128, 4, NT]); T3 = BV
        YAb = sb("YAb", [64, 8, NT], BF16); YBb = sb("YBb", [128, 4, NT], BF16)
        VTOKb = sb("VTOKb", [64, 512], BF16); GVTOKb = sb("GVTOKb", [64, 512], BF16)
        RT = [ARN[0:64, 3072 + i * 512:3072 + (i + 1) * 512] for i in range(4)]
        GT = [sb("GT%d" % i, [64, 512]) for i in range(2)]
        STG = XIO[0:64, 0:512]
        HW = [[sb("HW%d_%d" % (e, s), [64, 512]) for s in range(2)] for e in range(NE)]
        SG = [[sb("SG%d_%d" % (e, s), [64, 512]) for s in range(2)] for e in range(NE)]
        SH1 = [[sb("SH1_%d_%d" % (e, s), [64, 24, 1]) for s in range(2)] for e in range(NE)]
        SH2 = [[sb("SH2_%d_%d" % (e, s), [64, 2, 1]) for s in range(2)] for e in range(NE)]
        SH3 = [[sb("SH3_%d_%d" % (e, s), [128, 1, 1]) for s in range(2)] for e in range(NE)]
        EPW = dict(w2=sb("w2s", [64, 512]), a2=sb("a2s", [64, 512]), g2=sb("g2s", [128, 512]), au=sb("aus", [16, 256]))
        EP = []
        for e in range(NE):
            EP.append(dict(mu1=sb("mu1_%d" % e, [64, 24, 1]), mu2=sb("mu2_%d" % e, [64, 2, 1]), mu3=sb("mu3_%d" % e, [128, 1, 1]),
                           w0=sb("w0_%d" % e, [64, 8]), a0=sb("a0_%d" % e, [64, 8]), kk=sb("kk_%d" % e, [64, 8, 1]), ka=sb("ka_%d" % e, [64, 8, 1]),
                           rk=sb("rk_%d" % e, [64, 8, 1]), lw=sb("lw_%d" % e, [64, 8, 1]), lb=sb("lb_%d" % e, [64, 8, 1]),
                           ab=sb("ab_%d" % e, [64, 4]), nw=sb("nw_%d" % e, [128, 1])))
    if NO:
        QTe = sb("QTe", [128, 8, NT], BF16); QTo = sb("QTo", [128, 8, NT], BF16)
        KTf = ARN[:, 0:1024].rearrange("p (c n) -> p c n", c=8); VT = ARN[:, 1024:2048].rearrange("p (c n) -> p c n", c=8)
        KR = [sb("KR%d" % o, [128, 8, 5 * 128], BF16) for o in range(NO)]
        VR = [sb("VR%d" % o, [128, 5, 16, 65], BF16) for o in range(NO)]
        M3s = sb("M3s", [128, 16, 128], BF16); M4s = sb("M4s", [128, 16, 128], BF16)
        M3 = [M3s for o in range(NO)]
        M4 = [M4s for o in range(NO)]
        MSD = nc.dram_tensor("msd", [NO, 2, 128, 16, 128], BF16).ap()
        CH = [sb("CH_%d" % o, [128, 16]) for o in range(NO)]
        M0 = sb("M0", [128, 128], BF16); MLOW = sb("MLOW", [128, 128])
        PT = [sb("PT%d" % i, [128, 5, 128], BF16) for i in range(2)]
        OTOK = ARN[:, 2048:3072].rearrange("p (h d) -> p h d", h=16); RDEN = sb("RDEN", [128, 16, 1]); OTb = sb("OTb", [128, 8, NT], BF16)
        XIO3 = XIO[:, :].rearrange("p (h d) -> p h d", h=16)

    kb.ms("pool", idn[:], 0.0, ["const"])
    P.add("pool", lambda e: e.affine_select(out=idn[:], in_=idn[:], pattern=[[-1, 128]], compare_op=ALU.not_equal, fill=1.0, base=0, channel_multiplier=1), r=["const"], w=["const"])
    kb.ms("pool", ones[:], 1.0, ["const"])
    kb.ms("pool", onesb[:], 1.0, ["const"])
    kb.cp("dve", idnb[:], idn[:], ["const"], ["const"])
    kb.cp("dve", SELb[:], idn[0:64, 0:64].unsqueeze(2).to_broadcast([64, 64, 64]), ["const"], ["const"])
    for l in range(DEPTH):
        for j, nm in enumerate(["ln1_w", "ln1_b", "ln2_w", "ln2_b"]):
            kb.dma("pool", LNW[:, l, j, :], I[nm][l].rearrange("(c p) -> p c", p=128), [], ["const"], "setup", slow=True)
    for e in range(NE):
        ep = EP[e]
        sl = lambda ap, p: ap.rearrange("(c p) -> p c", p=p)
        kb.dma("pool", ep["mu1"][:, :, 0], sl(I["a_mu"][e, 0:1536], 64), [], ["const"], "setup", slow=True)
        kb.dma("pool", ep["mu2"][:, :, 0], sl(I["a_mu"][e, 1536:1664], 64), [], ["const"], "setup", slow=True)
        kb.dma("pool", ep["mu3"][:, :, 0], sl(I["a_mu"][e, 1664:1792], 128), [], ["const"], "setup", slow=True)
        kb.dma("pool", ep["w0"][:, :], sl(I["a_w0"][e], 64), [], ["const"], "setup", slow=True)
        kb.dma("pool", ep["a0"][:, :], sl(I["a_a0"][e], 64), [], ["const"], "setup", slow=True)
        kb.dma("pool", ep["kk"][:, :, 0], sl(I["a_k_k"][e], 64), [], ["const"], "setup", slow=True)
        kb.dma("pool", ep["ka"][:, :, 0], sl(I["a_k_a"][e], 64), [], ["const"], "setup", slow=True)
        kb.dma("pool", ep["rk"][:, :, 0], I["a_r_k"][e].rearrange("h p -> p h"), [], ["const"], "setup", slow=True)
        kb.dma("pool", ep["lw"][:, :, 0], sl(I["a_ln_w"][e], 64), [], ["const"], "setup", slow=True)
        kb.dma("pool", ep["lb"][:, :, 0], sl(I["a_ln_b"][e], 64), [], ["const"], "setup", slow=True)
        kb.dma("pool", ep["ab"][:, :], sl(I["b_alpha_bias"][e], 64), [], ["const"], "setup", slow=True)
        kb.dma("pool", ep["nw"][:, :], sl(I["b_norm_w"][e], 128), [], ["const"], "setup", slow=True)
        kb.act(ep["ab"][:, :], ep["ab"][:, :], AF.Copy, ["const"], ["const"], scale=-1.0)
        for s in range(2):
            for t_ in (HW[e][s], SG[e][s]):
                kb.ms("pool", t_[:], 0.0, [t_.name])
            for t_ in (SH1[e][s], SH2[e][s], SH3[e][s]):
                kb.ms("pool", t_[:], 0.0, [t_.name])
    def cast_w(name, l, rows):
        for r0 in range(0, rows, 128):
            r1 = min(rows, r0 + 128)
            kb.dma("pool", WB[name][l, r0:r1, :], I[name][l, r0:r1, :], [], [("wb", name + str(l))], "cast_" + name + str(l))
    for l in range(DEPTH if not os.environ.get('NOCAST') else 0):
        if l % 2 == 0:
            cast_w("w_in_mix", l // 2, D); cast_w("w_out_mix", l // 2, D)
        else:
            cast_w("c_w_qkv", l // 2, D); cast_w("c_w_o", l // 2, D)
        cast_w("ffn_w_in", l, D); cast_w("ffn_w_out", l, FFH)
    if NO:
        kb.ms("pool", MLOW[:], 1.0, ["const"])
        P.add("pool", lambda e: e.affine_select(out=MLOW[:], in_=MLOW[:], pattern=[[-1, 128]], compare_op=ALU.is_gt, fill=0.0, base=0, channel_multiplier=1), r=["const"], w=["const"])
        kb.ms("pool", M0[:], 0.0, ["const"])
        kb.ms("pool", M0[0:64, 64:128], NEG, ["const"])
        TS8 = XIO[:, :].rearrange("p (h i) -> p h i", h=8)
        for o in range(NO):
            kb.ms("dve", XIO[0:16, 0:512], 0.0, ["XIO"])
            kb.dma("pool", EXT[:, :], XIO[0:16, 0:512], ["XIO"], ["ext"], "setup")
            kb.dma("pool", EXT[:, 0:257], I["c_rel_bias"][o], ["ext"], ["ext"], "setup")
            kb.dma("pool", REP, EXT.unsqueeze(1).to_broadcast([16, 128, 512]), ["ext"], ["rep"], "setup")
            kb.dma("pool", CH[o][:, :], I["c_rel_bias"][o][:, 256].partition_broadcast(128), [], ["const"], "setup", slow=True)
            for (MM, off) in ((M3[o], 256), (M4[o], 128)):
                for hh in range(2):
                    src = bass.AP(REP_T, off + hh * 8 * 65536, [[511, 128], [65536, 8], [1, 128]])
                    kb.dma("pool", TS8, src, ["rep"], ["XIO"], "setup")
                    kb.tt("dve", TS8, TS8, CH[o][:, hh * 8:hh * 8 + 8].unsqueeze(2).to_broadcast([128, 8, 128]), ALU.subtract, ["XIO", "const"], ["XIO"])
                    if off == 256:
                        kb.tt("dve", MM[:, hh * 8:hh * 8 + 8, :], TS8, MLOW[:].unsqueeze(1).to_broadcast([128, 8, 128]), ALU.mult, ["XIO", "const"], ["MB"])
                    else:
                        kb.cp("dve", MM[:, hh * 8:hh * 8 + 8, :], TS8, ["XIO"], ["MB"])
                if off == 128:
                    kb.ms("dve", MM[64:128, :, 0:64], NEG, ["MB"])
                kb.dma("pool", MSD[o, 0 if off == 256 else 1], MM[:], ["MB"], [("msd", o)], "stm")
            kb.ms("pool", VR[o][:, :, :, 64:65], 1.0, ["VR%d" % o])
            kb.ms("pool", QTe[:], 0.0, ["QTe"])
            kb.ms("pool", QTo[:], 0.0, ["QTo"])

    DBG = os.environ.get('DBG')
    dbg_on = [False]

    def dbg(name, ap, key, shape, dt=F32):
        if not (DBG and dbg_on[0]):
            return
        d = nc.dram_tensor("dbg_" + name, list(shape), dt, kind="ExternalOutput").ap()
        kb.dma("pool", d, ap, [key], [], "dbg_" + name)
    ev_flip = [0]

    def evac_copy(out, in_, r, w):
        ev_flip[0] ^= 1
        if ev_flip[0]:
            kb.act(out, in_, AF.Copy, r, w)
        else:
            kb.cp("dve", out, in_, r, w)

    def layer_norm(l, which):
        dbg("z%d_%d" % (l, which), Z[:], "Z", [128, 8, NT])
        kb.act(ZB, Z[:], AF.Copy, ["Z"], ["Hh"])
        kb.tt("dve", ZQ, Z[:], Z[:], ALU.mult, ["Z"], ["Hh"])
        bt, bk = kb.bank()
        for kc in range(8):
            kb.mm(bt[:, 0:NT], onesb[:], ZB[:, kc, :], ["const", "Hh"], [bk], start=(kc == 0), stop=(kc == 7))
        for kc in range(8):
            kb.mm(bt[:, NT:2 * NT], onesb[:], ZQ[:, kc, :], ["const", "Hh"], [bk], start=(kc == 0), stop=(kc == 7))
        kb.act(MEAN[:], bt[:, 0:NT], AF.Copy, [bk], ["MEAN"], scale=1.0 / D)
        kb.tt("dve", MSQ[:], MEAN[:], MEAN[:], ALU.mult, ["MEAN"], ["MSQ"])
        kb.stt("dve", VAR[:], bt[:, NT:2 * NT], 1.0 / D, MSQ[:], ALU.mult, ALU.subtract, [bk, "MSQ"], ["VAR"])
        kb.act(VAR[:], VAR[:], AF.Sqrt, ["VAR"], ["VAR"], bias=LN_EPS)
        kb.rcp(RSTD[:], VAR[:], ["VAR"], ["RSTD"])
        kb.tt("dve", Z[:], Z[:], MEAN[:].unsqueeze(1).to_broadcast([128, 8, NT]), ALU.subtract, ["Z", "MEAN"], ["Z"])
        kb.tt("dve", Z[:], Z[:], RSTD[:].unsqueeze(1).to_broadcast([128, 8, NT]), ALU.mult, ["Z", "RSTD"], ["Z"])
        kb.tt("dve", Z[:], Z[:], LNW[:, l, 2 * which, :].unsqueeze(2).to_broadcast([128, 8, NT]), ALU.mult, ["Z", "const"], ["Z"])
        kb.tt("dve", X[:], Z[:], LNW[:, l, 2 * which + 1, :].unsqueeze(2).to_broadcast([128, 8, NT]), ALU.add, ["Z", "const"], ["X"])
        kb.act(Xb[:], X[:], AF.Copy, ["X"], ["Xb"])
        dbg("x%d_%d" % (l, which), X[:], "X", [128, 8, NT])

    def resid_evac(pi):
        def f(banks):
            per = 4 // len(banks)
            for bi, (bt, bk) in enumerate(banks):
                c0 = 4 * pi + bi * per
                kb.stt("dve", Z[:, c0:c0 + per, :], X[:, c0:c0 + per, :], ALPHA,
                       bt[:, 0:per * NT].rearrange("p (c n) -> p c n", c=per), ALU.mult, ALU.add, ["X", bk], ["Z"])
        return f

    G4 = [(0, 128, 1), (128, 128, 1), (256, 128, 1), (384, 128, 1)]

    def ffn(l):
        def ev(j, n):
            def f(banks):
                (g, gk), (u, uk) = banks
                kb.act(SGT[:, 0:n, :], g[:, 0:n * NT].rearrange("p (c n) -> p c n", c=n), AF.Silu, [gk], ["SGT"])
                kb.tt("dve", Hh[:, 2 * j:2 * j + n, :], SGT[:, 0:n, :], u[:, 0:n * NT].rearrange("p (c n) -> p c n", c=n), ALU.mult, ["SGT", uk], ["Hh"])
            return f
        passes = []
        for j in range(11):
            passes.append(dict(segs=[(j * 256, 256), (FFH + j * 256, 256)], kslabs=[(0, 128, 8)], groups=[(0, 128, 2), (256, 128, 2)], evac=ev(j, 2)))
        kb.dense("ffn_w_in%d" % l, WB["ffn_w_in"][l], passes, lambda si, kc: (Xb[:, kc, :], "Xb"), NT)
        passes = [dict(segs=[(p * 512, 512)], kslabs=[(0, 128, 8), (1024, 128, 8), (2048, 128, 6)], groups=G4, evac=resid_evac(p)) for p in range(2)]
        kb.dense("ffn_w_out%d" % l, WB["ffn_w_out"][l], passes, lambda si, kc: (Hh[:, si * 8 + kc, :], "Hh"), NT)
        layer_norm(l, 1)

    def even_layer(l, segs):
        e = l // 2
        ep = dict(EP[e])
        ep.update(EPW)
        P.add("sp", lambda en: [en.dma_start(out=EPW["w2"][:, :], in_=I["a_w2"][e]), en.dma_start(out=EPW["a2"][:, :], in_=I["a_a2"][e]),
                                en.dma_start(out=EPW["g2"][:, :], in_=I["a_g2"][e]), en.dma_start(out=EPW["au"][:, :], in_=I["b_alpha_up"][e])],
              r=[], w=["EPW"], dkey="ldep", ndma=4)
        hb = lambda ap: ap.to_broadcast([64, 8, NT])

        def gcopy(dst, key, i0, n, M):
            def f(banks, gi):
                bt, bk = banks[gi]
                evac_copy(dst[0:M, i0:i0 + n, :], bt[0:M, 0:n * NT].rearrange("p (c n) -> p c n", c=n), [bk], [key])
            return f

        def mk(segs_, groups, fs):
            def ev(banks):
                for gi, f in enumerate(fs):
                    f(banks, gi)
            return dict(segs=segs_, kslabs=[(0, 128, 8)], groups=groups, evac=ev)
        passes = []
        for s_ in range(3):
            passes.append(mk([(s_ * 512, 512)], [(0, 64, 4), (256, 64, 4)], [gcopy(RKV, "RKV", s_ * 8, 4, 64), gcopy(RKV, "RKV", s_ * 8 + 4, 4, 64)]))
        passes.append(mk([(1536, 512)], [(0, 64, 2), (128, 128, 1), (256, 64, 4)], [gcopy(XWA, "XWA", 0, 2, 64), gcopy(XG, "XG", 0, 1, 128), gcopy(BQK, "BQK", 0, 4, 64)]))
        passes.append(mk([(2048, 512)], [(0, 64, 4), (256, 128, 2)], [gcopy(BQK, "BQK", 4, 4, 64), gcopy(BV, "BV", 0, 2, 128)]))
        passes.append(mk([(2560, 272)], [(0, 128, 2), (256, 16, 1)], [gcopy(BV, "BV", 2, 2, 128), gcopy(BXA, "BXA", 0, 1, 16)]))
        passes.append(mk([(2832, 512)], [(0, 128, 4)], [gcopy(BRG, "BRG", 0, 4, 128)]))
        kb.dense("w_in_mix%d" % e, WB["w_in_mix"][e], passes, lambda si, kc: (Xb[:, kc, :], "Xb"), NT)
        for (T_, Dt, MU, SHL, Pn, C, tk, dk) in ((RKV, DRKV, ep["mu1"], SH1[e], 64, 24, "RKV", "DRKV"), (XWA, DXWA, ep["mu2"], SH2[e], 64, 2, "XWA", "DXWA"),
                                                (XG, DXG, ep["mu3"], SH3[e], 128, 1, "XG", "DXG")):
            for (slot, t0, t1) in segs:
                SH = SHL[slot]
                kb.tt("dve", Dt[:, :, t0 + 1:t1], T_[:, :, t0:t1 - 1], T_[:, :, t0 + 1:t1], ALU.subtract, [tk], [dk])
                kb.tt("dve", Dt[:, :, t0:t0 + 1], SH[:], T_[:, :, t0:t0 + 1], ALU.subtract, [tk, SH.name], [dk])
                kb.cp("dve", SH[:], T_[:, :, t1 - 1:t1], [tk], [SH.name])
            kb.tt("dve", Dt[:], Dt[:], MU[:].to_broadcast([Pn, C, NT]), ALU.mult, [dk, "const"], [dk])
            kb.tt("dve", T_[:], T_[:], Dt[:], ALU.add, [tk, dk], [tk])
        R_ = RKV[:, 0:8, :]; K_ = RKV[:, 8:16, :]; V_ = RKV[:, 16:24, :]
        kb.act(TH[:], XWA[:, 0, :], AF.Tanh, ["XWA"], ["TH"])
        kb.act(SGX[:], XG[:, 0, :], AF.Sigmoid, ["XG"], ["SGX"])
        for (wt, rhs, rk_, dst, dk_, bias, fn) in ((ep["w2"], TH[:], "TH", WD, "WD", ep["w0"], AF.Sigmoid), (ep["a2"], XWA[:, 1, :], "XWA", AA, "AA", ep["a0"], AF.Sigmoid),
                                                  (ep["g2"], SGX[:], "SGX", GG, "GG", None, AF.Copy)):
            for half in range(2):
                bt, bk = kb.bank()
                for j in range(4):
                    h = half * 4 + j
                    kb.mm(bt[0:64, j * NT:(j + 1) * NT], wt[:, h * 64:(h + 1) * 64], rhs, ["EPW", rk_], [bk])
                for j in range(4):
                    h = half * 4 + j
                    if bias is not None:
                        kb.act(dst[:, h, :], bt[0:64, j * NT:(j + 1) * NT], fn, [bk, "const"], [dk_], bias=bias[:, h:h + 1])
                    else:
                        kb.act(dst[:, h, :], bt[0:64, j * NT:(j + 1) * NT], fn, [bk], [dk_])
        kb.act(WD[:], WD[:], AF.Exp, ["WD"], ["WD"], scale=-ED)
        kb.tt("dve", KK[:], K_, hb(ep["kk"][:]), ALU.mult, ["RKV", "const"], ["KK"])
        kb.tt("dve", T1, KK[:], KK[:], ALU.mult, ["KK"], ["DRKV"])
        for half in range(2):
            bt, bk = kb.bank()
            kb.mm(bt[0:64, :], ones[0:64, 0:64], T1[:, half * 4:half * 4 + 4, :], ["const", "DRKV"], [bk])
            kb.act(T2[:, half * 4:half * 4 + 4, :], bt[0:64, :].rearrange("p (c n) -> p c n", c=4), AF.Sqrt, [bk], ["DRKV"], bias=1e-12)
        kb.rcp(T2, T2, ["DRKV"], ["DRKV"])
        kb.tt("dve", KK[:], KK[:], T2, ALU.mult, ["KK", "DRKV"], ["KK"])
        kb.stt("dve", NB[:], KK[:], -1.0, AA[:], ALU.mult, ALU.mult, ["KK", "AA"], ["NB"])
        kb.stt("dve", T1, AA[:], -1.0, hb(ep["ka"][:]), ALU.add, ALU.mult, ["AA", "const"], ["DRKV"])
        kb.stt("dve", K_, T1, 1.0, K_, ALU.add, ALU.mult, ["DRKV", "RKV"], ["RKV"])
        kb.tt("dve", T1, R_, K_, ALU.mult, ["RKV"], ["DRKV"])
        kb.tt("dve", T1, T1, hb(ep["rk"][:]), ALU.mult, ["DRKV", "const"], ["DRKV"])
        for half in range(2):
            bt, bk = kb.bank()
            kb.mm(bt[0:64, :], ones[0:64, 0:64], T1[:, half * 4:half * 4 + 4, :], ["const", "DRKV"], [bk])
            kb.tt("dve", BON[:, half * 4:half * 4 + 4, :], bt[0:64, :].rearrange("p (c n) -> p c n", c=4), V_[:, half * 4:half * 4 + 4, :], ALU.mult, [bk, "RKV"], ["DRKV"])
        bt, bk = kb.bank()
        for h in range(4):
            kb.mm(bt[0:64, h * NT:(h + 1) * NT], ep["au"][:, h * 64:(h + 1) * 64], BXA[:, 0, :], ["EPW", "BXA"], [bk])
        for h in range(4):
            kb.act(AL[:, h, :], bt[0:64, h * NT:(h + 1) * NT], AF.Exp, [bk, "const"], ["AL"], bias=ep["ab"][:, h:h + 1], scale=-1.0)
        kb.act(AL[:], AL[:], AF.Ln, ["AL"], ["AL"], bias=1.0)
        kb.act(AL[:], AL[:], AF.Exp, ["AL"], ["AL"], scale=-1.0 / 16.0)
        kb.act(BQK[:, 0:4, :], BQK[:, 0:4, :], AF.Copy, ["BQK"], ["BQK"], scale=0.125)
        P.add("dve", lambda en: en.memset(RT[0][:, 0:1], 0.0), r=[], w=["DRKV", "RT0", "RT1", "RT2", "RT3"])
        obr, obrk = fixed[0]
        obg, obgk = fixed[1]
        for (slot, t0, t1) in segs:
            Hs = HW[e][slot]; Ss = SG[e][slot]
            hk = Hs.name; sk = Ss.name
            H3 = Hs[:].rearrange("p (h v) -> p h v", h=8)
            S3 = Ss[:].rearrange("p (h v) -> p h v", h=4)
            for c0 in range(t0, t1, 64):
                bt, bk = kb.bank()
                for h in range(8):
                    kb.tr(bt[0:64, h * 64:(h + 1) * 64], V_[:, h, c0:c0 + 64], idn[0:64, 0:64], ["RKV"], [bk])
                kb.act(VTOKb[:], bt[0:64, :], AF.Copy, [bk], ["VTOKb"])
                bt, bk = kb.bank()
                for h in range(4):
                    kb.tr(bt[0:64, h * 128:(h + 1) * 128], BV[:, h, c0:c0 + 64], idn[:, :], ["BV"], [bk])
                kb.cp("dve", GVTOKb[:], bt[0:64, :], [bk], ["GVTOKb"])
                v8 = lambda ap: ap.rearrange("p (h v) -> p h v", h=8)
                v4g = lambda ap: ap.rearrange("p (h v) -> p h v", h=4)

                def matvecs(tl_):
                    t_ = c0 + tl_
                    for h in range(8):
                        kb.mm(obr[0:64, h * 64 + tl_:h * 64 + tl_ + 1], Hs[:, h * 64:(h + 1) * 64], R_[:, h, t_:t_ + 1], [hk, "RKV"], [obrk])

                def matvecs_g(tl_):
                    t_ = c0 + tl_
                    for h in range(4):
                        kb.mm(obg[:, h * 64 + tl_:h * 64 + tl_ + 1], Ss[:, h * 128:(h + 1) * 128], BQK[:, h, t_:t_ + 1], [sk, "BQK"], [obgk])

                for tl in range(64):
                    t = c0 + tl
                    col = lambda T_: T_[:, :, t:t + 1]
                    vb, vbk = kb.bank()
                    kb.mm(vb[0:64, :], SELb[:, tl, :], VTOKb[:], ["const", "VTOKb"], [vbk])
                    vg, vgk = kb.bank()
                    kb.mm(vg[0:64, :], SELb[:, tl, :], GVTOKb[:], ["const", "GVTOKb"], [vgk])
                    kb.tt("dve", v8(RT[3][:]), v8(vb[0:64, :]), col(K_).to_broadcast([64, 8, 64]), ALU.mult, [vbk, "RKV"], ["RT3"])
                    kb.tt("dve", v4g(GT[1][:]), v4g(vg[0:64, :]), BQK[:, 4:8, t:t + 1].to_broadcast([64, 4, 128]), ALU.mult, [vgk, "BQK"], ["GT1"])
                    kb.tt("pool", v8(RT[0][:]), H3, col(KK).to_broadcast([64, 8, 64]), ALU.mult, [hk, "KK"], ["RT0"])
                    sa, sak = kb.bank()
                    kb.mm(sa[0:64, :], ones[0:64, 0:64], RT[0][:], ["const", "RT0"], [sak])
                    kb.tt("pool", v8(RT[1][:]), H3, col(WD).to_broadcast([64, 8, 64]), ALU.mult, [hk, "WD"], ["RT1"])
                    kb.tt("pool", RT[1][:], RT[1][:], RT[3][:], ALU.add, ["RT1", "RT3"], ["RT1"])
                    if tl > 0:
                        matvecs(tl - 1)
                    kb.tt("dve", v8(RT[2][:]), v8(sa[0:64, :]), col(NB).to_broadcast([64, 8, 64]), ALU.mult, [sak, "NB"], ["RT2"])
                    kb.tt("dve", Hs[:], RT[1][:], RT[2][:], ALU.add, ["RT1", "RT2"], [hk])
                    kb.tt("pool", v4g(GT[0][:]), S3, col(AL).to_broadcast([64, 4, 128]), ALU.mult, [sk, "AL"], ["GT0"])
                    if tl > 0:
                        matvecs_g(tl - 1)
                    kb.tt("dve", Ss[:], GT[0][:], GT[1][:], ALU.add, ["GT0", "GT1"], [sk])
                matvecs(63)
                matvecs_g(63)
                kb.act(OT[:, :, c0:c0 + 64], obr[0:64, :].rearrange("p (h t) -> p h t", h=8), AF.Copy, [obrk], ["AA"])
                kb.cp("dve", OG[:, :, c0:c0 + 64], obg[:, 0:256].rearrange("p (h t) -> p h t", h=4), [obgk], ["OG"])
        P.add("dve", lambda en: en.tensor_tensor(out=T1, in0=OT[:], in1=OT[:], op=ALU.mult), r=["AA"], w=["DRKV", "RT0", "RT1", "RT2", "RT3"])
        for half in range(2):
            hs = slice(half * 4, half * 4 + 4)
            b1, b1k = kb.bank()
            kb.mm(b1[0:64, :], ones[0:64, 0:64], OT[:, hs, :], ["const", "AA"], [b1k])
            b2, b2k = kb.bank()
            kb.mm(b2[0:64, :], ones[0:64, 0:64], T1[:, hs, :], ["const", "DRKV"], [b2k])
            v4 = lambda b: b[0:64, :].rearrange("p (c n) -> p c n", c=4)
            kb.act(T2[:, hs, :], v4(b1), AF.Copy, [b1k], ["DRKV"], scale=1.0 / 64)
            kb.stt("dve", OT[:, hs, :], v4(b1), -1.0 / 64, OT[:, hs, :], ALU.mult, ALU.add, [b1k, "AA"], ["AA"])
            kb.tt("dve", T2[:, hs, :], T2[:, hs, :], T2[:, hs, :], ALU.mult, ["DRKV"], ["DRKV"])
            kb.stt("dve", T2[:, hs, :], v4(b2), 1.0 / 64, T2[:, hs, :], ALU.mult, ALU.subtract, [b2k, "DRKV"], ["DRKV"])
        kb.act(T2, T2, AF.Sqrt, ["DRKV"], ["DRKV"], bias=A_EPS)
        kb.rcp(T2, T2, ["DRKV"], ["DRKV"])
        kb.tt("dve", OT[:], OT[:], T2, ALU.mult, ["AA", "DRKV"], ["AA"])
        kb.tt("dve", OT[:], OT[:], hb(ep["lw"][:]), ALU.mult, ["AA", "const"], ["AA"])
        kb.tt("dve", OT[:], OT[:], hb(ep["lb"][:]), ALU.add, ["AA", "const"], ["AA"])
        kb.tt("dve", OT[:], OT[:], BON, ALU.add, ["AA", "DRKV"], ["AA"])
        kb.tt("dve", YAb[:], OT[:], GG[:], ALU.mult, ["AA", "GG"], ["YAb"])
        kb.tt("dve", T3[:], OG[:], OG[:], ALU.mult, ["OG"], ["BV"])
        bt, bk = kb.bank()
        kb.mm(bt[:, :], ones[:, :], T3[:], ["const", "BV"], [bk])
        kb.act(T3[:], bt[:, :].rearrange("p (c n) -> p c n", c=4), AF.Sqrt, [bk], ["BV"], bias=LN_EPS, scale=1.0 / 128)
        kb.rcp(T3[:], T3[:], ["BV"], ["BV"])
        kb.tt("dve", OG[:], OG[:], T3[:], ALU.mult, ["OG", "BV"], ["OG"])
        kb.act(T3[:], BRG[:], AF.Silu, ["BRG"], ["BV"])
        kb.stt("dve", YBb[:], OG[:], ep["nw"][:, 0:1], T3[:], ALU.mult, ALU.mult, ["OG", "const", "BV"], ["YBb"])
        dbg("ya%d" % l, YAb[:], "YAb", [64, 8, NT], BF16)
        dbg("yb%d" % l, YBb[:], "YBb", [128, 4, NT], BF16)
        dbg("gg%d" % l, GG[:], "GG", [64, 8, NT])
        dbg("ot%d" % l, OT[:], "AA", [64, 8, NT])
        passes = [dict(segs=[(p * 512, 512)], kslabs=[(0, 64, 8), (512, 128, 4)], groups=G4, evac=resid_evac(p)) for p in range(2)]
        kb.dense("w_out_mix%d" % e, WB["w_out_mix"][e], passes, lambda si, kc: ((YAb[:, kc, :], "YAb") if si == 0 else (YBb[:, kc, :], "YBb")), NT)
        layer_norm(l, 0)

    def odd_layer(l, tile):
        o = l // 2
        segs = tile["segs"]
        krk = "KR%d" % o; vrk = "VR%d" % o
        if NO > 1:
            P.add("sp", lambda en: [en.dma_start(out=M3s[:], in_=MSD[o, 0]), en.dma_start(out=M4s[:], in_=MSD[o, 1])],
                  r=[("msd", o)], w=["MB"], dkey="ldm", ndma=2)
        is_p = tile["kind"] == "p"
        g = tile.get("g", 0)
        kslot = (g % 5) if is_p else 4

        def ev_q(ci):
            def f(banks, gi):
                bt, bk = banks[gi]
                v = bt[:, 0:4 * NT].rearrange("p (c n) -> p c n", c=4)
                kb.act(QTe[0:64, ci:ci + 4, :], v[0:64], AF.Copy, [bk], ["QTe"], scale=0.125)
                kb.act(QTo[64:128, ci:ci + 4, :], v[64:128], AF.Copy, [bk], ["QTo"], scale=0.125)
            return f

        def ev_k(ci):
            def f(banks, gi):
                bt, bk = banks[gi]
                v = bt[:, 0:4 * NT].rearrange("p (c n) -> p c n", c=4)
                kb.cp("dve", KTf[:, ci:ci + 4, :], v, [bk], ["KTf"])
            return f

        def ev_v(ci):
            def f(banks, gi):
                bt, bk = banks[gi]
                v = bt[:, 0:4 * NT].rearrange("p (c n) -> p c n", c=4)
                evac_copy(VT[:, ci:ci + 4, :], v, [bk], ["VT"])
            return f

        passes = []
        for j, evf in enumerate([ev_q(0), ev_q(4), ev_k(0), ev_k(4), ev_v(0), ev_v(4)]):
            passes.append(dict(segs=[(j * 512, 512)], kslabs=[(0, 128, 8)], groups=[(0, 128, 4)], evac=(lambda banks, evf=evf: evf(banks, 0))))
        kb.dense("c_w_qkv%d" % o, WB["c_w_qkv"][o], passes, lambda si, kc: (Xb[:, kc, :], "Xb"), NT)

        def tok_major(src, skey, t0, n, dst3, dkey_):
            for half in range(2):
                bt, bk = kb.bank()
                for j in range(4):
                    c = half * 4 + j
                    kb.tr(bt[0:n, j * 128:(j + 1) * 128], src[:, c, t0:t0 + n], idn[:, :], [skey], [bk])
                evac_copy(dst3[0:n, half * 8:half * 8 + 8, :], bt[0:n, :].rearrange("p (h d) -> p h d", h=8), [bk], [dkey_])

        for (slot, t0, t1) in segs:
            n = t1 - t0
            if is_p:
                kb.act(KR[o][:, :, kslot * 128:kslot * 128 + n], KTf[:, :, t0:t1], AF.Copy, ["KTf"], [krk])
            else:
                sq = tile["seqs"][slot]
                for kbk in range(4):
                    kb.dma("sp", XIO3, I["cache_c_k"][o, sq, :, kbk * 128:(kbk + 1) * 128, :].rearrange("h t d -> t h d"), [], ["XIO"], "ldx")
                    for half in range(2):
                        bt, bk = kb.bank()
                        for j in range(4):
                            c = half * 4 + j
                            kb.tr(bt[:, j * 128:(j + 1) * 128], XIO[:, c * 128:(c + 1) * 128], idn[:, :], ["XIO"], [bk])
                        evac_copy(KR[o][:, half * 4:half * 4 + 4, kbk * 128:(kbk + 1) * 128], bt[:, :].rearrange("p (c n) -> p c n", c=4), [bk], [krk])
                    kb.dma("pool", VR[o][:, kbk, :, 0:64], I["cache_c_v"][o, sq, :, kbk * 128:(kbk + 1) * 128, :].rearrange("h t d -> t h d"), [], [vrk], "ldv")
                kb.act(KR[o][:, :, 4 * 128:4 * 128 + n], KTf[:, :, t0:t1], AF.Copy, ["KTf"], [krk])
            tok_major(VT, "VT", t0, n, XIO3, "XIO")
            kb.cp("dve", VR[o][0:n, kslot, :, 0:64], XIO3[0:n], ["XIO"], [vrk])
            emit = (not is_p) or (g >= NG - KEEP // 128)
            if emit:
                if is_p:
                    r0 = (g - (NG - KEEP // 128)) * 128
                    dv = O["p_v"][o, :, r0:r0 + 128, :]; dk_ = O["p_k"][o, :, r0:r0 + 128, :]
                else:
                    dv = O["s_v"][o, slot, :, :, :]; dk_ = O["s_k"][o, slot, :, :, :]
                kb.dma("pool", dv.rearrange("h t d -> t h d"), XIO3[0:n], ["XIO"], [], "stx")
                tok_major(KTf, "KTf", t0, n, XIO3, "XIO")
                kb.dma("pool", dk_.rearrange("h t d -> t h d"), XIO3[0:n], ["XIO"], [], "stx")
            if is_p:
                blocks = []
                for b in range(5):
                    kbi = g - 4 + b
                    if kbi < 0:
                        continue
                    bias = {0: "M0", 3: "M3", 4: "M4"}.get(b)
                    blocks.append((kbi % 5, 128, bias))
            else:
                blocks = [(0, 128, None), (1, 128, None), (2, 128, None), (3, 128, "M3"), (4, n, "M4")]
            nb_ = len(blocks)
            obanks = fixed
            for h in range(16 if not os.environ.get('SKIP_ATT') else 0):
                c = h // 2
                QTs = QTe if h % 2 == 0 else QTo
                qk = "QTe" if h % 2 == 0 else "QTo"
                pt = PT[h % 2]; ptk = "PT%d" % (h % 2)
                b1 = kb.bank(); b2 = kb.bank()
                for bi, (ks, nk, bias) in enumerate(blocks):
                    bt, bk = (b1 if bi < 4 else b2)
                    oc = (bi % 4) * 128
                    kb.mm(bt[0:nk, oc:oc + n], KR[o][:, c, ks * 128:ks * 128 + nk], QTs[:, c, t0:t1], [krk, qk], [bk], start=True, stop=(bias is None))
                    if bias == "M0":
                        kb.mm(bt[0:nk, oc:oc + n], idnb[0:nk, 0:nk], M0[0:nk, 0:n], ["const"], [bk], start=False, stop=True)
                    elif bias == "M3":
                        kb.mm(bt[0:nk, oc:oc + n], idnb[0:nk, 0:nk], M3[o][0:nk, h, 0:n], ["const", "MB"], [bk], start=False, stop=True)
                    elif bias == "M4":
                        kb.mm(bt[0:nk, oc:oc + n], idnb[0:nk, 0:nk], M4[o][0:nk, h, 0:n], ["const", "MB"], [bk], start=False, stop=True)
                for bi, (ks, nk, bias) in enumerate(blocks):
                    bt, bk = (b1 if bi < 4 else b2)
                    oc = (bi % 4) * 128
                    kb.act(pt[0:nk, bi, 0:n], bt[0:nk, oc:oc + n], AF.Exp, [bk, "const"], [ptk], bias=CH[o][0:nk, h:h + 1])
                ob, obk = obanks[h // 7]
                hc = (h % 7) * 65
                for bi, (ks, nk, bias) in enumerate(blocks):
                    kb.mm(ob[0:n, hc:hc + 65], pt[0:nk, bi, 0:n], VR[o][0:nk, ks, h, :], [ptk, vrk], [obk], start=(bi == 0), stop=(bi == nb_ - 1))
            for bi_, (h0, nh) in enumerate(((0, 7), (7, 7), (14, 2))):
                ob, obk = obanks[bi_]
                v = ob[0:n, 0:nh * 65].rearrange("p (h d) -> p h d", h=nh)
                kb.rcp(RDEN[0:n, h0:h0 + nh, :], v[:, :, 64:65], [obk], ["RDEN"])
                kb.tt("dve", OTOK[0:n, h0:h0 + nh, :], v[:, :, 0:64], RDEN[0:n, h0:h0 + nh, :].to_broadcast([n, nh, 64]), ALU.mult, [obk, "RDEN"], ["OTOK"])
            OT2 = OTOK[:, :, :].rearrange("p h d -> p (h d)")
            for half in range(2):
                bt, bk = kb.bank()
                for j in range(4):
                    c = half * 4 + j
                    kb.tr(bt[:, j * 128:j * 128 + n], OT2[0:n, c * 128:(c + 1) * 128], idn[0:n, 0:n], ["OTOK"], [bk])
                evac_copy(OTb[:, half * 4:half * 4 + 4, t0:t1], bt[:, :].rearrange("p (c n) -> p c n", c=4)[:, :, 0:n], [bk], ["OTb"])
        passes = [dict(segs=[(p * 512, 512)], kslabs=[(0, 128, 8)], groups=[(0, 128, 4)], evac=resid_evac(p)) for p in range(2)]
        kb.dense("c_w_o%d" % o, WB["c_w_o"][o], passes, lambda si, kc: (OTb[:, kc, :], "OTb"), NT)
        layer_norm(l, 0)

    def run_tile(tile):
        xsrc = tile["x"]
        IOSTEP = int(os.environ.get('IOSTEP', '9'))
        kb.dma("sp", XIO[:, :], xsrc, [], ["XIO"], "ldx")
        for half in range(2 if IOSTEP >= 2 else 0):
            bt, bk = kb.bank()
            for j in range(4):
                c = half * 4 + j
                kb.tr(bt[:, j * 128:(j + 1) * 128], XIO[:, c * 128:(c + 1) * 128], idn[:, :], ["XIO"], [bk])
            v = bt[:, :].rearrange("p (c n) -> p c n", c=4)
            if IOSTEP >= 3:
                kb.act(X[:, half * 4:half * 4 + 4, :], v, AF.Copy, [bk], ["X"])
            if IOSTEP >= 4:
                kb.cp("dve", Xb[:, half * 4:half * 4 + 4, :], v, [bk, "X"], ["Xb"])
        MAXL = int(os.environ.get('MAXL', '99'))
        NOFFN = os.environ.get('NOFFN')
        NOMIX = os.environ.get('NOMIX')
        for l in range(min(DEPTH, MAXL)):
            if NOMIX:
                pass
            elif l % 2 == 0:
                even_layer(l, tile["segs"])
            else:
                odd_layer(l, tile)
            if not NOFFN:
                ffn(l)
        for half in range(2 if IOSTEP >= 5 else 0):
            bt, bk = kb.bank()
            for j in range(4):
                c = half * 4 + j
                kb.tr(bt[:, j * 128:(j + 1) * 128], X[:, c, :], idn[:, :], ["X"], [bk])
            if IOSTEP >= 6:
                evac_copy(XIO[:, half * 512:(half + 1) * 512], bt[:, :], [bk], ["XIO"])
        kb.dma("pool", tile["y"], XIO[:, :], ["XIO"], [], "stx")

    def state_out(e, slot, dw, dsh, dg):
        bt, bk = kb.bank()
        for h in range(8):
            kb.tr(bt[0:64, h * 64:(h + 1) * 64], HW[e][slot][:, h * 64:(h + 1) * 64], idn[0:64, 0:64], [HW[e][slot].name], [bk])
        kb.act(STG, bt[0:64, :], AF.Copy, [bk], ["XIO"])
        kb.dma("pool", dw.rearrange("h v k -> v h k"), STG.rearrange("p (h k) -> p h k", h=8), ["XIO"], [], "stx")
        kb.dma("pool", dg.rearrange("h d v -> d h v"), SG[e][slot][:].rearrange("p (h v) -> p h v", h=4), [SG[e][slot].name], [], "sts_g%d" % e)
        kb.dma("pool", dsh[0:1536].rearrange("(c p) -> p c", p=64), SH1[e][slot][:, :, 0], [SH1[e][slot].name], [], "sts_1%d" % e, slow=True)
        kb.dma("pool", dsh[1536:1664].rearrange("(c p) -> p c", p=64), SH2[e][slot][:, :, 0], [SH2[e][slot].name], [], "sts_2%d" % e, slow=True)
        kb.dma("pool", dsh[1664:1792].rearrange("(c p) -> p c", p=128), SH3[e][slot][:, :, 0], [SH3[e][slot].name], [], "sts_3%d" % e, slow=True)

    STOP = os.environ.get('STOP', '')
    for g in range(NG if STOP != 'setup' else 0):
        dbg_on[0] = (g == 0)
        run_tile(dict(kind="p", g=g, segs=[(0, 0, 128)], x=I["x_prompt"][g * 128:(g + 1) * 128, :], y=O["y_p"][g * 128:(g + 1) * 128, :]))
    for e in range(NE if STOP not in ('setup', 'p2') else 0):
        state_out(e, 0, O["p_wkv"][e], O["p_shift"][e], O["p_gla"][e])
    dbg_on[0] = False
    for e in range(NE if STOP == '' else 0):
        for slot in range(2):
            kb.dma("sp", STG.rearrange("p (h k) -> p h k", h=8), I["state_a_wkv"][e, slot].rearrange("h v k -> v h k"), [], ["XIO"], "ldx")
            bt, bk = kb.bank()
            for h in range(8):
                kb.tr(bt[0:64, h * 64:(h + 1) * 64], STG[:, h * 64:(h + 1) * 64], idn[0:64, 0:64], ["XIO"], [bk])
            kb.act(HW[e][slot][:], bt[0:64, :], AF.Copy, [bk], [HW[e][slot].name])
            kb.dma("sp", SG[e][slot][:].rearrange("p (h v) -> p h v", h=4), I["state_b_gla"][e, slot].rearrange("h d v -> d h v"), [], [SG[e][slot].name], "ldsg")
            sh = I["state_a_shift"][e, slot]
            kb.dma("pool", SH1[e][slot][:, :, 0], sh[0:1536].rearrange("(c p) -> p c", p=64), [], [SH1[e][slot].name], "ldsh1", slow=True)
            kb.dma("pool", SH2[e][slot][:, :, 0], sh[1536:1664].rearrange("(c p) -> p c", p=64), [], [SH2[e][slot].name], "ldsh2", slow=True)
            kb.dma("pool", SH3[e][slot][:, :, 0], sh[1664:1792].rearrange("(c p) -> p c", p=128), [], [SH3[e][slot].name], "ldsh3", slow=True)
    if STOP == '':
      run_tile(dict(kind="s", segs=[(0, 0, 64), (1, 64, 128)], seqs=[0, 1], x=I["x_sample"][:, :], y=O["y_s"][:, :]))
    for e in range(NE if STOP == '' else 0):
        for slot in range(2):
            state_out(e, slot, O["s_wkv"][e, slot], O["s_shift"][e, slot], O["s_gla"][e, slot])
    P.emit(nc.Block())
    for cm in reversed(kb.stack):
        cm.__exit__(None, None, None)
    return nc


_OUT_ORDER = ["y_p", "y_s", "p_wkv", "p_shift", "p_gla", "p_k", "p_v", "s_wkv", "s_shift", "s_gla", "s_k", "s_v"]


def run(inputs, SEQ=8192, DEPTH=4, ncores=8):
    nc = build(SEQ, DEPTH)
    f = lambda a: np.ascontiguousarray(np.asarray(a, dtype=np.float32))
    in_maps = []
    for c in range(ncores):
        m = {"x_prompt": f(inputs["x_prompt"][c]), "x_sample": f(inputs["x_sample"][2 * c:2 * c + 2]).reshape(128, D)}
        for k_ in ("state_a_wkv", "state_a_shift", "state_b_gla", "cache_c_k", "cache_c_v"):
            m[k_] = f(np.asarray(inputs[k_])[:, 2 * c:2 * c + 2])
        for k_ in inputs:
            if k_ not in m:
                m[k_] = f(inputs[k_])
        in_maps.append(m)
    res = run_bass_kernel_spmd(nc, in_maps, core_ids=list(range(ncores)))
    R = res.results
    cat = lambda name, ax: np.stack([np.asarray(r[name]) for r in R], axis=ax)
    y_p = cat("y_p", 0)
    y_s = np.concatenate([np.asarray(r["y_s"]).reshape(2, 64, D) for r in R], axis=0)
    p_wkv = cat("p_wkv", 1); p_shift = cat("p_shift", 1); p_gla = cat("p_gla", 1)
    p_k = cat("p_k", 1); p_v = cat("p_v", 1)
    c2 = lambda name: np.concatenate([np.asarray(r[name]) for r in R], axis=1)
    return (y_p, y_s, p_wkv, p_shift, p_gla, p_k, p_v, c2("s_wkv"), c2("s_shift"), c2("s_gla"), c2("s_k"), c2("s_v"))


def kernel(**inputs):
    outs = run(inputs)
    return tuple(np.ascontiguousarray(o, dtype=np.float32) for o in outs)
```

```python
import os
import numpy as np
import concourse.bass as bass
import concourse.mybir as mybir
from concourse.bass_utils import run_bass_kernel_spmd

F32 = mybir.dt.float32
BF16 = mybir.dt.bfloat16
AF = mybir.ActivationFunctionType
ALU = mybir.AluOpType
AX = mybir.AxisListType


class Op:
    __slots__ = ("eng", "fn", "deps", "signal", "dkey", "ndma", "sem", "semname", "val", "idx")


class Prog:
    ENGS = ("pe", "act", "dve", "pool", "sp")
    EPOCH = int(os.environ.get('EPOCH', '8000'))
    DEPOCH = int(os.environ.get('DEPOCH', '500'))

    def __init__(self, nc):
        self.nc = nc
        self.ops = {e: [] for e in self.ENGS}
        self.lastw = {}
        self.readers = {}
        self.dma_last = {}

    @staticmethod
    def _cls(op):
        return op.dkey if op.dkey is not None else op.eng

    def add(self, eng, fn, r=(), w=(), dkey=None, ndma=1):
        op = Op()
        op.eng = eng
        op.fn = fn
        op.signal = dkey is not None
        op.dkey = dkey
        op.ndma = ndma
        op.sem = None
        op.val = 0
        op.idx = len(self.ops[eng])
        deps = {}

        def push(d):
            if d is None or d is op:
                return
            if d.dkey is None and dkey is None and d.eng == "pe" and eng == "pe":
                return
            c = self._cls(d)
            o = deps.get(c)
            if o is None or o.idx < d.idx:
                deps[c] = d

        for k in r:
            for d in self.lastw.get(k, {}).values():
                push(d)
            if isinstance(k, str) and k.startswith("pb"):
                for d in self.readers.get(k, {}).values():
                    if d.eng != eng:
                        push(d)
        for k in w:
            for d in self.lastw.get(k, {}).values():
                push(d)
            for d in self.readers.get(k, {}).values():
                push(d)
        op.deps = list(deps.values())
        for d in op.deps:
            d.signal = True
        me = self._cls(op)
        for k in r:
            self.readers.setdefault(k, {})[me] = op
        for k in w:
            self.lastw.setdefault(k, {})[me] = op
            self.readers[k] = {}
        self.ops[eng].append(op)
        if dkey is not None:
            self.dma_last[dkey] = op
        return op

    def emit(self, block_cm):
        nc = self.nc
        sems = {}
        stack = []

        def getsem(name):
            s = sems.get(name)
            if s is None:
                cm = nc.semaphore(name)
                s = cm.__enter__()
                stack.append(cm)
                sems[name] = s
            return s

        ecount = {e: 0 for e in self.ENGS}
        dcount = {}
        for e in self.ENGS:
            for op in self.ops[e]:
                if op.dkey is not None:
                    n = dcount.get(op.dkey, 0)
                    ep = n // self.DEPOCH
                    base = ep * self.DEPOCH
                    if (n - base) + op.ndma > self.DEPOCH:
                        n = base + self.DEPOCH
                        ep += 1
                        base = ep * self.DEPOCH
                    n += op.ndma
                    dcount[op.dkey] = n
                    op.semname = "d_%s_%d" % (op.dkey, ep)
                    op.sem = getsem(op.semname)
                    op.val = 16 * (n - base)
                elif op.signal:
                    n = ecount[e]
                    ep = n // self.EPOCH
                    ecount[e] = n + 1
                    op.semname = "e_%s_%d" % (e, ep)
                    op.sem = getsem(op.semname)
                    op.val = n - ep * self.EPOCH + 1
        finals = list(self.dma_last.values())

        with block_cm as block:
            def run(engname, eng):
                seen = {}

                def waits(deps):
                    for d in deps:
                        key = d.semname
                        if seen.get(key, 0) < d.val:
                            eng.wait_ge(d.sem, d.val)
                            seen[key] = d.val

                for op in self.ops[engname]:
                    waits(op.deps)
                    res = op.fn(eng)
                    if op.dkey is not None:
                        if not isinstance(res, (list, tuple)):
                            res = [res]
                        assert len(res) == op.ndma, (len(res), op.ndma)
                        for ins in res:
                            ins.then_inc(op.sem, 16)
                    elif op.signal:
                        res.then_inc(op.sem, 1)
                if engname == "sp":
                    waits(finals)

            @block.tensor
            def _(eng):
                run("pe", eng)

            @block.scalar
            def _(eng):
                run("act", eng)

            @block.vector
            def _(eng):
                run("dve", eng)

            @block.gpsimd
            def _(eng):
                run("pool", eng)

            @block.sync
            def _(eng):
                run("sp", eng)
        for cm in reversed(stack):
            cm.__exit__(None, None, None)


D = 1024
TT = 128
FFH = 2816
ALPHA = (2.0 * 4) ** 0.25
LN_EPS = 1e-5
A_EPS = 64e-5
NEG = -30000.0
ED = float(np.exp(-0.5))


class KB:
    def __init__(self, SEQ, DEPTH):
        self.SEQ = SEQ
        self.DEPTH = DEPTH
        self.NE = (DEPTH + 1) // 2
        self.NO = DEPTH // 2
        self.NG = SEQ // TT
        self.KEEP = min(512, SEQ)
        self.nc = bass.Bass("TRN2", target_bir_lowering=False)
        self.P = Prog(self.nc)
        self.stack = []
        self.tiles = {}
        self.uid = 0

    def sb(self, name, shape, dt=F32):
        cm = self.nc.sbuf_tensor(name, list(shape), dt)
        t = cm.__enter__()
        self.stack.append(cm)
        return t

    def din(self, name, shape):
        return self.nc.dram_tensor(name, list(shape), F32, kind="ExternalInput").ap()

    def dout(self, name, shape):
        return self.nc.dram_tensor(name, list(shape), F32, kind="ExternalOutput").ap()

    def bank(self):
        b = self.rot[self.rot_i % len(self.rot)]
        self.rot_i += 1
        return b

    def mm(self, out, lhsT, rhs, r, w, start=True, stop=True):
        self.P.add("pe", lambda e: e.matmul(out, lhsT, rhs, start=start, stop=stop), r=r, w=w)

    def tr(self, out, in_, idn, r, w):
        self.P.add("pe", lambda e: e.transpose(out, in_, idn), r=list(r) + ["const"], w=w)

    def act(self, out, in_, func, r, w, bias=None, scale=None):
        kw = {}
        if bias is not None:
            kw["bias"] = bias
        if scale is not None:
            kw["scale"] = scale
        self.P.add("act", lambda e: e.activation(out=out, in_=in_, func=func, **kw), r=r, w=w)

    def tt(self, eng, out, in0, in1, op, r, w):
        self.P.add(eng, lambda e: e.tensor_tensor(out=out, in0=in0, in1=in1, op=op), r=r, w=w)

    def stt(self, eng, out, in0, scalar, in1, op0, op1, r, w):
        self.P.add(eng, lambda e: e.scalar_tensor_tensor(out=out, in0=in0, scalar=scalar, in1=in1, op0=op0, op1=op1), r=r, w=w)

    def ts(self, eng, out, in0, s1, s2, op0, op1, r, w):
        self.P.add(eng, lambda e: e.tensor_scalar(out=out, in0=in0, scalar1=s1, scalar2=s2, op0=op0, op1=op1), r=r, w=w)

    def cp(self, eng, out, in_, r, w):
        self.P.add(eng, lambda e: e.tensor_copy(out=out, in_=in_), r=r, w=w)

    def ms(self, eng, ap, val, w):
        self.P.add(eng, lambda e: e.memset(ap, val), w=w)

    def rcp(self, out, in_, r, w):
        self.P.add("dve", lambda e: e.reciprocal(out=out, in_=in_), r=r, w=w)

    def dma(self, q, out, in_, r, w, dkey, slow=False):
        if slow:
            self.P.add(q, lambda e: e.dma_start(out=out, in_=in_, allow_slow_non_contiguous=True), r=r, w=w, dkey=dkey)
        else:
            self.P.add(q, lambda e: e.dma_start(out=out, in_=in_), r=r, w=w, dkey=dkey)

    def dense(self, wname, Wb, passes, rhs_fn, NT):
        for ps in passes:
            segs = ps["segs"]
            tot = sum(n for _, n in segs)
            groups = ps["groups"]
            banks = [self.bank() for _ in groups]
            nks = len(ps["kslabs"])
            for si, (row0, p, nk) in enumerate(ps["kslabs"]):
                slot = self.slab_i % len(self.slabs)
                self.slab_i += 1
                st = self.slabs[slot]
                skey = "slab%d" % slot
                view = st[0:p, 0:nk * tot].rearrange("p (k n) -> p k n", k=nk)
                off = 0
                outs = []
                for (c0, n) in segs:
                    outs.append((view[:, :, off:off + n], Wb[row0:row0 + nk * p, c0:c0 + n].rearrange("(k p) n -> p k n", p=p)))
                    off += n
                self.P.add("sp", lambda e, outs=outs: [e.dma_start(out=o, in_=i) for o, i in outs],
                           r=[("wb", wname)], w=[skey], dkey=skey, ndma=len(outs))
                for gi, (col, M, n) in enumerate(groups):
                    bt, bk = banks[gi]
                    for j in range(n):
                        for kc in range(nk):
                            rap, rkey = rhs_fn(si, kc)
                            self.mm(bt[0:M, j * NT:(j + 1) * NT], view[:, kc, col + j * M: col + (j + 1) * M], rap,
                                    r=[skey, rkey], w=[bk], start=(si == 0 and kc == 0), stop=(si == nks - 1 and kc == nk - 1))
            ps["evac"](banks)


def build(SEQ=8192, DEPTH=4):
    kb = KB(SEQ, DEPTH)
    nc, P = kb.nc, kb.P
    NE, NO, NG, KEEP = kb.NE, kb.NO, kb.NG, kb.KEEP
    NT = TT
    I = {}
    I["x_prompt"] = kb.din("x_prompt", [SEQ, D])
    I["x_sample"] = kb.din("x_sample", [128, D])
    I["state_a_wkv"] = kb.din("state_a_wkv", [NE, 2, 8, 64, 64])
    I["state_a_shift"] = kb.din("state_a_shift", [NE, 2, 1792])
    I["state_b_gla"] = kb.din("state_b_gla", [NE, 2, 4, 64, 128])
    I["cache_c_k"] = kb.din("cache_c_k", [max(NO, 1), 2, 16, 512, 64])
    I["cache_c_v"] = kb.din("cache_c_v", [max(NO, 1), 2, 16, 512, 64])
    wshapes = dict(w_in_mix=[NE, D, 3344], w_out_mix=[NE, D, D], c_w_qkv=[max(NO, 1), D, 3 * D], c_w_o=[max(NO, 1), D, D],
                   ffn_w_in=[DEPTH, D, 2 * FFH], ffn_w_out=[DEPTH, FFH, D])
    for k_, s_ in wshapes.items():
        I[k_] = kb.din(k_, s_)
    small = dict(a_mu=[NE, 1792], a_w0=[NE, 512], a_w2=[NE, 64, 512], a_a0=[NE, 512], a_a2=[NE, 64, 512], a_g2=[NE, 128, 512],
                 a_k_k=[NE, 512], a_k_a=[NE, 512], a_r_k=[NE, 8, 64], a_ln_w=[NE, 512], a_ln_b=[NE, 512],
                 b_alpha_up=[NE, 16, 256], b_alpha_bias=[NE, 256], b_norm_w=[NE, 128], c_rel_bias=[max(NO, 1), 16, 257],
                 ln1_w=[DEPTH, D], ln1_b=[DEPTH, D], ln2_w=[DEPTH, D], ln2_b=[DEPTH, D])
    for k_, s_ in small.items():
        I[k_] = kb.din(k_, s_)
    O = {}
    O["y_p"] = kb.dout("y_p", [SEQ, D])
    O["y_s"] = kb.dout("y_s", [128, D])
    O["p_wkv"] = kb.dout("p_wkv", [NE, 8, 64, 64])
    O["p_shift"] = kb.dout("p_shift", [NE, 1792])
    O["p_gla"] = kb.dout("p_gla", [NE, 4, 64, 128])
    O["p_k"] = kb.dout("p_k", [max(NO, 1), 16, KEEP, 64])
    O["p_v"] = kb.dout("p_v", [max(NO, 1), 16, KEEP, 64])
    O["s_wkv"] = kb.dout("s_wkv", [NE, 2, 8, 64, 64])
    O["s_shift"] = kb.dout("s_shift", [NE, 2, 1792])
    O["s_gla"] = kb.dout("s_gla", [NE, 2, 4, 64, 128])
    O["s_k"] = kb.dout("s_k", [max(NO, 1), 2, 16, 64, 64])
    O["s_v"] = kb.dout("s_v", [max(NO, 1), 2, 16, 64, 64])
    WB = {k_: nc.dram_tensor(k_ + "_bf", list(s_), BF16).ap() for k_, s_ in wshapes.items()}
    EXT = nc.dram_tensor("relext", [16, 512], F32).ap()
    REP_T = nc.dram_tensor("relrep", [16, 128, 512], F32)
    REP = REP_T.ap()

    pbanks = []
    for i in range(8):
        cm = nc.psum_tensor("pb%d" % i, [128, 512], F32)
        t = cm.__enter__()
        kb.stack.append(cm)
        pbanks.append((t, "pb%d" % i))
    fixed = pbanks[0:3]
    kb.rot = pbanks[3:8]
    kb.rot_i = 0
    sb = kb.sb
    kb.slabs = [sb("slab%d" % i, [128, 4096], BF16) for i in range(2)]
    kb.slab_i = 0
    idn = sb("idn", [128, 128]); idnb = sb("idnb", [128, 128], BF16)
    ones = sb("ones", [128, 128]); onesb = sb("onesb", [128, 128], BF16)
    SELb = sb("SELb", [64, 64, 64], BF16)
    X = sb("X", [128, 8, NT]); Xb = sb("Xb", [128, 8, NT], BF16); Z = sb("Z", [128, 8, NT])
    XIO = sb("XIO", [128, D])
    MEAN = sb("MEAN", [128, NT]); MSQ = sb("MSQ", [128, NT]); VAR = sb("VAR", [128, NT]); RSTD = MSQ
    Hh = sb("Hh", [128, 22, NT], BF16); SGT = sb("SGT", [128, 2, NT])
    ZB = Hh[:, 0:8, :]; ZQ = Hh[:, 8:16, :]
    LNW = sb("LNW", [128, DEPTH, 4, 8])
    ARN = sb("ARN", [128, 6144])
    if NE:
        RKV = ARN[0:64, 0:3072].rearrange("p (c n) -> p c n", c=24); DRKV = ARN[0:64, 3072:6144].rearrange("p (c n) -> p c n", c=24)
        XWA = sb("XWA", [64, 2, NT]); DXWA = sb("DXWA", [64, 2, NT])
        XG = sb("XG", [128, 1, NT]); DXG = sb("DXG", [128, 1, NT])
        BQK = sb("BQK", [64, 8, NT]); BV = sb("BV", [128, 4, NT]); BXA = sb("BXA", [16, 1, NT]); BRG = sb("BRG", [128, 4, NT])
        TH = sb("TH", [64, NT]); SGX = sb("SGX", [128, NT])
        WD = sb("WD", [64, 8, NT]); AA = sb("AA", [64, 8, NT]); GG = sb("GG", [64, 8, NT]); KK = sb("KK", [64, 8, NT])
        NB = sb("NB", [64, 8, NT])
        T1 = DRKV[:, 0:8, :]; T2 = DRKV[:, 8:16, :]; BON = DRKV[:, 16:24, :]
        OT = AA
        AL = sb("AL", [64, 4, NT]); OG = sb("OG", [128, 4, NT]); T3 = BV
        YAb = sb("YAb", [64, 8, NT], BF16); YBb = sb("YBb", [128, 4, NT], BF16)
        VTOKb = sb("VTOKb", [64, 512], BF16); GVTOKb = sb("GVTOKb", [64, 512], BF16)
        Hb = sb("Hb", [64, 512], BF16); Sb = sb("Sb", [64, 512], BF16); TMPb = sb("TMPb", [64, 512], BF16)
        Rb = sb("Rb", [64, 8, NT], BF16); QSb = sb("QSb", [64, 4, NT], BF16)
        RT = [ARN[0:64, 3072 + i * 512:3072 + (i + 1) * 512] for i in range(4)]
        GT = [sb("GT%d" % i, [64, 512]) for i in range(2)]
        STG = XIO[0:64, 0:512]
        HW = [[sb("HW%d_%d" % (e, s), [64, 512]) for s in range(2)] for e in range(NE)]
        SG = [[sb("SG%d_%d" % (e, s), [64, 512]) for s in range(2)] for e in range(NE)]
        SH1 = [[sb("SH1_%d_%d" % (e, s), [64, 24, 1]) for s in range(2)] for e in range(NE)]
        SH2 = [[sb("SH2_%d_%d" % (e, s), [64, 2, 1]) for s in range(2)] for e in range(NE)]
        SH3 = [[sb("SH3_%d_%d" % (e, s), [128, 1, 1]) for s in range(2)] for e in range(NE)]
        EPW = dict(w2=sb("w2s", [64, 512]), a2=sb("a2s", [64, 512]), g2=sb("g2s", [128, 512]), au=sb("aus", [16, 256]))
        EP = []
        for e in range(NE):
            EP.append(dict(mu1=sb("mu1_%d" % e, [64, 24, 1]), mu2=sb("mu2_%d" % e, [64, 2, 1]), mu3=sb("mu3_%d" % e, [128, 1, 1]),
                           w0=sb("w0_%d" % e, [64, 8]), a0=sb("a0_%d" % e, [64, 8]), kk=sb("kk_%d" % e, [64, 8, 1]), ka=sb("ka_%d" % e, [64, 8, 1]),
                           rk=sb("rk_%d" % e, [64, 8, 1]), lw=sb("lw_%d" % e, [64, 8, 1]), lb=sb("lb_%d" % e, [64, 8, 1]),
                           ab=sb("ab_%d" % e, [64, 4]), nw=sb("nw_%d" % e, [128, 1])))
    if NO:
        QTe = sb("QTe", [128, 8, NT], BF16); QTo = sb("QTo", [128, 8, NT], BF16)
        KTf = ARN[:, 0:1024].rearrange("p (c n) -> p c n", c=8); VT = ARN[:, 1024:2048].rearrange("p (c n) -> p c n", c=8)
        KR = [sb("KR%d" % o, [128, 8, 5 * 128], BF16) for o in range(NO)]
        VR = [sb("VR%d" % o, [128, 5, 16, 65], BF16) for o in range(NO)]
        M3s = sb("M3s", [128, 16, 128], BF16); M4s = sb("M4s", [128, 16, 128], BF16)
        M3 = [M3s for o in range(NO)]
        M4 = [M4s for o in range(NO)]
        MSD = nc.dram_tensor("msd", [NO, 2, 128, 16, 128], BF16).ap()
        CH = [sb("CH_%d" % o, [128, 16]) for o in range(NO)]
        M0 = sb("M0", [128, 128], BF16); MLOW = sb("MLOW", [128, 128])
        PT = [sb("PT%d" % i, [128, 5, 128], BF16) for i in range(2)]
        OTOK = ARN[:, 2048:3072].rearrange("p (h d) -> p h d", h=16); RDEN = sb("RDEN", [128, 16, 1]); OTb = sb("OTb", [128, 8, NT], BF16)
        XIO3 = XIO[:, :].rearrange("p (h d) -> p h d", h=16)

    if os.environ.get('KTRACE'):
        print('SBUF_REMAINING_AFTER_ALLOC', nc.sbuf_bytes_remaining)
    kb.ms("pool", idn[:], 0.0, ["const"])
    P.add("pool", lambda e: e.affine_select(out=idn[:], in_=idn[:], pattern=[[-1, 128]], compare_op=ALU.not_equal, fill=1.0, base=0, channel_multiplier=1), r=["const"], w=["const"])
    kb.ms("pool", ones[:], 1.0, ["const"])
    kb.ms("pool", onesb[:], 1.0, ["const"])
    kb.cp("dve", idnb[:], idn[:], ["const"], ["const"])
    kb.cp("dve", SELb[:], idn[0:64, 0:64].unsqueeze(2).to_broadcast([64, 64, 64]), ["const"], ["const"])
    for l in range(DEPTH):
        for j, nm in enumerate(["ln1_w", "ln1_b", "ln2_w", "ln2_b"]):
            kb.dma("pool", LNW[:, l, j, :], I[nm][l].rearrange("(c p) -> p c", p=128), [], ["const"], "setup", slow=True)
    for e in range(NE):
        ep = EP[e]
        sl = lambda ap, p: ap.rearrange("(c p) -> p c", p=p)
        kb.dma("pool", ep["mu1"][:, :, 0], sl(I["a_mu"][e, 0:1536], 64), [], ["const"], "setup", slow=True)
        kb.dma("pool", ep["mu2"][:, :, 0], sl(I["a_mu"][e, 1536:1664], 64), [], ["const"], "setup", slow=True)
        kb.dma("pool", ep["mu3"][:, :, 0], sl(I["a_mu"][e, 1664:1792], 128), [], ["const"], "setup", slow=True)
        kb.dma("pool", ep["w0"][:, :], sl(I["a_w0"][e], 64), [], ["const"], "setup", slow=True)
        kb.dma("pool", ep["a0"][:, :], sl(I["a_a0"][e], 64), [], ["const"], "setup", slow=True)
        kb.dma("pool", ep["kk"][:, :, 0], sl(I["a_k_k"][e], 64), [], ["const"], "setup", slow=True)
        kb.dma("pool", ep["ka"][:, :, 0], sl(I["a_k_a"][e], 64), [], ["const"], "setup", slow=True)
        kb.dma("pool", ep["rk"][:, :, 0], I["a_r_k"][e].rearrange("h p -> p h"), [], ["const"], "setup", slow=True)
        kb.dma("pool", ep["lw"][:, :, 0], sl(I["a_ln_w"][e], 64), [], ["const"], "setup", slow=True)
        kb.dma("pool", ep["lb"][:, :, 0], sl(I["a_ln_b"][e], 64), [], ["const"], "setup", slow=True)
        kb.dma("pool", ep["ab"][:, :], sl(I["b_alpha_bias"][e], 64), [], ["const"], "setup", slow=True)
        kb.dma("pool", ep["nw"][:, :], sl(I["b_norm_w"][e], 128), [], ["const"], "setup", slow=True)
        kb.act(ep["ab"][:, :], ep["ab"][:, :], AF.Copy, ["const"], ["const"], scale=-1.0)
        for s in range(2):
            for t_ in (HW[e][s], SG[e][s]):
                kb.ms("pool", t_[:], 0.0, [t_.name])
            for t_ in (SH1[e][s], SH2[e][s], SH3[e][s]):
                kb.ms("pool", t_[:], 0.0, [t_.name])
    def cast_w(name, l, rows):
        for r0 in range(0, rows, 128):
            r1 = min(rows, r0 + 128)
            kb.dma("pool", WB[name][l, r0:r1, :], I[name][l, r0:r1, :], [], [("wb", name + str(l))], "cast_" + name + str(l))
    for l in range(DEPTH if not os.environ.get('NOCAST') else 0):
        if l % 2 == 0:
            cast_w("w_in_mix", l // 2, D); cast_w("w_out_mix", l // 2, D)
        else:
            cast_w("c_w_qkv", l // 2, D); cast_w("c_w_o", l // 2, D)
        cast_w("ffn_w_in", l, D); cast_w("ffn_w_out", l, FFH)
    if NO:
        kb.ms("pool", MLOW[:], 1.0, ["const"])
        P.add("pool", lambda e: e.affine_select(out=MLOW[:], in_=MLOW[:], pattern=[[-1, 128]], compare_op=ALU.is_gt, fill=0.0, base=0, channel_multiplier=1), r=["const"], w=["const"])
        kb.ms("pool", M0[:], 0.0, ["const"])
        kb.ms("pool", M0[0:64, 64:128], NEG, ["const"])
        TS8 = XIO[:, :].rearrange("p (h i) -> p h i", h=8)
        for o in range(NO):
            kb.ms("dve", XIO[0:16, 0:512], 0.0, ["XIO"])
            kb.dma("pool", EXT[:, :], XIO[0:16, 0:512], ["XIO"], ["ext"], "setup")
            kb.dma("pool", EXT[:, 0:257], I["c_rel_bias"][o], ["ext"], ["ext"], "setup")
            kb.dma("pool", REP, EXT.unsqueeze(1).to_broadcast([16, 128, 512]), ["ext"], ["rep"], "setup")
            kb.dma("pool", CH[o][:, :], I["c_rel_bias"][o][:, 256].partition_broadcast(128), [], ["const"], "setup", slow=True)
            for (MM, off) in ((M3[o], 256), (M4[o], 128)):
                for hh in range(2):
                    src = bass.AP(REP_T, off + hh * 8 * 65536, [[511, 128], [65536, 8], [1, 128]])
                    kb.dma("pool", TS8, src, ["rep"], ["XIO"], "setup")
                    kb.tt("dve", TS8, TS8, CH[o][:, hh * 8:hh * 8 + 8].unsqueeze(2).to_broadcast([128, 8, 128]), ALU.subtract, ["XIO", "const"], ["XIO"])
                    if off == 256:
                        kb.tt("dve", MM[:, hh * 8:hh * 8 + 8, :], TS8, MLOW[:].unsqueeze(1).to_broadcast([128, 8, 128]), ALU.mult, ["XIO", "const"], ["MB"])
                    else:
                        kb.cp("dve", MM[:, hh * 8:hh * 8 + 8, :], TS8, ["XIO"], ["MB"])
                if off == 128:
                    kb.ms("dve", MM[64:128, :, 0:64], NEG, ["MB"])
                kb.dma("pool", MSD[o, 0 if off == 256 else 1], MM[:], ["MB"], [("msd", o)], "stm")
            kb.ms("pool", VR[o][:, :, :, 64:65], 1.0, ["VR%d" % o])
            kb.ms("pool", QTe[:], 0.0, ["QTe"])
            kb.ms("pool", QTo[:], 0.0, ["QTo"])

    DBG = os.environ.get('DBG')
    dbg_on = [False]

    def dbg(name, ap, key, shape, dt=F32):
        if not (DBG and dbg_on[0]):
            return
        d = nc.dram_tensor("dbg_" + name, list(shape), dt, kind="ExternalOutput").ap()
        kb.dma("pool", d, ap, [key], [], "dbg_" + name)
    ev_flip = [0]

    def evac_copy(out, in_, r, w):
        ev_flip[0] ^= 1
        if ev_flip[0]:
            kb.act(out, in_, AF.Copy, r, w)
        else:
            kb.cp("dve", out, in_, r, w)

    def layer_norm(l, which):
        dbg("z%d_%d" % (l, which), Z[:], "Z", [128, 8, NT])
        kb.act(ZB, Z[:], AF.Copy, ["Z"], ["Hh"])
        kb.tt("dve", ZQ, Z[:], Z[:], ALU.mult, ["Z"], ["Hh"])
        bt, bk = kb.bank()
        for kc in range(8):
            kb.mm(bt[:, 0:NT], onesb[:], ZB[:, kc, :], ["const", "Hh"], [bk], start=(kc == 0), stop=(kc == 7))
        for kc in range(8):
            kb.mm(bt[:, NT:2 * NT], onesb[:], ZQ[:, kc, :], ["const", "Hh"], [bk], start=(kc == 0), stop=(kc == 7))
        kb.act(MEAN[:], bt[:, 0:NT], AF.Copy, [bk], ["MEAN"], scale=1.0 / D)
        kb.tt("dve", MSQ[:], MEAN[:], MEAN[:], ALU.mult, ["MEAN"], ["MSQ"])
        kb.stt("dve", VAR[:], bt[:, NT:2 * NT], 1.0 / D, MSQ[:], ALU.mult, ALU.subtract, [bk, "MSQ"], ["VAR"])
        kb.act(VAR[:], VAR[:], AF.Sqrt, ["VAR"], ["VAR"], bias=LN_EPS)
        kb.rcp(RSTD[:], VAR[:], ["VAR"], ["MSQ"])
        kb.tt("dve", Z[:], Z[:], MEAN[:].unsqueeze(1).to_broadcast([128, 8, NT]), ALU.subtract, ["Z", "MEAN"], ["Z"])
        kb.tt("dve", Z[:], Z[:], RSTD[:].unsqueeze(1).to_broadcast([128, 8, NT]), ALU.mult, ["Z", "MSQ"], ["Z"])
        kb.tt("dve", Z[:], Z[:], LNW[:, l, 2 * which, :].unsqueeze(2).to_broadcast([128, 8, NT]), ALU.mult, ["Z", "const"], ["Z"])
        kb.tt("dve", X[:], Z[:], LNW[:, l, 2 * which + 1, :].unsqueeze(2).to_broadcast([128, 8, NT]), ALU.add, ["Z", "const"], ["X"])
        kb.act(Xb[:], X[:], AF.Copy, ["X"], ["Xb"])
        dbg("x%d_%d" % (l, which), X[:], "X", [128, 8, NT])

    def resid_evac(pi):
        def f(banks):
            per = 4 // len(banks)
            for bi, (bt, bk) in enumerate(banks):
                c0 = 4 * pi + bi * per
                kb.stt("dve", Z[:, c0:c0 + per, :], X[:, c0:c0 + per, :], ALPHA,
                       bt[:, 0:per * NT].rearrange("p (c n) -> p c n", c=per), ALU.mult, ALU.add, ["X", bk], ["Z"])
        return f

    G4 = [(0, 128, 1), (128, 128, 1), (256, 128, 1), (384, 128, 1)]

    def ffn(l):
        def ev(j, n):
            def f(banks):
                (g, gk), (u, uk) = banks
                kb.act(SGT[:, 0:n, :], g[:, 0:n * NT].rearrange("p (c n) -> p c n", c=n), AF.Silu, [gk], ["SGT"])
                kb.tt("dve", Hh[:, 2 * j:2 * j + n, :], SGT[:, 0:n, :], u[:, 0:n * NT].rearrange("p (c n) -> p c n", c=n), ALU.mult, ["SGT", uk], ["Hh"])
            return f
        passes = []
        for j in range(11):
            passes.append(dict(segs=[(j * 256, 256), (FFH + j * 256, 256)], kslabs=[(0, 128, 8)], groups=[(0, 128, 2), (256, 128, 2)], evac=ev(j, 2)))
        kb.dense("ffn_w_in%d" % l, WB["ffn_w_in"][l], passes, lambda si, kc: (Xb[:, kc, :], "Xb"), NT)
        passes = [dict(segs=[(p * 512, 512)], kslabs=[(0, 128, 8), (1024, 128, 8), (2048, 128, 6)], groups=G4, evac=resid_evac(p)) for p in range(2)]
        kb.dense("ffn_w_out%d" % l, WB["ffn_w_out"][l], passes, lambda si, kc: (Hh[:, si * 8 + kc, :], "Hh"), NT)
        layer_norm(l, 1)

    def even_layer(l, segs):
        e = l // 2
        ep = dict(EP[e])
        ep.update(EPW)
        P.add("sp", lambda en: [en.dma_start(out=EPW["w2"][:, :], in_=I["a_w2"][e]), en.dma_start(out=EPW["a2"][:, :], in_=I["a_a2"][e]),
                                en.dma_start(out=EPW["g2"][:, :], in_=I["a_g2"][e]), en.dma_start(out=EPW["au"][:, :], in_=I["b_alpha_up"][e])],
              r=[], w=["EPW"], dkey="ldep", ndma=4)
        hb = lambda ap: ap.to_broadcast([64, 8, NT])

        def gcopy(dst, key, i0, n, M):
            def f(banks, gi):
                bt, bk = banks[gi]
                evac_copy(dst[0:M, i0:i0 + n, :], bt[0:M, 0:n * NT].rearrange("p (c n) -> p c n", c=n), [bk], [key])
            return f

        def mk(segs_, groups, fs):
            def ev(banks):
                for gi, f in enumerate(fs):
                    f(banks, gi)
            return dict(segs=segs_, kslabs=[(0, 128, 8)], groups=groups, evac=ev)
        passes = []
        for s_ in range(3):
            passes.append(mk([(s_ * 512, 512)], [(0, 64, 4), (256, 64, 4)], [gcopy(RKV, "RKV", s_ * 8, 4, 64), gcopy(RKV, "RKV", s_ * 8 + 4, 4, 64)]))
        passes.append(mk([(1536, 512)], [(0, 64, 2), (128, 128, 1), (256, 64, 4)], [gcopy(XWA, "XWA", 0, 2, 64), gcopy(XG, "XG", 0, 1, 128), gcopy(BQK, "BQK", 0, 4, 64)]))
        passes.append(mk([(2048, 512)], [(0, 64, 4), (256, 128, 2)], [gcopy(BQK, "BQK", 4, 4, 64), gcopy(BV, "BV", 0, 2, 128)]))
        passes.append(mk([(2560, 272)], [(0, 128, 2), (256, 16, 1)], [gcopy(BV, "BV", 2, 2, 128), gcopy(BXA, "BXA", 0, 1, 16)]))
        passes.append(mk([(2832, 512)], [(0, 128, 4)], [gcopy(BRG, "BRG", 0, 4, 128)]))
        kb.dense("w_in_mix%d" % e, WB["w_in_mix"][e], passes, lambda si, kc: (Xb[:, kc, :], "Xb"), NT)
        for (T_, Dt, MU, SHL, Pn, C, tk, dk) in ((RKV, DRKV, ep["mu1"], SH1[e], 64, 24, "RKV", "DRKV"), (XWA, DXWA, ep["mu2"], SH2[e], 64, 2, "XWA", "DXWA"),
                                                (XG, DXG, ep["mu3"], SH3[e], 128, 1, "XG", "DXG")):
            for (slot, t0, t1) in segs:
                SH = SHL[slot]
                kb.tt("dve", Dt[:, :, t0 + 1:t1], T_[:, :, t0:t1 - 1], T_[:, :, t0 + 1:t1], ALU.subtract, [tk], [dk])
                kb.tt("dve", Dt[:, :, t0:t0 + 1], SH[:], T_[:, :, t0:t0 + 1], ALU.subtract, [tk, SH.name], [dk])
                kb.cp("dve", SH[:], T_[:, :, t1 - 1:t1], [tk], [SH.name])
            kb.tt("dve", Dt[:], Dt[:], MU[:].to_broadcast([Pn, C, NT]), ALU.mult, [dk, "const"], [dk])
            kb.tt("dve", T_[:], T_[:], Dt[:], ALU.add, [tk, dk], [tk])
        R_ = RKV[:, 0:8, :]; K_ = RKV[:, 8:16, :]; V_ = RKV[:, 16:24, :]
        kb.act(TH[:], XWA[:, 0, :], AF.Tanh, ["XWA"], ["TH"])
        kb.act(SGX[:], XG[:, 0, :], AF.Sigmoid, ["XG"], ["SGX"])
        for (wt, rhs, rk_, dst, dk_, bias, fn) in ((ep["w2"], TH[:], "TH", WD, "WD", ep["w0"], AF.Sigmoid), (ep["a2"], XWA[:, 1, :], "XWA", AA, "AA", ep["a0"], AF.Sigmoid),
                                                  (ep["g2"], SGX[:], "SGX", GG, "GG", None, AF.Copy)):
            for half in range(2):
                bt, bk = kb.bank()
                for j in range(4):
                    h = half * 4 + j
                    kb.mm(bt[0:64, j * NT:(j + 1) * NT], wt[:, h * 64:(h + 1) * 64], rhs, ["EPW", rk_], [bk])
                for j in range(4):
                    h = half * 4 + j
                    if bias is not None:
                        kb.act(dst[:, h, :], bt[0:64, j * NT:(j + 1) * NT], fn, [bk, "const"], [dk_], bias=bias[:, h:h + 1])
                    else:
                        kb.act(dst[:, h, :], bt[0:64, j * NT:(j + 1) * NT], fn, [bk], [dk_])
        kb.act(WD[:], WD[:], AF.Exp, ["WD"], ["WD"], scale=-ED)
        kb.tt("dve", KK[:], K_, hb(ep["kk"][:]), ALU.mult, ["RKV", "const"], ["KK"])
        kb.tt("dve", T1, KK[:], KK[:], ALU.mult, ["KK"], ["DRKV"])
        for half in range(2):
            bt, bk = kb.bank()
            kb.mm(bt[0:64, :], ones[0:64, 0:64], T1[:, half * 4:half * 4 + 4, :], ["const", "DRKV"], [bk])
            kb.act(T2[:, half * 4:half * 4 + 4, :], bt[0:64, :].rearrange("p (c n) -> p c n", c=4), AF.Sqrt, [bk], ["DRKV"], bias=1e-12)
        kb.rcp(T2, T2, ["DRKV"], ["DRKV"])
        kb.tt("dve", KK[:], KK[:], T2, ALU.mult, ["KK", "DRKV"], ["KK"])
        kb.stt("dve", NB[:], KK[:], -1.0, AA[:], ALU.mult, ALU.mult, ["KK", "AA"], ["NB"])
        kb.stt("dve", T1, AA[:], -1.0, hb(ep["ka"][:]), ALU.add, ALU.mult, ["AA", "const"], ["DRKV"])
        kb.stt("dve", K_, T1, 1.0, K_, ALU.add, ALU.mult, ["DRKV", "RKV"], ["RKV"])
        kb.tt("dve", T1, R_, K_, ALU.mult, ["RKV"], ["DRKV"])
        kb.tt("dve", T1, T1, hb(ep["rk"][:]), ALU.mult, ["DRKV", "const"], ["DRKV"])
        for half in range(2):
            bt, bk = kb.bank()
            kb.mm(bt[0:64, :], ones[0:64, 0:64], T1[:, half * 4:half * 4 + 4, :], ["const", "DRKV"], [bk])
            kb.tt("dve", BON[:, half * 4:half * 4 + 4, :], bt[0:64, :].rearrange("p (c n) -> p c n", c=4), V_[:, half * 4:half * 4 + 4, :], ALU.mult, [bk, "RKV"], ["DRKV"])
        bt, bk = kb.bank()
        for h in range(4):
            kb.mm(bt[0:64, h * NT:(h + 1) * NT], ep["au"][:, h * 64:(h + 1) * 64], BXA[:, 0, :], ["EPW", "BXA"], [bk])
        for h in range(4):
            kb.act(AL[:, h, :], bt[0:64, h * NT:(h + 1) * NT], AF.Exp, [bk, "const"], ["AL"], bias=ep["ab"][:, h:h + 1], scale=-1.0)
        kb.act(AL[:], AL[:], AF.Ln, ["AL"], ["AL"], bias=1.0)
        kb.act(AL[:], AL[:], AF.Exp, ["AL"], ["AL"], scale=-1.0 / 16.0)
        kb.act(QSb[:], BQK[:, 0:4, :], AF.Copy, ["BQK"], ["QSb"], scale=0.125)
        kb.act(Rb[:], R_, AF.Copy, ["RKV"], ["Rb"])
        P.add("dve", lambda en: en.memset(RT[0][:, 0:1], 0.0), r=[], w=["DRKV", "RT0", "RT1", "RT2", "RT3"])
        obr, obrk = fixed[0]
        obg, obgk = fixed[1]
        for (slot, t0, t1) in segs:
            Hs = HW[e][slot]; Ss = SG[e][slot]
            hk = Hs.name; sk = Ss.name
            H3 = Hs[:].rearrange("p (h v) -> p h v", h=8)
            S3 = Ss[:].rearrange("p (h v) -> p h v", h=4)
            for c0 in range(t0, t1, 64):
                bt, bk = kb.bank()
                for h in range(8):
                    kb.tr(bt[0:64, h * 64:(h + 1) * 64], V_[:, h, c0:c0 + 64], idn[0:64, 0:64], ["RKV"], [bk])
                kb.act(VTOKb[:], bt[0:64, :], AF.Copy, [bk], ["VTOKb"])
                bt, bk = kb.bank()
                for h in range(4):
                    kb.tr(bt[0:64, h * 128:(h + 1) * 128], BV[:, h, c0:c0 + 64], idn[:, :], ["BV"], [bk])
                kb.cp("dve", GVTOKb[:], bt[0:64, :], [bk], ["GVTOKb"])
                v8 = lambda ap: ap.rearrange("p (h v) -> p h v", h=8)
                v4g = lambda ap: ap.rearrange("p (h v) -> p h v", h=4)

                def matvecs(tl_):
                    t_ = c0 + tl_
                    for h in range(8):
                        kb.mm(obr[0:64, h * 64 + tl_:h * 64 + tl_ + 1], Hb[:, h * 64:(h + 1) * 64], Rb[:, h, t_:t_ + 1], ["Hb", "Rb"], [obrk])

                def matvecs_g(tl_):
                    t_ = c0 + tl_
                    for h in range(4):
                        kb.mm(obg[:, h * 64 + tl_:h * 64 + tl_ + 1], Sb[:, h * 128:(h + 1) * 128], QSb[:, h, t_:t_ + 1], ["Sb", "QSb"], [obgk])

                def vbg(tl_):
                    vb_, vbk_ = kb.bank()
                    kb.mm(vb_[0:64, :], SELb[:, tl_, :], VTOKb[:], ["const", "VTOKb"], [vbk_])
                    vg_, vgk_ = kb.bank()
                    kb.mm(vg_[0:64, :], SELb[:, tl_, :], GVTOKb[:], ["const", "GVTOKb"], [vgk_])
                    return vb_, vbk_, vg_, vgk_

                nxt = vbg(0) if not os.environ.get('SKIP_REC') else None
                for tl in range(64 if not os.environ.get('SKIP_REC') else 0):
                    t = c0 + tl
                    col = lambda T_: T_[:, :, t:t + 1]
                    vb, vbk, vg, vgk = nxt
                    kb.tt("dve", v8(TMPb[:]), H3, col(KK).to_broadcast([64, 8, 64]), ALU.mult, [hk, "KK"], ["TMPb"])
                    sa, sak = kb.bank()
                    kb.mm(sa[0:64, :], onesb[0:64, 0:64], TMPb[:], ["const", "TMPb"], [sak])
                    if tl > 0:
                        matvecs(tl - 1)
                    kb.tt("dve", v8(RT[3][:]), v8(vb[0:64, :]), col(K_).to_broadcast([64, 8, 64]), ALU.mult, [vbk, "RKV"], ["RT3"])
                    kb.tt("dve", v4g(GT[1][:]), v4g(vg[0:64, :]), BQK[:, 4:8, t:t + 1].to_broadcast([64, 4, 128]), ALU.mult, [vgk, "BQK"], ["GT1"])
                    kb.tt("pool", v8(RT[1][:]), H3, col(WD).to_broadcast([64, 8, 64]), ALU.mult, [hk, "WD"], ["RT1"])
                    kb.tt("pool", v4g(GT[0][:]), S3, col(AL).to_broadcast([64, 4, 128]), ALU.mult, [sk, "AL"], ["GT0"])
                    if tl > 0:
                        matvecs_g(tl - 1)
                    kb.tt("pool", Ss[:], GT[0][:], GT[1][:], ALU.add, ["GT0", "GT1"], [sk])
                    kb.act(Sb[:], Ss[:], AF.Copy, [sk], ["Sb"])
                    if tl < 63:
                        nxt = vbg(tl + 1)
                    kb.tt("dve", v8(RT[2][:]), v8(sa[0:64, :]), col(NB).to_broadcast([64, 8, 64]), ALU.mult, [sak, "NB"], ["RT2"])
                    kb.tt("dve", RT[2][:], RT[2][:], RT[3][:], ALU.add, ["RT2", "RT3"], ["RT2"])
                    kb.tt("dve", Hs[:], RT[1][:], RT[2][:], ALU.add, ["RT1", "RT2"], [hk])
                    kb.act(Hb[:], Hs[:], AF.Copy, [hk], ["Hb"])
                if not os.environ.get('SKIP_REC'):
                    matvecs(63)
                    matvecs_g(63)
                kb.act(OT[:, :, c0:c0 + 64], obr[0:64, :].rearrange("p (h t) -> p h t", h=8), AF.Copy, [obrk], ["AA"])
                kb.cp("dve", OG[:, :, c0:c0 + 64], obg[:, 0:256].rearrange("p (h t) -> p h t", h=4), [obgk], ["OG"])
        P.add("dve", lambda en: en.tensor_tensor(out=T1, in0=OT[:], in1=OT[:], op=ALU.mult), r=["AA"], w=["DRKV", "RT0", "RT1", "RT2", "RT3"])
        for half in range(2):
            hs = slice(half * 4, half * 4 + 4)
            b1, b1k = kb.bank()
            kb.mm(b1[0:64, :], ones[0:64, 0:64], OT[:, hs, :], ["const", "AA"], [b1k])
            b2, b2k = kb.bank()
            kb.mm(b2[0:64, :], ones[0:64, 0:64], T1[:, hs, :], ["const", "DRKV"], [b2k])
            v4 = lambda b: b[0:64, :].rearrange("p (c n) -> p c n", c=4)
            kb.act(T2[:, hs, :], v4(b1), AF.Copy, [b1k], ["DRKV"], scale=1.0 / 64)
            kb.stt("dve", OT[:, hs, :], v4(b1), -1.0 / 64, OT[:, hs, :], ALU.mult, ALU.add, [b1k, "AA"], ["AA"])
            kb.tt("dve", T2[:, hs, :], T2[:, hs, :], T2[:, hs, :], ALU.mult, ["DRKV"], ["DRKV"])
            kb.stt("dve", T2[:, hs, :], v4(b2), 1.0 / 64, T2[:, hs, :], ALU.mult, ALU.subtract, [b2k, "DRKV"], ["DRKV"])
        kb.act(T2, T2, AF.Sqrt, ["DRKV"], ["DRKV"], bias=A_EPS)
        kb.rcp(T2, T2, ["DRKV"], ["DRKV"])
        kb.tt("dve", OT[:], OT[:], T2, ALU.mult, ["AA", "DRKV"], ["AA"])
        kb.tt("dve", OT[:], OT[:], hb(ep["lw"][:]), ALU.mult, ["AA", "const"], ["AA"])
        kb.tt("dve", OT[:], OT[:], hb(ep["lb"][:]), ALU.add, ["AA", "const"], ["AA"])
        kb.tt("dve", OT[:], OT[:], BON, ALU.add, ["AA", "DRKV"], ["AA"])
        kb.tt("dve", YAb[:], OT[:], GG[:], ALU.mult, ["AA", "GG"], ["YAb"])
        kb.tt("dve", T3[:], OG[:], OG[:], ALU.mult, ["OG"], ["BV"])
        bt, bk = kb.bank()
        kb.mm(bt[:, :], ones[:, :], T3[:], ["const", "BV"], [bk])
        kb.act(T3[:], bt[:, :].rearrange("p (c n) -> p c n", c=4), AF.Sqrt, [bk], ["BV"], bias=LN_EPS, scale=1.0 / 128)
        kb.rcp(T3[:], T3[:], ["BV"], ["BV"])
        kb.tt("dve", OG[:], OG[:], T3[:], ALU.mult, ["OG", "BV"], ["OG"])
        kb.act(T3[:], BRG[:], AF.Silu, ["BRG"], ["BV"])
        kb.stt("dve", YBb[:], OG[:], ep["nw"][:, 0:1], T3[:], ALU.mult, ALU.mult, ["OG", "const", "BV"], ["YBb"])
        dbg("ya%d" % l, YAb[:], "YAb", [64, 8, NT], BF16)
        dbg("yb%d" % l, YBb[:], "YBb", [128, 4, NT], BF16)
        dbg("gg%d" % l, GG[:], "GG", [64, 8, NT])
        dbg("ot%d" % l, OT[:], "AA", [64, 8, NT])
        passes = [dict(segs=[(p * 512, 512)], kslabs=[(0, 64, 8), (512, 128, 4)], groups=G4, evac=resid_evac(p)) for p in range(2)]
        kb.dense("w_out_mix%d" % e, WB["w_out_mix"][e], passes, lambda si, kc: ((YAb[:, kc, :], "YAb") if si == 0 else (YBb[:, kc, :], "YBb")), NT)
        layer_norm(l, 0)

    def odd_layer(l, tile):
        o = l // 2
        segs = tile["segs"]
        krk = "KR%d" % o; vrk = "VR%d" % o
        if NO > 1:
            P.add("sp", lambda en: [en.dma_start(out=M3s[:], in_=MSD[o, 0]), en.dma_start(out=M4s[:], in_=MSD[o, 1])],
                  r=[("msd", o)], w=["MB"], dkey="ldm", ndma=2)
        is_p = tile["kind"] == "p"
        g = tile.get("g", 0)
        kslot = (g % 5) if is_p else 4

        def ev_q(ci):
            def f(banks, gi):
                bt, bk = banks[gi]
                v = bt[:, 0:4 * NT].rearrange("p (c n) -> p c n", c=4)
                kb.act(QTe[0:64, ci:ci + 4, :], v[0:64], AF.Copy, [bk], ["QTe"], scale=0.125)
                kb.act(QTo[64:128, ci:ci + 4, :], v[64:128], AF.Copy, [bk], ["QTo"], scale=0.125)
            return f

        def ev_k(ci):
            def f(banks, gi):
                bt, bk = banks[gi]
                v = bt[:, 0:4 * NT].rearrange("p (c n) -> p c n", c=4)
                kb.cp("dve", KTf[:, ci:ci + 4, :], v, [bk], ["KTf"])
            return f

        def ev_v(ci):
            def f(banks, gi):
                bt, bk = banks[gi]
                v = bt[:, 0:4 * NT].rearrange("p (c n) -> p c n", c=4)
                evac_copy(VT[:, ci:ci + 4, :], v, [bk], ["VT"])
            return f

        passes = []
        for j, evf in enumerate([ev_q(0), ev_q(4), ev_k(0), ev_k(4), ev_v(0), ev_v(4)]):
            passes.append(dict(segs=[(j * 512, 512)], kslabs=[(0, 128, 8)], groups=[(0, 128, 4)], evac=(lambda banks, evf=evf: evf(banks, 0))))
        kb.dense("c_w_qkv%d" % o, WB["c_w_qkv"][o], passes, lambda si, kc: (Xb[:, kc, :], "Xb"), NT)

        def tok_major(src, skey, t0, n, dst3, dkey_):
            for half in range(2):
                bt, bk = kb.bank()
                for j in range(4):
                    c = half * 4 + j
                    kb.tr(bt[0:n, j * 128:(j + 1) * 128], src[:, c, t0:t0 + n], idn[:, :], [skey], [bk])
                evac_copy(dst3[0:n, half * 8:half * 8 + 8, :], bt[0:n, :].rearrange("p (h d) -> p h d", h=8), [bk], [dkey_])

        for (slot, t0, t1) in segs:
            n = t1 - t0
            if is_p:
                kb.act(KR[o][:, :, kslot * 128:kslot * 128 + n], KTf[:, :, t0:t1], AF.Copy, ["KTf"], [krk])
            else:
                sq = tile["seqs"][slot]
                for kbk in range(4):
                    kb.dma("sp", XIO3, I["cache_c_k"][o, sq, :, kbk * 128:(kbk + 1) * 128, :].rearrange("h t d -> t h d"), [], ["XIO"], "ldx")
                    for half in range(2):
                        bt, bk = kb.bank()
                        for j in range(4):
                            c = half * 4 + j
                            kb.tr(bt[:, j * 128:(j + 1) * 128], XIO[:, c * 128:(c + 1) * 128], idn[:, :], ["XIO"], [bk])
                        evac_copy(KR[o][:, half * 4:half * 4 + 4, kbk * 128:(kbk + 1) * 128], bt[:, :].rearrange("p (c n) -> p c n", c=4), [bk], [krk])
                    kb.dma("pool", VR[o][:, kbk, :, 0:64], I["cache_c_v"][o, sq, :, kbk * 128:(kbk + 1) * 128, :].rearrange("h t d -> t h d"), [], [vrk], "ldv")
                kb.act(KR[o][:, :, 4 * 128:4 * 128 + n], KTf[:, :, t0:t1], AF.Copy, ["KTf"], [krk])
            tok_major(VT, "VT", t0, n, XIO3, "XIO")
            kb.cp("dve", VR[o][0:n, kslot, :, 0:64], XIO3[0:n], ["XIO"], [vrk])
            emit = (not is_p) or (g >= NG - KEEP // 128)
            if emit:
                if is_p:
                    r0 = (g - (NG - KEEP // 128)) * 128
                    dv = O["p_v"][o, :, r0:r0 + 128, :]; dk_ = O["p_k"][o, :, r0:r0 + 128, :]
                else:
                    dv = O["s_v"][o, slot, :, :, :]; dk_ = O["s_k"][o, slot, :, :, :]
                kb.dma("pool", dv.rearrange("h t d -> t h d"), XIO3[0:n], ["XIO"], [], "stx")
                tok_major(KTf, "KTf", t0, n, XIO3, "XIO")
                kb.dma("pool", dk_.rearrange("h t d -> t h d"), XIO3[0:n], ["XIO"], [], "stx")
            if is_p:
                blocks = []
                for b in range(5):
                    kbi = g - 4 + b
                    if kbi < 0:
                        continue
                    bias = {0: "M0", 3: "M3", 4: "M4"}.get(b)
                    blocks.append((kbi % 5, 128, bias))
            else:
                blocks = [(0, 128, None), (1, 128, None), (2, 128, None), (3, 128, "M3"), (4, n, "M4")]
            nb_ = len(blocks)
            obanks = fixed
            for h in range(16 if not os.environ.get('SKIP_ATT') else 0):
                c = h // 2
                QTs = QTe if h % 2 == 0 else QTo
                qk = "QTe" if h % 2 == 0 else "QTo"
                pt = PT[h % 2]; ptk = "PT%d" % (h % 2)
                b1 = kb.bank(); b2 = kb.bank()
                for bi, (ks, nk, bias) in enumerate(blocks):
                    bt, bk = (b1 if bi < 4 else b2)
                    oc = (bi % 4) * 128
                    kb.mm(bt[0:nk, oc:oc + n], KR[o][:, c, ks * 128:ks * 128 + nk], QTs[:, c, t0:t1], [krk, qk], [bk], start=True, stop=(bias is None))
                    if bias == "M0":
                        kb.mm(bt[0:nk, oc:oc + n], idnb[0:nk, 0:nk], M0[0:nk, 0:n], ["const"], [bk], start=False, stop=True)
                    elif bias == "M3":
                        kb.mm(bt[0:nk, oc:oc + n], idnb[0:nk, 0:nk], M3[o][0:nk, h, 0:n], ["const", "MB"], [bk], start=False, stop=True)
                    elif bias == "M4":
                        kb.mm(bt[0:nk, oc:oc + n], idnb[0:nk, 0:nk], M4[o][0:nk, h, 0:n], ["const", "MB"], [bk], start=False, stop=True)
                for bi, (ks, nk, bias) in enumerate(blocks):
                    bt, bk = (b1 if bi < 4 else b2)
                    oc = (bi % 4) * 128
                    kb.act(pt[0:nk, bi, 0:n], bt[0:nk, oc:oc + n], AF.Exp, [bk, "const"], [ptk], bias=CH[o][0:nk, h:h + 1])
                ob, obk = obanks[h // 7]
                hc = (h % 7) * 65
                for bi, (ks, nk, bias) in enumerate(blocks):
                    kb.mm(ob[0:n, hc:hc + 65], pt[0:nk, bi, 0:n], VR[o][0:nk, ks, h, :], [ptk, vrk], [obk], start=(bi == 0), stop=(bi == nb_ - 1))
            for bi_, (h0, nh) in enumerate(((0, 7), (7, 7), (14, 2))):
                ob, obk = obanks[bi_]
                v = ob[0:n, 0:nh * 65].rearrange("p (h d) -> p h d", h=nh)
                kb.rcp(RDEN[0:n, h0:h0 + nh, :], v[:, :, 64:65], [obk], ["RDEN"])
                kb.tt("dve", OTOK[0:n, h0:h0 + nh, :], v[:, :, 0:64], RDEN[0:n, h0:h0 + nh, :].to_broadcast([n, nh, 64]), ALU.mult, [obk, "RDEN"], ["OTOK"])
            OT2 = OTOK[:, :, :].rearrange("p h d -> p (h d)")
            for half in range(2):
                bt, bk = kb.bank()
                for j in range(4):
                    c = half * 4 + j
                    kb.tr(bt[:, j * 128:j * 128 + n], OT2[0:n, c * 128:(c + 1) * 128], idn[0:n, 0:n], ["OTOK"], [bk])
                evac_copy(OTb[:, half * 4:half * 4 + 4, t0:t1], bt[:, :].rearrange("p (c n) -> p c n", c=4)[:, :, 0:n], [bk], ["OTb"])
        passes = [dict(segs=[(p * 512, 512)], kslabs=[(0, 128, 8)], groups=[(0, 128, 4)], evac=resid_evac(p)) for p in range(2)]
        kb.dense("c_w_o%d" % o, WB["c_w_o"][o], passes, lambda si, kc: (OTb[:, kc, :], "OTb"), NT)
        layer_norm(l, 0)

    def run_tile(tile):
        xsrc = tile["x"]
        IOSTEP = int(os.environ.get('IOSTEP', '9'))
        kb.dma("sp", XIO[:, :], xsrc, [], ["XIO"], "ldx")
        for half in range(2 if IOSTEP >= 2 else 0):
            bt, bk = kb.bank()
            for j in range(4):
                c = half * 4 + j
                kb.tr(bt[:, j * 128:(j + 1) * 128], XIO[:, c * 128:(c + 1) * 128], idn[:, :], ["XIO"], [bk])
            v = bt[:, :].rearrange("p (c n) -> p c n", c=4)
            if IOSTEP >= 3:
                kb.act(X[:, half * 4:half * 4 + 4, :], v, AF.Copy, [bk], ["X"])
            if IOSTEP >= 4:
                kb.cp("dve", Xb[:, half * 4:half * 4 + 4, :], v, [bk, "X"], ["Xb"])
        MAXL = int(os.environ.get('MAXL', '99'))
        NOFFN = os.environ.get('NOFFN')
        NOMIX = os.environ.get('NOMIX')
        for l in range(min(DEPTH, MAXL)):
            if NOMIX:
                pass
            elif l % 2 == 0:
                even_layer(l, tile["segs"])
            else:
                odd_layer(l, tile)
            if not NOFFN:
                ffn(l)
        for half in range(2 if IOSTEP >= 5 else 0):
            bt, bk = kb.bank()
            for j in range(4):
                c = half * 4 + j
                kb.tr(bt[:, j * 128:(j + 1) * 128], X[:, c, :], idn[:, :], ["X"], [bk])
            if IOSTEP >= 6:
                evac_copy(XIO[:, half * 512:(half + 1) * 512], bt[:, :], [bk], ["XIO"])
        kb.dma("pool", tile["y"], XIO[:, :], ["XIO"], [], "stx")

    def state_out(e, slot, dw, dsh, dg):
        bt, bk = kb.bank()
        for h in range(8):
            kb.tr(bt[0:64, h * 64:(h + 1) * 64], HW[e][slot][:, h * 64:(h + 1) * 64], idn[0:64, 0:64], [HW[e][slot].name], [bk])
        kb.act(STG, bt[0:64, :], AF.Copy, [bk], ["XIO"])
        kb.dma("pool", dw.rearrange("h v k -> v h k"), STG.rearrange("p (h k) -> p h k", h=8), ["XIO"], [], "stx")
        kb.dma("pool", dg.rearrange("h d v -> d h v"), SG[e][slot][:].rearrange("p (h v) -> p h v", h=4), [SG[e][slot].name], [], "sts_g%d" % e)
        kb.dma("pool", dsh[0:1536].rearrange("(c p) -> p c", p=64), SH1[e][slot][:, :, 0], [SH1[e][slot].name], [], "sts_1%d" % e, slow=True)
        kb.dma("pool", dsh[1536:1664].rearrange("(c p) -> p c", p=64), SH2[e][slot][:, :, 0], [SH2[e][slot].name], [], "sts_2%d" % e, slow=True)
        kb.dma("pool", dsh[1664:1792].rearrange("(c p) -> p c", p=128), SH3[e][slot][:, :, 0], [SH3[e][slot].name], [], "sts_3%d" % e, slow=True)

    STOP = os.environ.get('STOP', '')
    for g in range(NG if STOP != 'setup' else 0):
        dbg_on[0] = (g == 0)
        run_tile(dict(kind="p", g=g, segs=[(0, 0, 128)], x=I["x_prompt"][g * 128:(g + 1) * 128, :], y=O["y_p"][g * 128:(g + 1) * 128, :]))
    for e in range(NE if STOP not in ('setup', 'p2') else 0):
        state_out(e, 0, O["p_wkv"][e], O["p_shift"][e], O["p_gla"][e])
    dbg_on[0] = False
    for e in range(NE if STOP == '' else 0):
        for slot in range(2):
            kb.dma("sp", STG.rearrange("p (h k) -> p h k", h=8), I["state_a_wkv"][e, slot].rearrange("h v k -> v h k"), [], ["XIO"], "ldx")
            bt, bk = kb.bank()
            for h in range(8):
                kb.tr(bt[0:64, h * 64:(h + 1) * 64], STG[:, h * 64:(h + 1) * 64], idn[0:64, 0:64], ["XIO"], [bk])
            kb.act(HW[e][slot][:], bt[0:64, :], AF.Copy, [bk], [HW[e][slot].name])
            kb.dma("sp", SG[e][slot][:].rearrange("p (h v) -> p h v", h=4), I["state_b_gla"][e, slot].rearrange("h d v -> d h v"), [], [SG[e][slot].name], "ldsg")
            sh = I["state_a_shift"][e, slot]
            kb.dma("pool", SH1[e][slot][:, :, 0], sh[0:1536].rearrange("(c p) -> p c", p=64), [], [SH1[e][slot].name], "ldsh1", slow=True)
            kb.dma("pool", SH2[e][slot][:, :, 0], sh[1536:1664].rearrange("(c p) -> p c", p=64), [], [SH2[e][slot].name], "ldsh2", slow=True)
            kb.dma("pool", SH3[e][slot][:, :, 0], sh[1664:1792].rearrange("(c p) -> p c", p=128), [], [SH3[e][slot].name], "ldsh3", slow=True)
    if STOP == '':
      run_tile(dict(kind="s", segs=[(0, 0, 64), (1, 64, 128)], seqs=[0, 1], x=I["x_sample"][:, :], y=O["y_s"][:, :]))
    for e in range(NE if STOP == '' else 0):
        for slot in range(2):
            state_out(e, slot, O["s_wkv"][e, slot], O["s_shift"][e, slot], O["s_gla"][e, slot])
    P.emit(nc.Block())
    for cm in reversed(kb.stack):
        cm.__exit__(None, None, None)
    return nc


_OUT_ORDER = ["y_p", "y_s", "p_wkv", "p_shift", "p_gla", "p_k", "p_v", "s_wkv", "s_shift", "s_gla", "s_k", "s_v"]


def run(inputs, SEQ=8192, DEPTH=4, ncores=8):
    nc = build(SEQ, DEPTH)
    f = lambda a: np.ascontiguousarray(np.asarray(a, dtype=np.float32))
    in_maps = []
    for c in range(ncores):
        m = {"x_prompt": f(inputs["x_prompt"][c]), "x_sample": f(inputs["x_sample"][2 * c:2 * c + 2]).reshape(128, D)}
        for k_ in ("state_a_wkv", "state_a_shift", "state_b_gla", "cache_c_k", "cache_c_v"):
            m[k_] = f(np.asarray(inputs[k_])[:, 2 * c:2 * c + 2])
        for k_ in inputs:
            if k_ not in m:
                m[k_] = f(inputs[k_])
        in_maps.append(m)
    print('sbuf bytes remaining', nc.sbuf_bytes_remaining) if os.environ.get('KTRACE') else None
    res = run_bass_kernel_spmd(nc, in_maps, core_ids=list(range(ncores)), **({'trace': True} if os.environ.get('KTRACE') else {}))
    print('EXEC_NS', res.exec_time_ns) if os.environ.get('KTRACE') else None
    R = res.results
    cat = lambda name, ax: np.stack([np.asarray(r[name]) for r in R], axis=ax)
    y_p = cat("y_p", 0)
    y_s = np.concatenate([np.asarray(r["y_s"]).reshape(2, 64, D) for r in R], axis=0)
    p_wkv = cat("p_wkv", 1); p_shift = cat("p_shift", 1); p_gla = cat("p_gla", 1)
    p_k = cat("p_k", 1); p_v = cat("p_v", 1)
    c2 = lambda name: np.concatenate([np.asarray(r[name]) for r in R], axis=1)
    return (y_p, y_s, p_wkv, p_shift, p_gla, p_k, p_v, c2("s_wkv"), c2("s_shift"), c2("s_gla"), c2("s_k"), c2("s_v"))


def kernel(**inputs):
    outs = run(inputs)
    return tuple(np.ascontiguousarray(o, dtype=np.float32) for o in outs)
```

```python
import os
import numpy as np
import concourse.bass as bass
import concourse.mybir as mybir
from concourse.bass_utils import run_bass_kernel_spmd

F32 = mybir.dt.float32
BF16 = mybir.dt.bfloat16
AF = mybir.ActivationFunctionType
ALU = mybir.AluOpType
AX = mybir.AxisListType


class Op:
    __slots__ = ("eng", "fn", "deps", "signal", "dkey", "ndma", "sem", "semname", "val", "idx")


class Prog:
    ENGS = ("pe", "act", "dve", "pool", "sp")
    EPOCH = int(os.environ.get('EPOCH', '8000'))
    DEPOCH = int(os.environ.get('DEPOCH', '500'))

    def __init__(self, nc):
        self.nc = nc
        self.ops = {e: [] for e in self.ENGS}
        self.lastw = {}
        self.readers = {}
        self.dma_last = {}

    @staticmethod
    def _cls(op):
        return op.dkey if op.dkey is not None else op.eng

    def add(self, eng, fn, r=(), w=(), dkey=None, ndma=1):
        op = Op()
        op.eng = eng
        op.fn = fn
        op.signal = dkey is not None
        op.dkey = dkey
        op.ndma = ndma
        op.sem = None
        op.val = 0
        op.idx = len(self.ops[eng])
        deps = {}

        def push(d):
            if d is None or d is op:
                return
            if d.dkey is None and dkey is None and d.eng == "pe" and eng == "pe":
                return
            c = self._cls(d)
            o = deps.get(c)
            if o is None or o.idx < d.idx:
                deps[c] = d

        for k in r:
            for d in self.lastw.get(k, {}).values():
                push(d)
            if isinstance(k, str) and k.startswith("pb"):
                for d in self.readers.get(k, {}).values():
                    if d.eng != eng:
                        push(d)
        for k in w:
            for d in self.lastw.get(k, {}).values():
                push(d)
            for d in self.readers.get(k, {}).values():
                push(d)
        op.deps = list(deps.values())
        for d in op.deps:
            d.signal = True
        me = self._cls(op)
        for k in r:
            self.readers.setdefault(k, {})[me] = op
        for k in w:
            self.lastw.setdefault(k, {})[me] = op
            self.readers[k] = {}
        self.ops[eng].append(op)
        if dkey is not None:
            self.dma_last[dkey] = op
        return op

    def emit(self, block_cm):
        nc = self.nc
        sems = {}
        stack = []

        def getsem(name):
            s = sems.get(name)
            if s is None:
                cm = nc.semaphore(name)
                s = cm.__enter__()
                stack.append(cm)
                sems[name] = s
            return s

        ecount = {e: 0 for e in self.ENGS}
        dcount = {}
        for e in self.ENGS:
            for op in self.ops[e]:
                if op.dkey is not None:
                    n = dcount.get(op.dkey, 0)
                    ep = n // self.DEPOCH
                    base = ep * self.DEPOCH
                    if (n - base) + op.ndma > self.DEPOCH:
                        n = base + self.DEPOCH
                        ep += 1
                        base = ep * self.DEPOCH
                    n += op.ndma
                    dcount[op.dkey] = n
                    op.semname = "d_%s_%d" % (op.dkey, ep)
                    op.sem = getsem(op.semname)
                    op.val = 16 * (n - base)
                elif op.signal:
                    n = ecount[e]
                    ep = n // self.EPOCH
                    ecount[e] = n + 1
                    op.semname = "e_%s_%d" % (e, ep)
                    op.sem = getsem(op.semname)
                    op.val = n - ep * self.EPOCH + 1
        finals = list(self.dma_last.values())

        with block_cm as block:
            def run(engname, eng):
                seen = {}

                def waits(deps):
                    for d in deps:
                        key = d.semname
                        if seen.get(key, 0) < d.val:
                            eng.wait_ge(d.sem, d.val)
                            seen[key] = d.val

                for op in self.ops[engname]:
                    waits(op.deps)
                    res = op.fn(eng)
                    if op.dkey is not None:
                        if not isinstance(res, (list, tuple)):
                            res = [res]
                        assert len(res) == op.ndma, (len(res), op.ndma)
                        for ins in res:
                            ins.then_inc(op.sem, 16)
                    elif op.signal:
                        res.then_inc(op.sem, 1)
                if engname == "sp":
                    waits(finals)

            @block.tensor
            def _(eng):
                run("pe", eng)

            @block.scalar
            def _(eng):
                run("act", eng)

            @block.vector
            def _(eng):
                run("dve", eng)

            @block.gpsimd
            def _(eng):
                run("pool", eng)

            @block.sync
            def _(eng):
                run("sp", eng)
        for cm in reversed(stack):
            cm.__exit__(None, None, None)


D = 1024
TT = 128
FFH = 2816
ALPHA = (2.0 * 4) ** 0.25
LN_EPS = 1e-5
A_EPS = 64e-5
NEG = -30000.0
ED = float(np.exp(-0.5))


class KB:
    def __init__(self, SEQ, DEPTH):
        self.SEQ = SEQ
        self.DEPTH = DEPTH
        self.NE = (DEPTH + 1) // 2
        self.NO = DEPTH // 2
        self.NG = SEQ // TT
        self.KEEP = min(512, SEQ)
        self.nc = bass.Bass("TRN2", target_bir_lowering=False)
        self.P = Prog(self.nc)
        self.stack = []
        self.tiles = {}
        self.uid = 0

    def sb(self, name, shape, dt=F32):
        cm = self.nc.sbuf_tensor(name, list(shape), dt)
        t = cm.__enter__()
        self.stack.append(cm)
        return t

    def din(self, name, shape):
        return self.nc.dram_tensor(name, list(shape), F32, kind="ExternalInput").ap()

    def dout(self, name, shape):
        return self.nc.dram_tensor(name, list(shape), F32, kind="ExternalOutput").ap()

    def bank(self):
        b = self.rot[self.rot_i % len(self.rot)]
        self.rot_i += 1
        return b

    def mm(self, out, lhsT, rhs, r, w, start=True, stop=True):
        self.P.add("pe", lambda e: e.matmul(out, lhsT, rhs, start=start, stop=stop), r=r, w=w)

    def tr(self, out, in_, idn, r, w):
        self.P.add("pe", lambda e: e.transpose(out, in_, idn), r=list(r) + ["const"], w=w)

    def act(self, out, in_, func, r, w, bias=None, scale=None):
        kw = {}
        if bias is not None:
            kw["bias"] = bias
        if scale is not None:
            kw["scale"] = scale
        self.P.add("act", lambda e: e.activation(out=out, in_=in_, func=func, **kw), r=r, w=w)

    def tt(self, eng, out, in0, in1, op, r, w):
        self.P.add(eng, lambda e: e.tensor_tensor(out=out, in0=in0, in1=in1, op=op), r=r, w=w)

    def stt(self, eng, out, in0, scalar, in1, op0, op1, r, w):
        self.P.add(eng, lambda e: e.scalar_tensor_tensor(out=out, in0=in0, scalar=scalar, in1=in1, op0=op0, op1=op1), r=r, w=w)

    def ts(self, eng, out, in0, s1, s2, op0, op1, r, w):
        self.P.add(eng, lambda e: e.tensor_scalar(out=out, in0=in0, scalar1=s1, scalar2=s2, op0=op0, op1=op1), r=r, w=w)

    def cp(self, eng, out, in_, r, w):
        self.P.add(eng, lambda e: e.tensor_copy(out=out, in_=in_), r=r, w=w)

    def ms(self, eng, ap, val, w):
        self.P.add(eng, lambda e: e.memset(ap, val), w=w)

    def rcp(self, out, in_, r, w):
        self.P.add("dve", lambda e: e.reciprocal(out=out, in_=in_), r=r, w=w)

    def dma(self, q, out, in_, r, w, dkey, slow=False):
        if slow:
            self.P.add(q, lambda e: e.dma_start(out=out, in_=in_, allow_slow_non_contiguous=True), r=r, w=w, dkey=dkey)
        else:
            self.P.add(q, lambda e: e.dma_start(out=out, in_=in_), r=r, w=w, dkey=dkey)

    def dense(self, wname, Wb, passes, rhs_fn, NT):
        for ps in passes:
            segs = ps["segs"]
            tot = sum(n for _, n in segs)
            groups = ps["groups"]
            banks = [self.bank() for _ in groups]
            nks = len(ps["kslabs"])
            for si, (row0, p, nk) in enumerate(ps["kslabs"]):
                slot = self.slab_i % len(self.slabs)
                self.slab_i += 1
                st = self.slabs[slot]
                skey = "slab%d" % slot
                view = st[0:p, 0:nk * tot].rearrange("p (k n) -> p k n", k=nk)
                off = 0
                outs = []
                for (c0, n) in segs:
                    outs.append((view[:, :, off:off + n], Wb[row0:row0 + nk * p, c0:c0 + n].rearrange("(k p) n -> p k n", p=p)))
                    off += n
                self.P.add("sp", lambda e, outs=outs: [e.dma_start(out=o, in_=i) for o, i in outs],
                           r=[("wb", wname)], w=[skey], dkey=skey, ndma=len(outs))
                for gi, (col, M, n) in enumerate(groups):
                    bt, bk = banks[gi]
                    for j in range(n):
                        for kc in range(nk):
                            rap, rkey = rhs_fn(si, kc)
                            self.mm(bt[0:M, j * NT:(j + 1) * NT], view[:, kc, col + j * M: col + (j + 1) * M], rap,
                                    r=[skey, rkey], w=[bk], start=(si == 0 and kc == 0), stop=(si == nks - 1 and kc == nk - 1))
            ps["evac"](banks)


def build(SEQ=8192, DEPTH=4):
    kb = KB(SEQ, DEPTH)
    nc, P = kb.nc, kb.P
    NE, NO, NG, KEEP = kb.NE, kb.NO, kb.NG, kb.KEEP
    NT = TT
    I = {}
    I["x_prompt"] = kb.din("x_prompt", [SEQ, D])
    I["x_sample"] = kb.din("x_sample", [128, D])
    I["state_a_wkv"] = kb.din("state_a_wkv", [NE, 2, 8, 64, 64])
    I["state_a_shift"] = kb.din("state_a_shift", [NE, 2, 1792])
    I["state_b_gla"] = kb.din("state_b_gla", [NE, 2, 4, 64, 128])
    I["cache_c_k"] = kb.din("cache_c_k", [max(NO, 1), 2, 16, 512, 64])
    I["cache_c_v"] = kb.din("cache_c_v", [max(NO, 1), 2, 16, 512, 64])
    wshapes = dict(w_in_mix=[NE, D, 3344], w_out_mix=[NE, D, D], c_w_qkv=[max(NO, 1), D, 3 * D], c_w_o=[max(NO, 1), D, D],
                   ffn_w_in=[DEPTH, D, 2 * FFH], ffn_w_out=[DEPTH, FFH, D])
    for k_, s_ in wshapes.items():
        I[k_] = kb.din(k_, s_)
    small = dict(a_mu=[NE, 1792], a_w0=[NE, 512], a_w2=[NE, 64, 512], a_a0=[NE, 512], a_a2=[NE, 64, 512], a_g2=[NE, 128, 512],
                 a_k_k=[NE, 512], a_k_a=[NE, 512], a_r_k=[NE, 8, 64], a_ln_w=[NE, 512], a_ln_b=[NE, 512],
                 b_alpha_up=[NE, 16, 256], b_alpha_bias=[NE, 256], b_norm_w=[NE, 128], c_rel_bias=[max(NO, 1), 16, 257],
                 ln1_w=[DEPTH, D], ln1_b=[DEPTH, D], ln2_w=[DEPTH, D], ln2_b=[DEPTH, D])
    for k_, s_ in small.items():
        I[k_] = kb.din(k_, s_)
    O = {}
    O["y_p"] = kb.dout("y_p", [SEQ, D])
    O["y_s"] = kb.dout("y_s", [128, D])
    O["p_wkv"] = kb.dout("p_wkv", [NE, 8, 64, 64])
    O["p_shift"] = kb.dout("p_shift", [NE, 1792])
    O["p_gla"] = kb.dout("p_gla", [NE, 4, 64, 128])
    O["p_k"] = kb.dout("p_k", [max(NO, 1), 16, KEEP, 64])
    O["p_v"] = kb.dout("p_v", [max(NO, 1), 16, KEEP, 64])
    O["s_wkv"] = kb.dout("s_wkv", [NE, 2, 8, 64, 64])
    O["s_shift"] = kb.dout("s_shift", [NE, 2, 1792])
    O["s_gla"] = kb.dout("s_gla", [NE, 2, 4, 64, 128])
    O["s_k"] = kb.dout("s_k", [max(NO, 1), 2, 16, 64, 64])
    O["s_v"] = kb.dout("s_v", [max(NO, 1), 2, 16, 64, 64])
    WB = {k_: nc.dram_tensor(k_ + "_bf", list(s_), BF16).ap() for k_, s_ in wshapes.items()}
    EXT = nc.dram_tensor("relext", [16, 512], F32).ap()
    REP_T = nc.dram_tensor("relrep", [16, 128, 512], F32)
    REP = REP_T.ap()

    pbanks = []
    for i in range(8):
        cm = nc.psum_tensor("pb%d" % i, [128, 512], F32)
        t = cm.__enter__()
        kb.stack.append(cm)
        pbanks.append((t, "pb%d" % i))
    fixed = pbanks[0:3]
    kb.rot = pbanks[3:8]
    kb.rot_i = 0
    sb = kb.sb
    kb.slabs = [sb("slab%d" % i, [128, 4096], BF16) for i in range(2)]
    kb.slab_i = 0
    idn = sb("idn", [128, 128]); idnb = sb("idnb", [128, 128], BF16)
    ones = sb("ones", [128, 128]); onesb = sb("onesb", [128, 128], BF16)
    SELb = sb("SELb", [64, 64, 64], BF16)
    X = sb("X", [128, 8, NT]); Xb = sb("Xb", [128, 8, NT], BF16); Z = sb("Z", [128, 8, NT])
    XIO = sb("XIO", [128, D])
    MEAN = sb("MEAN", [128, NT]); MSQ = sb("MSQ", [128, NT]); VAR = sb("VAR", [128, NT]); RSTD = MSQ
    Hh = sb("Hh", [128, 22, NT], BF16); SGT = sb("SGT", [128, 2, NT])
    ZB = Hh[:, 0:8, :]; ZQ = Hh[:, 8:16, :]
    LNW = sb("LNW", [128, DEPTH, 4, 8])
    ARN = sb("ARN", [128, 6144])
    if NE:
        RKV = ARN[0:64, 0:3072].rearrange("p (c n) -> p c n", c=24); DRKV = ARN[0:64, 3072:6144].rearrange("p (c n) -> p c n", c=24)
        XWA = sb("XWA", [64, 2, NT]); DXWA = sb("DXWA", [64, 2, NT])
        XG = sb("XG", [128, 1, NT]); DXG = sb("DXG", [128, 1, NT])
        BQK = sb("BQK", [64, 8, NT]); BV = sb("BV", [128, 4, NT]); BXA = sb("BXA", [16, 1, NT]); BRG = sb("BRG", [128, 4, NT])
        TH = sb("TH", [64, NT]); SGX = sb("SGX", [128, NT])
        WD = sb("WD", [64, 8, NT]); AA = sb("AA", [64, 8, NT]); GG = sb("GG", [64, 8, NT]); KK = sb("KK", [64, 8, NT])
        NB = sb("NB", [64, 8, NT])
        T1 = DRKV[:, 0:8, :]; T2 = DRKV[:, 8:16, :]; BON = DRKV[:, 16:24, :]
        OT = AA
        AL = sb("AL", [64, 4, NT]); OG = sb("OG", [128, 4, NT]); T3 = BV
        YAb = sb("YAb", [64, 8, NT], BF16); YBb = sb("YBb", [128, 4, NT], BF16)
        VTOKb = sb("VTOKb", [64, 512], BF16); GVTOKb = sb("GVTOKb", [64, 512], BF16)
        Hb = sb("Hb", [64, 512], BF16); Sb = sb("Sb", [64, 512], BF16); TMPb = sb("TMPb", [64, 512], BF16)
        Rb = sb("Rb", [64, 8, NT], BF16); QSb = sb("QSb", [64, 4, NT], BF16)
        RT = [ARN[0:64, 3072 + i * 512:3072 + (i + 1) * 512] for i in range(4)]
        GT = [sb("GT%d" % i, [64, 512]) for i in range(2)]
        STG = XIO[0:64, 0:512]
        HW = [[sb("HW%d_%d" % (e, s), [64, 512]) for s in range(2)] for e in range(NE)]
        SG = [[sb("SG%d_%d" % (e, s), [64, 512]) for s in range(2)] for e in range(NE)]
        SH1 = [[sb("SH1_%d_%d" % (e, s), [64, 24, 1]) for s in range(2)] for e in range(NE)]
        SH2 = [[sb("SH2_%d_%d" % (e, s), [64, 2, 1]) for s in range(2)] for e in range(NE)]
        SH3 = [[sb("SH3_%d_%d" % (e, s), [128, 1, 1]) for s in range(2)] for e in range(NE)]
        EPW = dict(w2=sb("w2s", [64, 512]), a2=sb("a2s", [64, 512]), g2=sb("g2s", [128, 512]), au=sb("aus", [16, 256]))
        EP = []
        for e in range(NE):
            EP.append(dict(mu1=sb("mu1_%d" % e, [64, 24, 1]), mu2=sb("mu2_%d" % e, [64, 2, 1]), mu3=sb("mu3_%d" % e, [128, 1, 1]),
                           w0=sb("w0_%d" % e, [64, 8]), a0=sb("a0_%d" % e, [64, 8]), kk=sb("kk_%d" % e, [64, 8, 1]), ka=sb("ka_%d" % e, [64, 8, 1]),
                           rk=sb("rk_%d" % e, [64, 8, 1]), lw=sb("lw_%d" % e, [64, 8, 1]), lb=sb("lb_%d" % e, [64, 8, 1]),
                           ab=sb("ab_%d" % e, [64, 4]), nw=sb("nw_%d" % e, [128, 1])))
    if NO:
        QTe = sb("QTe", [128, 8, NT], BF16); QTo = sb("QTo", [128, 8, NT], BF16)
        KTf = ARN[:, 0:1024].rearrange("p (c n) -> p c n", c=8); VT = ARN[:, 1024:2048].rearrange("p (c n) -> p c n", c=8)
        KR = [sb("KR%d" % o, [128, 8, 5 * 128], BF16) for o in range(NO)]
        VR = [sb("VR%d" % o, [128, 5, 16, 65], BF16) for o in range(NO)]
        M3s = sb("M3s", [128, 16, 128], BF16); M4s = sb("M4s", [128, 16, 128], BF16)
        M3 = [M3s for o in range(NO)]
        M4 = [M4s for o in range(NO)]
        MSD = nc.dram_tensor("msd", [NO, 2, 128, 16, 128], BF16).ap()
        CH = [sb("CH_%d" % o, [128, 16]) for o in range(NO)]
        M0 = sb("M0", [128, 128], BF16); MLOW = sb("MLOW", [128, 128])
        PT = [sb("PT%d" % i, [128, 5, 128], BF16) for i in range(2)]
        OTOK = ARN[:, 2048:3072].rearrange("p (h d) -> p h d", h=16); RDEN = sb("RDEN", [128, 16, 1]); OTb = sb("OTb", [128, 8, NT], BF16)
        XIO3 = XIO[:, :].rearrange("p (h d) -> p h d", h=16)

    if os.environ.get('KTRACE'):
        print('SBUF_REMAINING_AFTER_ALLOC', nc.sbuf_bytes_remaining)
    kb.ms("pool", idn[:], 0.0, ["const"])
    P.add("pool", lambda e: e.affine_select(out=idn[:], in_=idn[:], pattern=[[-1, 128]], compare_op=ALU.not_equal, fill=1.0, base=0, channel_multiplier=1), r=["const"], w=["const"])
    kb.ms("pool", ones[:], 1.0, ["const"])
    kb.ms("pool", onesb[:], 1.0, ["const"])
    kb.cp("dve", idnb[:], idn[:], ["const"], ["const"])
    kb.cp("dve", SELb[:], idn[0:64, 0:64].unsqueeze(2).to_broadcast([64, 64, 64]), ["const"], ["const"])
    for l in range(DEPTH):
        for j, nm in enumerate(["ln1_w", "ln1_b", "ln2_w", "ln2_b"]):
            kb.dma("pool", LNW[:, l, j, :], I[nm][l].rearrange("(c p) -> p c", p=128), [], ["const"], "setup", slow=True)
    for e in range(NE):
        ep = EP[e]
        sl = lambda ap, p: ap.rearrange("(c p) -> p c", p=p)
        kb.dma("pool", ep["mu1"][:, :, 0], sl(I["a_mu"][e, 0:1536], 64), [], ["const"], "setup", slow=True)
        kb.dma("pool", ep["mu2"][:, :, 0], sl(I["a_mu"][e, 1536:1664], 64), [], ["const"], "setup", slow=True)
        kb.dma("pool", ep["mu3"][:, :, 0], sl(I["a_mu"][e, 1664:1792], 128), [], ["const"], "setup", slow=True)
        kb.dma("pool", ep["w0"][:, :], sl(I["a_w0"][e], 64), [], ["const"], "setup", slow=True)
        kb.dma("pool", ep["a0"][:, :], sl(I["a_a0"][e], 64), [], ["const"], "setup", slow=True)
        kb.dma("pool", ep["kk"][:, :, 0], sl(I["a_k_k"][e], 64), [], ["const"], "setup", slow=True)
        kb.dma("pool", ep["ka"][:, :, 0], sl(I["a_k_a"][e], 64), [], ["const"], "setup", slow=True)
        kb.dma("pool", ep["rk"][:, :, 0], I["a_r_k"][e].rearrange("h p -> p h"), [], ["const"], "setup", slow=True)
        kb.dma("pool", ep["lw"][:, :, 0], sl(I["a_ln_w"][e], 64), [], ["const"], "setup", slow=True)
        kb.dma("pool", ep["lb"][:, :, 0], sl(I["a_ln_b"][e], 64), [], ["const"], "setup", slow=True)
        kb.dma("pool", ep["ab"][:, :], sl(I["b_alpha_bias"][e], 64), [], ["const"], "setup", slow=True)
        kb.dma("pool", ep["nw"][:, :], sl(I["b_norm_w"][e], 128), [], ["const"], "setup", slow=True)
        kb.act(ep["ab"][:, :], ep["ab"][:, :], AF.Copy, ["const"], ["const"], scale=-1.0)
        for s in range(2):
            for t_ in (HW[e][s], SG[e][s]):
                kb.ms("pool", t_[:], 0.0, [t_.name])
            for t_ in (SH1[e][s], SH2[e][s], SH3[e][s]):
                kb.ms("pool", t_[:], 0.0, [t_.name])
    def cast_w(name, l, rows):
        for r0 in range(0, rows, 128):
            r1 = min(rows, r0 + 128)
            kb.dma("pool", WB[name][l, r0:r1, :], I[name][l, r0:r1, :], [], [("wb", name + str(l))], "cast_" + name + str(l))
    for l in range(DEPTH if not os.environ.get('NOCAST') else 0):
        if l % 2 == 0:
            cast_w("w_in_mix", l // 2, D); cast_w("w_out_mix", l // 2, D)
        else:
            cast_w("c_w_qkv", l // 2, D); cast_w("c_w_o", l // 2, D)
        cast_w("ffn_w_in", l, D); cast_w("ffn_w_out", l, FFH)
    if NO:
        kb.ms("pool", MLOW[:], 1.0, ["const"])
        P.add("pool", lambda e: e.affine_select(out=MLOW[:], in_=MLOW[:], pattern=[[-1, 128]], compare_op=ALU.is_gt, fill=0.0, base=0, channel_multiplier=1), r=["const"], w=["const"])
        kb.ms("pool", M0[:], 0.0, ["const"])
        kb.ms("pool", M0[0:64, 64:128], NEG, ["const"])
        TS8 = XIO[:, :].rearrange("p (h i) -> p h i", h=8)
        for o in range(NO):
            kb.ms("dve", XIO[0:16, 0:512], 0.0, ["XIO"])
            kb.dma("pool", EXT[:, :], XIO[0:16, 0:512], ["XIO"], ["ext"], "setup")
            kb.dma("pool", EXT[:, 0:257], I["c_rel_bias"][o], ["ext"], ["ext"], "setup")
            kb.dma("pool", REP, EXT.unsqueeze(1).to_broadcast([16, 128, 512]), ["ext"], ["rep"], "setup")
            kb.dma("pool", CH[o][:, :], I["c_rel_bias"][o][:, 256].partition_broadcast(128), [], ["const"], "setup", slow=True)
            for (MM, off) in ((M3[o], 256), (M4[o], 128)):
                for hh in range(2):
                    src = bass.AP(REP_T, off + hh * 8 * 65536, [[511, 128], [65536, 8], [1, 128]])
                    kb.dma("pool", TS8, src, ["rep"], ["XIO"], "setup")
                    kb.tt("dve", TS8, TS8, CH[o][:, hh * 8:hh * 8 + 8].unsqueeze(2).to_broadcast([128, 8, 128]), ALU.subtract, ["XIO", "const"], ["XIO"])
                    if off == 256:
                        kb.tt("dve", MM[:, hh * 8:hh * 8 + 8, :], TS8, MLOW[:].unsqueeze(1).to_broadcast([128, 8, 128]), ALU.mult, ["XIO", "const"], ["MB"])
                    else:
                        kb.cp("dve", MM[:, hh * 8:hh * 8 + 8, :], TS8, ["XIO"], ["MB"])
                if off == 128:
                    kb.ms("dve", MM[64:128, :, 0:64], NEG, ["MB"])
                kb.dma("pool", MSD[o, 0 if off == 256 else 1], MM[:], ["MB"], [("msd", o)], "stm")
            kb.ms("pool", VR[o][:, :, :, 64:65], 1.0, ["VR%d" % o])
            kb.ms("pool", QTe[:], 0.0, ["QTe"])
            kb.ms("pool", QTo[:], 0.0, ["QTo"])

    DBG = os.environ.get('DBG')
    dbg_on = [False]

    def dbg(name, ap, key, shape, dt=F32):
        if not (DBG and dbg_on[0]):
            return
        d = nc.dram_tensor("dbg_" + name, list(shape), dt, kind="ExternalOutput").ap()
        kb.dma("pool", d, ap, [key], [], "dbg_" + name)
    ev_flip = [0]

    def evac_copy(out, in_, r, w):
        ev_flip[0] ^= 1
        if ev_flip[0]:
            kb.act(out, in_, AF.Copy, r, w)
        else:
            kb.cp("dve", out, in_, r, w)

    def layer_norm(l, which):
        dbg("z%d_%d" % (l, which), Z[:], "Z", [128, 8, NT])
        kb.act(ZB, Z[:], AF.Copy, ["Z"], ["Hh"])
        kb.tt("dve", ZQ, Z[:], Z[:], ALU.mult, ["Z"], ["Hh"])
        bt, bk = kb.bank()
        for kc in range(8):
            kb.mm(bt[:, 0:NT], onesb[:], ZB[:, kc, :], ["const", "Hh"], [bk], start=(kc == 0), stop=(kc == 7))
        for kc in range(8):
            kb.mm(bt[:, NT:2 * NT], onesb[:], ZQ[:, kc, :], ["const", "Hh"], [bk], start=(kc == 0), stop=(kc == 7))
        kb.act(MEAN[:], bt[:, 0:NT], AF.Copy, [bk], ["MEAN"], scale=1.0 / D)
        kb.tt("dve", MSQ[:], MEAN[:], MEAN[:], ALU.mult, ["MEAN"], ["MSQ"])
        kb.stt("dve", VAR[:], bt[:, NT:2 * NT], 1.0 / D, MSQ[:], ALU.mult, ALU.subtract, [bk, "MSQ"], ["VAR"])
        kb.act(VAR[:], VAR[:], AF.Sqrt, ["VAR"], ["VAR"], bias=LN_EPS)
        kb.rcp(RSTD[:], VAR[:], ["VAR"], ["MSQ"])
        kb.tt("dve", Z[:], Z[:], MEAN[:].unsqueeze(1).to_broadcast([128, 8, NT]), ALU.subtract, ["Z", "MEAN"], ["Z"])
        kb.tt("dve", Z[:], Z[:], RSTD[:].unsqueeze(1).to_broadcast([128, 8, NT]), ALU.mult, ["Z", "MSQ"], ["Z"])
        kb.tt("dve", Z[:], Z[:], LNW[:, l, 2 * which, :].unsqueeze(2).to_broadcast([128, 8, NT]), ALU.mult, ["Z", "const"], ["Z"])
        kb.tt("dve", X[:], Z[:], LNW[:, l, 2 * which + 1, :].unsqueeze(2).to_broadcast([128, 8, NT]), ALU.add, ["Z", "const"], ["X"])
        kb.act(Xb[:], X[:], AF.Copy, ["X"], ["Xb"])
        dbg("x%d_%d" % (l, which), X[:], "X", [128, 8, NT])

    def resid_evac(pi):
        def f(banks):
            per = 4 // len(banks)
            for bi, (bt, bk) in enumerate(banks):
                c0 = 4 * pi + bi * per
                kb.stt("dve", Z[:, c0:c0 + per, :], X[:, c0:c0 + per, :], ALPHA,
                       bt[:, 0:per * NT].rearrange("p (c n) -> p c n", c=per), ALU.mult, ALU.add, ["X", bk], ["Z"])
        return f

    G4 = [(0, 128, 1), (128, 128, 1), (256, 128, 1), (384, 128, 1)]

    def ffn(l):
        def ev(j, n):
            def f(banks):
                (g, gk), (u, uk) = banks
                kb.act(SGT[:, 0:n, :], g[:, 0:n * NT].rearrange("p (c n) -> p c n", c=n), AF.Silu, [gk], ["SGT"])
                kb.tt("dve", Hh[:, 2 * j:2 * j + n, :], SGT[:, 0:n, :], u[:, 0:n * NT].rearrange("p (c n) -> p c n", c=n), ALU.mult, ["SGT", uk], ["Hh"])
            return f
        passes = []
        for j in range(11):
            passes.append(dict(segs=[(j * 256, 256), (FFH + j * 256, 256)], kslabs=[(0, 128, 8)], groups=[(0, 128, 2), (256, 128, 2)], evac=ev(j, 2)))
        kb.dense("ffn_w_in%d" % l, WB["ffn_w_in"][l], passes, lambda si, kc: (Xb[:, kc, :], "Xb"), NT)
        passes = [dict(segs=[(p * 512, 512)], kslabs=[(0, 128, 8), (1024, 128, 8), (2048, 128, 6)], groups=G4, evac=resid_evac(p)) for p in range(2)]
        kb.dense("ffn_w_out%d" % l, WB["ffn_w_out"][l], passes, lambda si, kc: (Hh[:, si * 8 + kc, :], "Hh"), NT)
        layer_norm(l, 1)

    def even_layer(l, segs):
        e = l // 2
        ep = dict(EP[e])
        ep.update(EPW)
        P.add("sp", lambda en: [en.dma_start(out=EPW["w2"][:, :], in_=I["a_w2"][e]), en.dma_start(out=EPW["a2"][:, :], in_=I["a_a2"][e]),
                                en.dma_start(out=EPW["g2"][:, :], in_=I["a_g2"][e]), en.dma_start(out=EPW["au"][:, :], in_=I["b_alpha_up"][e])],
              r=[], w=["EPW"], dkey="ldep", ndma=4)
        hb = lambda ap: ap.to_broadcast([64, 8, NT])

        def gcopy(dst, key, i0, n, M):
            def f(banks, gi):
                bt, bk = banks[gi]
                evac_copy(dst[0:M, i0:i0 + n, :], bt[0:M, 0:n * NT].rearrange("p (c n) -> p c n", c=n), [bk], [key])
            return f

        def mk(segs_, groups, fs):
            def ev(banks):
                for gi, f in enumerate(fs):
                    f(banks, gi)
            return dict(segs=segs_, kslabs=[(0, 128, 8)], groups=groups, evac=ev)
        passes = []
        for s_ in range(3):
            passes.append(mk([(s_ * 512, 512)], [(0, 64, 4), (256, 64, 4)], [gcopy(RKV, "RKV", s_ * 8, 4, 64), gcopy(RKV, "RKV", s_ * 8 + 4, 4, 64)]))
        passes.append(mk([(1536, 512)], [(0, 64, 2), (128, 128, 1), (256, 64, 4)], [gcopy(XWA, "XWA", 0, 2, 64), gcopy(XG, "XG", 0, 1, 128), gcopy(BQK, "BQK", 0, 4, 64)]))
        passes.append(mk([(2048, 512)], [(0, 64, 4), (256, 128, 2)], [gcopy(BQK, "BQK", 4, 4, 64), gcopy(BV, "BV", 0, 2, 128)]))
        passes.append(mk([(2560, 272)], [(0, 128, 2), (256, 16, 1)], [gcopy(BV, "BV", 2, 2, 128), gcopy(BXA, "BXA", 0, 1, 16)]))
        passes.append(mk([(2832, 512)], [(0, 128, 4)], [gcopy(BRG, "BRG", 0, 4, 128)]))
        kb.dense("w_in_mix%d" % e, WB["w_in_mix"][e], passes, lambda si, kc: (Xb[:, kc, :], "Xb"), NT)
        for (T_, Dt, MU, SHL, Pn, C, tk, dk) in ((RKV, DRKV, ep["mu1"], SH1[e], 64, 24, "RKV", "DRKV"), (XWA, DXWA, ep["mu2"], SH2[e], 64, 2, "XWA", "DXWA"),
                                                (XG, DXG, ep["mu3"], SH3[e], 128, 1, "XG", "DXG")):
            for (slot, t0, t1) in segs:
                SH = SHL[slot]
                kb.tt("dve", Dt[:, :, t0 + 1:t1], T_[:, :, t0:t1 - 1], T_[:, :, t0 + 1:t1], ALU.subtract, [tk], [dk])
                kb.tt("dve", Dt[:, :, t0:t0 + 1], SH[:], T_[:, :, t0:t0 + 1], ALU.subtract, [tk, SH.name], [dk])
                kb.cp("dve", SH[:], T_[:, :, t1 - 1:t1], [tk], [SH.name])
            kb.tt("dve", Dt[:], Dt[:], MU[:].to_broadcast([Pn, C, NT]), ALU.mult, [dk, "const"], [dk])
            kb.tt("dve", T_[:], T_[:], Dt[:], ALU.add, [tk, dk], [tk])
        R_ = RKV[:, 0:8, :]; K_ = RKV[:, 8:16, :]; V_ = RKV[:, 16:24, :]
        kb.act(TH[:], XWA[:, 0, :], AF.Tanh, ["XWA"], ["TH"])
        kb.act(SGX[:], XG[:, 0, :], AF.Sigmoid, ["XG"], ["SGX"])
        for (wt, rhs, rk_, dst, dk_, bias, fn) in ((ep["w2"], TH[:], "TH", WD, "WD", ep["w0"], AF.Sigmoid), (ep["a2"], XWA[:, 1, :], "XWA", AA, "AA", ep["a0"], AF.Sigmoid),
                                                  (ep["g2"], SGX[:], "SGX", GG, "GG", None, AF.Copy)):
            for half in range(2):
                bt, bk = kb.bank()
                for j in range(4):
                    h = half * 4 + j
                    kb.mm(bt[0:64, j * NT:(j + 1) * NT], wt[:, h * 64:(h + 1) * 64], rhs, ["EPW", rk_], [bk])
                for j in range(4):
                    h = half * 4 + j
                    if bias is not None:
                        kb.act(dst[:, h, :], bt[0:64, j * NT:(j + 1) * NT], fn, [bk, "const"], [dk_], bias=bias[:, h:h + 1])
                    else:
                        kb.act(dst[:, h, :], bt[0:64, j * NT:(j + 1) * NT], fn, [bk], [dk_])
        kb.act(WD[:], WD[:], AF.Exp, ["WD"], ["WD"], scale=-ED)
        kb.tt("dve", KK[:], K_, hb(ep["kk"][:]), ALU.mult, ["RKV", "const"], ["KK"])
        kb.tt("dve", T1, KK[:], KK[:], ALU.mult, ["KK"], ["DRKV"])
        for half in range(2):
            bt, bk = kb.bank()
            kb.mm(bt[0:64, :], ones[0:64, 0:64], T1[:, half * 4:half * 4 + 4, :], ["const", "DRKV"], [bk])
            kb.act(T2[:, half * 4:half * 4 + 4, :], bt[0:64, :].rearrange("p (c n) -> p c n", c=4), AF.Sqrt, [bk], ["DRKV"], bias=1e-12)
        kb.rcp(T2, T2, ["DRKV"], ["DRKV"])
        kb.tt("dve", KK[:], KK[:], T2, ALU.mult, ["KK", "DRKV"], ["KK"])
        kb.stt("dve", NB[:], KK[:], -1.0, AA[:], ALU.mult, ALU.mult, ["KK", "AA"], ["NB"])
        kb.stt("dve", T1, AA[:], -1.0, hb(ep["ka"][:]), ALU.add, ALU.mult, ["AA", "const"], ["DRKV"])
        kb.stt("dve", K_, T1, 1.0, K_, ALU.add, ALU.mult, ["DRKV", "RKV"], ["RKV"])
        kb.tt("dve", T1, R_, K_, ALU.mult, ["RKV"], ["DRKV"])
        kb.tt("dve", T1, T1, hb(ep["rk"][:]), ALU.mult, ["DRKV", "const"], ["DRKV"])
        for half in range(2):
            bt, bk = kb.bank()
            kb.mm(bt[0:64, :], ones[0:64, 0:64], T1[:, half * 4:half * 4 + 4, :], ["const", "DRKV"], [bk])
            kb.tt("dve", BON[:, half * 4:half * 4 + 4, :], bt[0:64, :].rearrange("p (c n) -> p c n", c=4), V_[:, half * 4:half * 4 + 4, :], ALU.mult, [bk, "RKV"], ["DRKV"])
        bt, bk = kb.bank()
        for h in range(4):
            kb.mm(bt[0:64, h * NT:(h + 1) * NT], ep["au"][:, h * 64:(h + 1) * 64], BXA[:, 0, :], ["EPW", "BXA"], [bk])
        for h in range(4):
            kb.act(AL[:, h, :], bt[0:64, h * NT:(h + 1) * NT], AF.Exp, [bk, "const"], ["AL"], bias=ep["ab"][:, h:h + 1], scale=-1.0)
        kb.act(AL[:], AL[:], AF.Ln, ["AL"], ["AL"], bias=1.0)
        kb.act(AL[:], AL[:], AF.Exp, ["AL"], ["AL"], scale=-1.0 / 16.0)
        kb.act(QSb[:], BQK[:, 0:4, :], AF.Copy, ["BQK"], ["QSb"], scale=0.125)
        kb.act(Rb[:], R_, AF.Copy, ["RKV"], ["Rb"])
        P.add("dve", lambda en: en.memset(RT[0][:, 0:1], 0.0), r=[], w=["DRKV", "RT0", "RT1", "RT2", "RT3"])
        obr, obrk = fixed[0]
        obg, obgk = fixed[1]
        for (slot, t0, t1) in segs:
            Hs = HW[e][slot]; Ss = SG[e][slot]
            hk = Hs.name; sk = Ss.name
            H3 = Hs[:].rearrange("p (h v) -> p h v", h=8)
            S3 = Ss[:].rearrange("p (h v) -> p h v", h=4)
            for c0 in range(t0, t1, 64):
                bt, bk = kb.bank()
                for h in range(8):
                    kb.tr(bt[0:64, h * 64:(h + 1) * 64], V_[:, h, c0:c0 + 64], idn[0:64, 0:64], ["RKV"], [bk])
                kb.act(VTOKb[:], bt[0:64, :], AF.Copy, [bk], ["VTOKb"])
                bt, bk = kb.bank()
                for h in range(4):
                    kb.tr(bt[0:64, h * 128:(h + 1) * 128], BV[:, h, c0:c0 + 64], idn[:, :], ["BV"], [bk])
                kb.cp("dve", GVTOKb[:], bt[0:64, :], [bk], ["GVTOKb"])
                v8 = lambda ap: ap.rearrange("p (h v) -> p h v", h=8)
                v4g = lambda ap: ap.rearrange("p (h v) -> p h v", h=4)

                def matvecs(tl_):
                    t_ = c0 + tl_
                    for h in range(8):
                        kb.mm(obr[0:64, h * 64 + tl_:h * 64 + tl_ + 1], Hb[:, h * 64:(h + 1) * 64], Rb[:, h, t_:t_ + 1], ["Hb", "Rb"], [obrk])

                def matvecs_g(tl_):
                    t_ = c0 + tl_
                    for h in range(4):
                        kb.mm(obg[:, h * 64 + tl_:h * 64 + tl_ + 1], Sb[:, h * 128:(h + 1) * 128], QSb[:, h, t_:t_ + 1], ["Sb", "QSb"], [obgk])

                def vbg(tl_):
                    vb_, vbk_ = kb.bank()
                    kb.mm(vb_[0:64, :], SELb[:, tl_, :], VTOKb[:], ["const", "VTOKb"], [vbk_])
                    vg_, vgk_ = kb.bank()
                    kb.mm(vg_[0:64, :], SELb[:, tl_, :], GVTOKb[:], ["const", "GVTOKb"], [vgk_])
                    return vb_, vbk_, vg_, vgk_

                nxt = vbg(0) if not os.environ.get('SKIP_REC') else None
                for tl in range(64 if not os.environ.get('SKIP_REC') else 0):
                    t = c0 + tl
                    col = lambda T_: T_[:, :, t:t + 1]
                    vb, vbk, vg, vgk = nxt
                    kb.tt("dve", v8(TMPb[:]), H3, col(KK).to_broadcast([64, 8, 64]), ALU.mult, [hk, "KK"], ["TMPb"])
                    sa, sak = kb.bank()
                    kb.mm(sa[0:64, :], onesb[0:64, 0:64], TMPb[:], ["const", "TMPb"], [sak])
                    if tl > 0:
                        matvecs(tl - 1)
                    kb.tt("dve", v8(RT[3][:]), v8(vb[0:64, :]), col(K_).to_broadcast([64, 8, 64]), ALU.mult, [vbk, "RKV"], ["RT3"])
                    kb.tt("dve", v4g(GT[1][:]), v4g(vg[0:64, :]), BQK[:, 4:8, t:t + 1].to_broadcast([64, 4, 128]), ALU.mult, [vgk, "BQK"], ["GT1"])
                    kb.tt("pool", v8(RT[1][:]), H3, col(WD).to_broadcast([64, 8, 64]), ALU.mult, [hk, "WD"], ["RT1"])
                    kb.tt("pool", v4g(GT[0][:]), S3, col(AL).to_broadcast([64, 4, 128]), ALU.mult, [sk, "AL"], ["GT0"])
                    if tl > 0:
                        matvecs_g(tl - 1)
                    kb.tt("pool", Ss[:], GT[0][:], GT[1][:], ALU.add, ["GT0", "GT1"], [sk])
                    kb.act(Sb[:], Ss[:], AF.Copy, [sk], ["Sb"])
                    if tl < 63:
                        nxt = vbg(tl + 1)
                    kb.tt("dve", RT[3][:], RT[1][:], RT[3][:], ALU.add, ["RT1", "RT3"], ["RT3"])
                    kb.tt("dve", v8(RT[2][:]), v8(sa[0:64, :]), col(NB).to_broadcast([64, 8, 64]), ALU.mult, [sak, "NB"], ["RT2"])
                    kb.tt("dve", Hs[:], RT[3][:], RT[2][:], ALU.add, ["RT3", "RT2"], [hk])
                    kb.act(Hb[:], Hs[:], AF.Copy, [hk], ["Hb"])
                if not os.environ.get('SKIP_REC'):
                    matvecs(63)
                    matvecs_g(63)
                kb.act(OT[:, :, c0:c0 + 64], obr[0:64, :].rearrange("p (h t) -> p h t", h=8), AF.Copy, [obrk], ["AA"])
                kb.cp("dve", OG[:, :, c0:c0 + 64], obg[:, 0:256].rearrange("p (h t) -> p h t", h=4), [obgk], ["OG"])
        P.add("dve", lambda en: en.tensor_tensor(out=T1, in0=OT[:], in1=OT[:], op=ALU.mult), r=["AA"], w=["DRKV", "RT0", "RT1", "RT2", "RT3"])
        for half in range(2):
            hs = slice(half * 4, half * 4 + 4)
            b1, b1k = kb.bank()
            kb.mm(b1[0:64, :], ones[0:64, 0:64], OT[:, hs, :], ["const", "AA"], [b1k])
            b2, b2k = kb.bank()
            kb.mm(b2[0:64, :], ones[0:64, 0:64], T1[:, hs, :], ["const", "DRKV"], [b2k])
            v4 = lambda b: b[0:64, :].rearrange("p (c n) -> p c n", c=4)
            kb.act(T2[:, hs, :], v4(b1), AF.Copy, [b1k], ["DRKV"], scale=1.0 / 64)
            kb.stt("dve", OT[:, hs, :], v4(b1), -1.0 / 64, OT[:, hs, :], ALU.mult, ALU.add, [b1k, "AA"], ["AA"])
            kb.tt("dve", T2[:, hs, :], T2[:, hs, :], T2[:, hs, :], ALU.mult, ["DRKV"], ["DRKV"])
            kb.stt("dve", T2[:, hs, :], v4(b2), 1.0 / 64, T2[:, hs, :], ALU.mult, ALU.subtract, [b2k, "DRKV"], ["DRKV"])
        kb.act(T2, T2, AF.Sqrt, ["DRKV"], ["DRKV"], bias=A_EPS)
        kb.rcp(T2, T2, ["DRKV"], ["DRKV"])
        kb.tt("dve", OT[:], OT[:], T2, ALU.mult, ["AA", "DRKV"], ["AA"])
        kb.tt("dve", OT[:], OT[:], hb(ep["lw"][:]), ALU.mult, ["AA", "const"], ["AA"])
        kb.tt("dve", OT[:], OT[:], hb(ep["lb"][:]), ALU.add, ["AA", "const"], ["AA"])
        kb.tt("dve", OT[:], OT[:], BON, ALU.add, ["AA", "DRKV"], ["AA"])
        kb.tt("dve", YAb[:], OT[:], GG[:], ALU.mult, ["AA", "GG"], ["YAb"])
        kb.tt("dve", T3[:], OG[:], OG[:], ALU.mult, ["OG"], ["BV"])
        bt, bk = kb.bank()
        kb.mm(bt[:, :], ones[:, :], T3[:], ["const", "BV"], [bk])
        kb.act(T3[:], bt[:, :].rearrange("p (c n) -> p c n", c=4), AF.Sqrt, [bk], ["BV"], bias=LN_EPS, scale=1.0 / 128)
        kb.rcp(T3[:], T3[:], ["BV"], ["BV"])
        kb.tt("dve", OG[:], OG[:], T3[:], ALU.mult, ["OG", "BV"], ["OG"])
        kb.act(T3[:], BRG[:], AF.Silu, ["BRG"], ["BV"])
        kb.stt("dve", YBb[:], OG[:], ep["nw"][:, 0:1], T3[:], ALU.mult, ALU.mult, ["OG", "const", "BV"], ["YBb"])
        dbg("ya%d" % l, YAb[:], "YAb", [64, 8, NT], BF16)
        dbg("yb%d" % l, YBb[:], "YBb", [128, 4, NT], BF16)
        dbg("gg%d" % l, GG[:], "GG", [64, 8, NT])
        dbg("ot%d" % l, OT[:], "AA", [64, 8, NT])
        passes = [dict(segs=[(p * 512, 512)], kslabs=[(0, 64, 8), (512, 128, 4)], groups=G4, evac=resid_evac(p)) for p in range(2)]
        kb.dense("w_out_mix%d" % e, WB["w_out_mix"][e], passes, lambda si, kc: ((YAb[:, kc, :], "YAb") if si == 0 else (YBb[:, kc, :], "YBb")), NT)
        layer_norm(l, 0)

    def odd_layer(l, tile):
        o = l // 2
        segs = tile["segs"]
        krk = "KR%d" % o; vrk = "VR%d" % o
        if NO > 1:
            P.add("sp", lambda en: [en.dma_start(out=M3s[:], in_=MSD[o, 0]), en.dma_start(out=M4s[:], in_=MSD[o, 1])],
                  r=[("msd", o)], w=["MB"], dkey="ldm", ndma=2)
        is_p = tile["kind"] == "p"
        g = tile.get("g", 0)
        kslot = (g % 5) if is_p else 4

        def ev_q(ci):
            def f(banks, gi):
                bt, bk = banks[gi]
                v = bt[:, 0:4 * NT].rearrange("p (c n) -> p c n", c=4)
                kb.act(QTe[0:64, ci:ci + 4, :], v[0:64], AF.Copy, [bk], ["QTe"], scale=0.125)
                kb.act(QTo[64:128, ci:ci + 4, :], v[64:128], AF.Copy, [bk], ["QTo"], scale=0.125)
            return f

        def ev_k(ci):
            def f(banks, gi):
                bt, bk = banks[gi]
                v = bt[:, 0:4 * NT].rearrange("p (c n) -> p c n", c=4)
                kb.cp("dve", KTf[:, ci:ci + 4, :], v, [bk], ["KTf"])
            return f

        def ev_v(ci):
            def f(banks, gi):
                bt, bk = banks[gi]
                v = bt[:, 0:4 * NT].rearrange("p (c n) -> p c n", c=4)
                evac_copy(VT[:, ci:ci + 4, :], v, [bk], ["VT"])
            return f

        passes = []
        for j, evf in enumerate([ev_q(0), ev_q(4), ev_k(0), ev_k(4), ev_v(0), ev_v(4)]):
            passes.append(dict(segs=[(j * 512, 512)], kslabs=[(0, 128, 8)], groups=[(0, 128, 4)], evac=(lambda banks, evf=evf: evf(banks, 0))))
        kb.dense("c_w_qkv%d" % o, WB["c_w_qkv"][o], passes, lambda si, kc: (Xb[:, kc, :], "Xb"), NT)

        def tok_major(src, skey, t0, n, dst3, dkey_):
            for half in range(2):
                bt, bk = kb.bank()
                for j in range(4):
                    c = half * 4 + j
                    kb.tr(bt[0:n, j * 128:(j + 1) * 128], src[:, c, t0:t0 + n], idn[:, :], [skey], [bk])
                evac_copy(dst3[0:n, half * 8:half * 8 + 8, :], bt[0:n, :].rearrange("p (h d) -> p h d", h=8), [bk], [dkey_])

        for (slot, t0, t1) in segs:
            n = t1 - t0
            if is_p:
                kb.act(KR[o][:, :, kslot * 128:kslot * 128 + n], KTf[:, :, t0:t1], AF.Copy, ["KTf"], [krk])
            else:
                sq = tile["seqs"][slot]
                for kbk in range(4):
                    kb.dma("sp", XIO3, I["cache_c_k"][o, sq, :, kbk * 128:(kbk + 1) * 128, :].rearrange("h t d -> t h d"), [], ["XIO"], "ldx")
                    for half in range(2):
                        bt, bk = kb.bank()
                        for j in range(4):
                            c = half * 4 + j
                            kb.tr(bt[:, j * 128:(j + 1) * 128], XIO[:, c * 128:(c + 1) * 128], idn[:, :], ["XIO"], [bk])
                        evac_copy(KR[o][:, half * 4:half * 4 + 4, kbk * 128:(kbk + 1) * 128], bt[:, :].rearrange("p (c n) -> p c n", c=4), [bk], [krk])
                    kb.dma("pool", VR[o][:, kbk, :, 0:64], I["cache_c_v"][o, sq, :, kbk * 128:(kbk + 1) * 128, :].rearrange("h t d -> t h d"), [], [vrk], "ldv")
                kb.act(KR[o][:, :, 4 * 128:4 * 128 + n], KTf[:, :, t0:t1], AF.Copy, ["KTf"], [krk])
            tok_major(VT, "VT", t0, n, XIO3, "XIO")
            kb.cp("dve", VR[o][0:n, kslot, :, 0:64], XIO3[0:n], ["XIO"], [vrk])
            emit = (not is_p) or (g >= NG - KEEP // 128)
            if emit:
                if is_p:
                    r0 = (g - (NG - KEEP // 128)) * 128
                    dv = O["p_v"][o, :, r0:r0 + 128, :]; dk_ = O["p_k"][o, :, r0:r0 + 128, :]
                else:
                    dv = O["s_v"][o, slot, :, :, :]; dk_ = O["s_k"][o, slot, :, :, :]
                kb.dma("pool", dv.rearrange("h t d -> t h d"), XIO3[0:n], ["XIO"], [], "stx")
                tok_major(KTf, "KTf", t0, n, XIO3, "XIO")
                kb.dma("pool", dk_.rearrange("h t d -> t h d"), XIO3[0:n], ["XIO"], [], "stx")
            if is_p:
                blocks = []
                for b in range(5):
                    kbi = g - 4 + b
                    if kbi < 0:
                        continue
                    bias = {0: "M0", 3: "M3", 4: "M4"}.get(b)
                    blocks.append((kbi % 5, 128, bias))
            else:
                blocks = [(0, 128, None), (1, 128, None), (2, 128, None), (3, 128, "M3"), (4, n, "M4")]
            nb_ = len(blocks)
            obanks = fixed
            for h in range(16 if not os.environ.get('SKIP_ATT') else 0):
                c = h // 2
                QTs = QTe if h % 2 == 0 else QTo
                qk = "QTe" if h % 2 == 0 else "QTo"
                pt = PT[h % 2]; ptk = "PT%d" % (h % 2)
                b1 = kb.bank(); b2 = kb.bank()
                for bi, (ks, nk, bias) in enumerate(blocks):
                    bt, bk = (b1 if bi < 4 else b2)
                    oc = (bi % 4) * 128
                    kb.mm(bt[0:nk, oc:oc + n], KR[o][:, c, ks * 128:ks * 128 + nk], QTs[:, c, t0:t1], [krk, qk], [bk], start=True, stop=(bias is None))
                    if bias == "M0":
                        kb.mm(bt[0:nk, oc:oc + n], idnb[0:nk, 0:nk], M0[0:nk, 0:n], ["const"], [bk], start=False, stop=True)
                    elif bias == "M3":
                        kb.mm(bt[0:nk, oc:oc + n], idnb[0:nk, 0:nk], M3[o][0:nk, h, 0:n], ["const", "MB"], [bk], start=False, stop=True)
                    elif bias == "M4":
                        kb.mm(bt[0:nk, oc:oc + n], idnb[0:nk, 0:nk], M4[o][0:nk, h, 0:n], ["const", "MB"], [bk], start=False, stop=True)
                for bi, (ks, nk, bias) in enumerate(blocks):
                    bt, bk = (b1 if bi < 4 else b2)
                    oc = (bi % 4) * 128
                    kb.act(pt[0:nk, bi, 0:n], bt[0:nk, oc:oc + n], AF.Exp, [bk, "const"], [ptk], bias=CH[o][0:nk, h:h + 1])
                ob, obk = obanks[h // 7]
                hc = (h % 7) * 65
                for bi, (ks, nk, bias) in enumerate(blocks):
                    kb.mm(ob[0:n, hc:hc + 65], pt[0:nk, bi, 0:n], VR[o][0:nk, ks, h, :], [ptk, vrk], [obk], start=(bi == 0), stop=(bi == nb_ - 1))
            for bi_, (h0, nh) in enumerate(((0, 7), (7, 7), (14, 2))):
                ob, obk = obanks[bi_]
                v = ob[0:n, 0:nh * 65].rearrange("p (h d) -> p h d", h=nh)
                kb.rcp(RDEN[0:n, h0:h0 + nh, :], v[:, :, 64:65], [obk], ["RDEN"])
                kb.tt("dve", OTOK[0:n, h0:h0 + nh, :], v[:, :, 0:64], RDEN[0:n, h0:h0 + nh, :].to_broadcast([n, nh, 64]), ALU.mult, [obk, "RDEN"], ["OTOK"])
            OT2 = OTOK[:, :, :].rearrange("p h d -> p (h d)")
            for half in range(2):
                bt, bk = kb.bank()
                for j in range(4):
                    c = half * 4 + j
                    kb.tr(bt[:, j * 128:j * 128 + n], OT2[0:n, c * 128:(c + 1) * 128], idn[0:n, 0:n], ["OTOK"], [bk])
                evac_copy(OTb[:, half * 4:half * 4 + 4, t0:t1], bt[:, :].rearrange("p (c n) -> p c n", c=4)[:, :, 0:n], [bk], ["OTb"])
        passes = [dict(segs=[(p * 512, 512)], kslabs=[(0, 128, 8)], groups=[(0, 128, 4)], evac=resid_evac(p)) for p in range(2)]
        kb.dense("c_w_o%d" % o, WB["c_w_o"][o], passes, lambda si, kc: (OTb[:, kc, :], "OTb"), NT)
        layer_norm(l, 0)

    def run_tile(tile):
        xsrc = tile["x"]
        IOSTEP = int(os.environ.get('IOSTEP', '9'))
        kb.dma("sp", XIO[:, :], xsrc, [], ["XIO"], "ldx")
        for half in range(2 if IOSTEP >= 2 else 0):
            bt, bk = kb.bank()
            for j in range(4):
                c = half * 4 + j
                kb.tr(bt[:, j * 128:(j + 1) * 128], XIO[:, c * 128:(c + 1) * 128], idn[:, :], ["XIO"], [bk])
            v = bt[:, :].rearrange("p (c n) -> p c n", c=4)
            if IOSTEP >= 3:
                kb.act(X[:, half * 4:half * 4 + 4, :], v, AF.Copy, [bk], ["X"])
            if IOSTEP >= 4:
                kb.cp("dve", Xb[:, half * 4:half * 4 + 4, :], v, [bk, "X"], ["Xb"])
        MAXL = int(os.environ.get('MAXL', '99'))
        NOFFN = os.environ.get('NOFFN')
        NOMIX = os.environ.get('NOMIX')
        for l in range(min(DEPTH, MAXL)):
            if NOMIX:
                pass
            elif l % 2 == 0:
                even_layer(l, tile["segs"])
            else:
                odd_layer(l, tile)
            if not NOFFN:
                ffn(l)
        for half in range(2 if IOSTEP >= 5 else 0):
            bt, bk = kb.bank()
            for j in range(4):
                c = half * 4 + j
                kb.tr(bt[:, j * 128:(j + 1) * 128], X[:, c, :], idn[:, :], ["X"], [bk])
            if IOSTEP >= 6:
                evac_copy(XIO[:, half * 512:(half + 1) * 512], bt[:, :], [bk], ["XIO"])
        kb.dma("pool", tile["y"], XIO[:, :], ["XIO"], [], "stx")

    def state_out(e, slot, dw, dsh, dg):
        bt, bk = kb.bank()
        for h in range(8):
            kb.tr(bt[0:64, h * 64:(h + 1) * 64], HW[e][slot][:, h * 64:(h + 1) * 64], idn[0:64, 0:64], [HW[e][slot].name], [bk])
        kb.act(STG, bt[0:64, :], AF.Copy, [bk], ["XIO"])
        kb.dma("pool", dw.rearrange("h v k -> v h k"), STG.rearrange("p (h k) -> p h k", h=8), ["XIO"], [], "stx")
        kb.dma("pool", dg.rearrange("h d v -> d h v"), SG[e][slot][:].rearrange("p (h v) -> p h v", h=4), [SG[e][slot].name], [], "sts_g%d" % e)
        kb.dma("pool", dsh[0:1536].rearrange("(c p) -> p c", p=64), SH1[e][slot][:, :, 0], [SH1[e][slot].name], [], "sts_1%d" % e, slow=True)
        kb.dma("pool", dsh[1536:1664].rearrange("(c p) -> p c", p=64), SH2[e][slot][:, :, 0], [SH2[e][slot].name], [], "sts_2%d" % e, slow=True)
        kb.dma("pool", dsh[1664:1792].rearrange("(c p) -> p c", p=128), SH3[e][slot][:, :, 0], [SH3[e][slot].name], [], "sts_3%d" % e, slow=True)

    STOP = os.environ.get('STOP', '')
    for g in range(NG if STOP != 'setup' else 0):
        dbg_on[0] = (g == 0)
        run_tile(dict(kind="p", g=g, segs=[(0, 0, 128)], x=I["x_prompt"][g * 128:(g + 1) * 128, :], y=O["y_p"][g * 128:(g + 1) * 128, :]))
    for e in range(NE if STOP not in ('setup', 'p2') else 0):
        state_out(e, 0, O["p_wkv"][e], O["p_shift"][e], O["p_gla"][e])
    dbg_on[0] = False
    for e in range(NE if STOP == '' else 0):
        for slot in range(2):
            kb.dma("sp", STG.rearrange("p (h k) -> p h k", h=8), I["state_a_wkv"][e, slot].rearrange("h v k -> v h k"), [], ["XIO"], "ldx")
            bt, bk = kb.bank()
            for h in range(8):
                kb.tr(bt[0:64, h * 64:(h + 1) * 64], STG[:, h * 64:(h + 1) * 64], idn[0:64, 0:64], ["XIO"], [bk])
            kb.act(HW[e][slot][:], bt[0:64, :], AF.Copy, [bk], [HW[e][slot].name])
            kb.dma("sp", SG[e][slot][:].rearrange("p (h v) -> p h v", h=4), I["state_b_gla"][e, slot].rearrange("h d v -> d h v"), [], [SG[e][slot].name], "ldsg")
            sh = I["state_a_shift"][e, slot]
            kb.dma("pool", SH1[e][slot][:, :, 0], sh[0:1536].rearrange("(c p) -> p c", p=64), [], [SH1[e][slot].name], "ldsh1", slow=True)
            kb.dma("pool", SH2[e][slot][:, :, 0], sh[1536:1664].rearrange("(c p) -> p c", p=64), [], [SH2[e][slot].name], "ldsh2", slow=True)
            kb.dma("pool", SH3[e][slot][:, :, 0], sh[1664:1792].rearrange("(c p) -> p c", p=128), [], [SH3[e][slot].name], "ldsh3", slow=True)
    if STOP == '':
      run_tile(dict(kind="s", segs=[(0, 0, 64), (1, 64, 128)], seqs=[0, 1], x=I["x_sample"][:, :], y=O["y_s"][:, :]))
    for e in range(NE if STOP == '' else 0):
        for slot in range(2):
            state_out(e, slot, O["s_wkv"][e, slot], O["s_shift"][e, slot], O["s_gla"][e, slot])
    P.emit(nc.Block())
    for cm in reversed(kb.stack):
        cm.__exit__(None, None, None)
    return nc


_OUT_ORDER = ["y_p", "y_s", "p_wkv", "p_shift", "p_gla", "p_k", "p_v", "s_wkv", "s_shift", "s_gla", "s_k", "s_v"]


def run(inputs, SEQ=8192, DEPTH=4, ncores=8):
    nc = build(SEQ, DEPTH)
    f = lambda a: np.ascontiguousarray(np.asarray(a, dtype=np.float32))
    in_maps = []
    for c in range(ncores):
        m = {"x_prompt": f(inputs["x_prompt"][c]), "x_sample": f(inputs["x_sample"][2 * c:2 * c + 2]).reshape(128, D)}
        for k_ in ("state_a_wkv", "state_a_shift", "state_b_gla", "cache_c_k", "cache_c_v"):
            m[k_] = f(np.asarray(inputs[k_])[:, 2 * c:2 * c + 2])
        for k_ in inputs:
            if k_ not in m:
                m[k_] = f(inputs[k_])
        in_maps.append(m)
    print('sbuf bytes remaining', nc.sbuf_bytes_remaining) if os.environ.get('KTRACE') else None
    res = run_bass_kernel_spmd(nc, in_maps, core_ids=list(range(ncores)), **({'trace': True} if os.environ.get('KTRACE') else {}))
    print('EXEC_NS', res.exec_time_ns) if os.environ.get('KTRACE') else None
    R = res.results
    cat = lambda name, ax: np.stack([np.asarray(r[name]) for r in R], axis=ax)
    y_p = cat("y_p", 0)
    y_s = np.concatenate([np.asarray(r["y_s"]).reshape(2, 64, D) for r in R], axis=0)
    p_wkv = cat("p_wkv", 1); p_shift = cat("p_shift", 1); p_gla = cat("p_gla", 1)
    p_k = cat("p_k", 1); p_v = cat("p_v", 1)
    c2 = lambda name: np.concatenate([np.asarray(r[name]) for r in R], axis=1)
    return (y_p, y_s, p_wkv, p_shift, p_gla, p_k, p_v, c2("s_wkv"), c2("s_shift"), c2("s_gla"), c2("s_k"), c2("s_v"))


def kernel(**inputs):
    outs = run(inputs)
    return tuple(np.ascontiguousarray(o, dtype=np.float32) for o in outs)
```
